# Optimizing a Trainium2 kernel written in Bass

```python
import jax, jax.numpy as jnp
from jax import lax
import numpy as np

D_MODEL = 1024
BATCH = 16
SEQ = 2048
DEPTH = 4

HEAD_DIM = 64
N_HEADS_MIX = 8
W_MIX = N_HEADS_MIX * HEAD_DIM
N_BRANCH = 3
DILATED_PAIRS = ((128, 1), (512, 4), (2048, 16))
QBLK = 128
ROT_DIM = HEAD_DIM // 4
ROPE_THETA = 500000.0
GRID_W = 64
NA_WIN_ROWS = 8
NA_WIN_COLS = 16
DECAY_LORA = 64
ICLR_LORA = 64
GATE_LORA = 128
DECAY_SCALE = 0.6065306597126334
D_FF = ((8 * D_MODEL // 3 + 255) // 256) * 256
N_MOD = 9
RMS_EPS = 1e-6
GN_EPS = 64e-5
NEG = -1e30
IN_SPLITS = (W_MIX, W_MIX, W_MIX,
             W_MIX, W_MIX, W_MIX,
             W_MIX, W_MIX, W_MIX,
             DECAY_LORA, DECAY_LORA,
             ICLR_LORA, ICLR_LORA,
             GATE_LORA,
             D_MODEL, D_MODEL, D_MODEL)
D_IN = sum(IN_SPLITS)

kernel_name = "hybrid_dilated_natten_rwkv7_encoder"


def rmsnorm(x, g):
    xf = x.astype(jnp.float32)
    y = xf * lax.rsqrt(jnp.mean(xf * xf, axis=-1, keepdims=True) + RMS_EPS)
    return (y * g.astype(jnp.float32)).astype(x.dtype)


def modulate(h, shift, scale):
    return h * (1 + scale[:, None, :]) + shift[:, None, :]


def swiglu(h, wi, wo):
    gate, up = jnp.split(h @ wi, 2, axis=-1)
    return (jax.nn.silu(gate) * up) @ wo


def partial_rotary(t, positions):
    half = ROT_DIM // 2
    inv_freq = ROPE_THETA ** (-jnp.arange(half, dtype=jnp.float32) * 2.0 / ROT_DIM)
    ang = positions.astype(jnp.float32)[..., None] * inv_freq
    cos, sin = jnp.cos(ang)[:, :, None, :], jnp.sin(ang)[:, :, None, :]
    tr = t[..., :ROT_DIM].astype(jnp.float32)
    t1, t2 = tr[..., :half], tr[..., half:]
    rot = jnp.concatenate([t1 * cos - t2 * sin, t2 * cos + t1 * sin], axis=-1)
    return jnp.concatenate([rot.astype(t.dtype), t[..., ROT_DIM:]], axis=-1)


def banded_attention(q, k, v, radius):
    lead = q.shape[:-2]
    L, hd = q.shape[-2], q.shape[-1]
    nb = -(-L // QBLK)
    lp = nb * QBLK
    kw = QBLK + 2 * radius
    pad = [(0, 0)] * len(lead)
    qb = jnp.pad(q, pad + [(0, lp - L), (0, 0)]).reshape(*lead, nb, QBLK, hd)
    kp = jnp.pad(k, pad + [(radius, radius + lp - L), (0, 0)])
    vp = jnp.pad(v, pad + [(radius, radius + lp - L), (0, 0)])
    idx = np.arange(nb)[:, None] * QBLK + np.arange(kw)[None, :]
    kb = kp[..., idx, :]
    vb = vp[..., idx, :]
    s = jnp.einsum("...nqd,...nkd->...nqk", qb, kb).astype(jnp.float32) * (hd ** -0.5)
    key_pos = idx - radius
    q_pos = np.arange(nb)[:, None] * QBLK + np.arange(QBLK)[None, :]
    off = key_pos[:, None, :] - q_pos[:, :, None]
    valid = (np.abs(off) <= radius) & (key_pos[:, None, :] >= 0) & (key_pos[:, None, :] < L)
    s = jnp.where(valid, s, NEG)
    m = jnp.max(s, axis=-1, keepdims=True)
    p = jnp.exp(s - m)
    den = jnp.sum(p, axis=-1, keepdims=True)
    o = jnp.einsum("...nqk,...nkd->...nqd", (p / den).astype(v.dtype), vb)
    lse = (m + jnp.log(den))[..., 0]
    return o.reshape(*lead, lp, hd)[..., :L, :], lse.reshape(*lead, lp)[..., :L]


def dilated_attention(q, k, v):
    B, S, H, hd = q.shape
    q, k, v = (t.transpose(0, 2, 1, 3) for t in (q, k, v))
    outs, lses = [], []
    for window, dil in DILATED_PAIRS:
        radius = window // (2 * dil)
        to_cls = lambda t: t.reshape(B, H, S // dil, dil, hd).swapaxes(2, 3)
        o, lse = banded_attention(to_cls(q), to_cls(k), to_cls(v), radius)
        outs.append(o.swapaxes(2, 3).reshape(B, H, S, hd))
        lses.append(lse.swapaxes(2, 3).reshape(B, H, S))
    alpha = jax.nn.softmax(jnp.stack(lses), axis=0)
    o = jnp.einsum("gbhs,gbhsd->bhsd", alpha, jnp.stack(outs).astype(jnp.float32))
    return o.astype(v.dtype).transpose(0, 2, 1, 3).reshape(B, S, H * hd)


def neighborhood_attention(q, k, v, rpb):
    B, S, H, hd = q.shape
    rows = S // GRID_W
    wr = min(NA_WIN_ROWS, rows)
    wc = NA_WIN_COLS
    grid = lambda t: t.reshape(B, rows, GRID_W, H, hd).transpose(0, 3, 1, 2, 4)
    qg, kg, vg = grid(q), grid(k), grid(v)
    r_start = np.clip(np.arange(rows) - wr // 2, 0, rows - wr)
    row_idx = r_start[:, None] + np.arange(wr)[None, :]
    kr = kg[:, :, row_idx]
    vr = vg[:, :, row_idx]
    s = jnp.einsum("bhrqd,bhrikd->bhrqik", qg, kr).astype(jnp.float32) * (hd ** -0.5)
    cols = np.arange(GRID_W)
    c_start = np.clip(cols - wc // 2, 0, GRID_W - wc)
    col_in = (cols[None, :] >= c_start[:, None]) & (cols[None, :] < c_start[:, None] + wc)
    roff = row_idx - np.arange(rows)[:, None] + NA_WIN_ROWS - 1
    coff = np.clip(cols[None, :] - cols[:, None], -(wc - 1), wc - 1) + wc - 1
    bias = rpb[:, roff[:, None, :, None], coff[None, :, None, :]]
    s = jnp.where(col_in[:, None, :], s + bias[None].astype(jnp.float32), NEG)
    shp = s.shape
    p = jax.nn.softmax(s.reshape(*shp[:-2], shp[-2] * shp[-1]), axis=-1).reshape(shp)
    o = jnp.einsum("bhrqik,bhrikd->bhrqd", p.astype(v.dtype), vr)
    return o.transpose(0, 2, 3, 1, 4).reshape(B, S, H * hd)


def _neighbour_pair(tf, tb):
    prev = jnp.pad(tf, ((0, 0), (1, 0), (0, 0)))[:, :-1]
    nxt = jnp.pad(tb, ((0, 0), (0, 1), (0, 0)))[:, 1:]
    return jnp.stack([prev, nxt])


def _token_shift(tf, tb, mu):
    base = jnp.stack([tf, tb])
    return base + (_neighbour_pair(tf, tb) - base) * mu[:, None, None, :]


def _flip_bwd(t):
    return jnp.stack([t[0], jnp.flip(t[1], axis=1)])


def _rwkv_step(state, inp):
    r, w, k, v, a_vec, b_vec = inp
    sa = jnp.einsum("dbhvk,dbhk->dbhv", state, a_vec)
    state = state * w[..., None, :] + sa[..., None] * b_vec[..., None, :] + v[..., None] * k[..., None, :]
    y = jnp.einsum("dbhvk,dbhk->dbhv", state, r)
    return state, y


def rwkv7_bidirectional(pr, pk, pv, pwf, pwb, paf, pab, pg, mu_rkv, mu_w, mu_a,
                        w0, w_up, a0, a_up, g_up, k_k, k_a, r_k, gn_w, gn_b):
    B, S, C = pr.shape
    H, N = N_HEADS_MIX, HEAD_DIM
    f32 = jnp.float32
    r = _token_shift(pr, pr, mu_rkv[:, 0])
    k = _token_shift(pk, pk, mu_rkv[:, 1])
    v = _token_shift(pv, pv, mu_rkv[:, 2])
    xw = _token_shift(pwf, pwb, mu_w)
    xa = _token_shift(paf, pab, mu_a)
    wz = w0[:, None, None, :] + jnp.einsum("dbsr,drc->dbsc", jnp.tanh(xw), w_up)
    decay = jnp.exp(-DECAY_SCALE * jax.nn.sigmoid(wz.astype(f32)))
    a = jax.nn.sigmoid((a0[:, None, None, :] + jnp.einsum("dbsr,drc->dbsc", xa, a_up)).astype(f32))
    hs = lambda t: t.reshape(2, B, S, H, N).astype(f32)
    r, k, v, decay, a = hs(r), hs(k), hs(v), hs(decay), hs(a)
    kk = k * k_k.reshape(H, N).astype(f32)
    kk = kk / jnp.maximum(jnp.sqrt(jnp.sum(kk * kk, axis=-1, keepdims=True)), 1e-12)
    k = k * (1 + (a - 1) * k_a.reshape(H, N).astype(f32))
    tm = lambda t: jnp.moveaxis(_flip_bwd(t), 2, 0)
    init = jnp.zeros((2, B, H, N, N), f32)
    _, ys = lax.scan(_rwkv_step, init, (tm(r), tm(decay), tm(k), tm(v), tm(-kk), tm(kk * a)))
    y = _flip_bwd(jnp.moveaxis(ys, 0, 2))
    mean = jnp.mean(y, axis=-1, keepdims=True)
    var = jnp.mean(jnp.square(y - mean), axis=-1, keepdims=True)
    yn = ((y - mean) * lax.rsqrt(var + GN_EPS)).reshape(2, B, S, C) * gn_w.astype(f32) + gn_b.astype(f32)
    bonus = (jnp.sum(r * k * r_k.astype(f32), axis=-1, keepdims=True) * v).reshape(2, B, S, C)
    g = jax.nn.sigmoid(pg) @ g_up
    return (jnp.sum(yn + bonus, axis=0) * g).astype(pr.dtype)


def hybrid_mixer(h, positions, w_in, rpb, mu_rkv, mu_w, mu_a, w0, w_up, a0, a_up, g_up,
                 k_k, k_a, r_k, gn_w, gn_b, w_branch, w_out):
    B, S, _ = h.shape
    proj = h @ w_in
    (aq, ak, av, nq, nk, nv, pr, pk, pv, pwf, pwb, paf, pab, pg, gz0, gz1, gz2) = jnp.split(
        proj, np.cumsum(IN_SPLITS)[:-1].tolist(), axis=-1)
    heads = lambda t: t.reshape(B, S, N_HEADS_MIX, HEAD_DIM)
    o_a = dilated_attention(partial_rotary(heads(aq), positions), partial_rotary(heads(ak), positions), heads(av))
    o_b = neighborhood_attention(heads(nq), heads(nk), heads(nv), rpb)
    o_c = rwkv7_bidirectional(pr, pk, pv, pwf, pwb, paf, pab, pg, mu_rkv, mu_w, mu_a,
                              w0, w_up, a0, a_up, g_up, k_k, k_a, r_k, gn_w, gn_b)
    merged = (jax.nn.sigmoid(gz0) * (o_a @ w_branch[0])
              + jax.nn.sigmoid(gz1) * (o_b @ w_branch[1])
              + jax.nn.sigmoid(gz2) * (o_c @ w_branch[2]))
    return merged @ w_out


def setup_inputs(seed: int = 0) -> dict:
    key = jax.random.key(seed)
    ks = jax.random.split(key, 28)
    L, D = DEPTH, D_MODEL
    f32 = jnp.float32
    nrm = lambda k, shape, scale: jax.random.normal(k, shape, f32) * scale
    x = nrm(ks[0], (BATCH, SEQ, D), 1.0)
    c = nrm(ks[1], (BATCH, D), 1.0)
    positions = (jnp.arange(SEQ, dtype=jnp.int32)[None, :]
                 + jax.random.randint(ks[2], (BATCH, 1), 0, 4096, dtype=jnp.int32))
    ada_w = nrm(ks[3], (L, D, N_MOD * D), 0.5 * D ** -0.5)
    ada_b = nrm(ks[4], (L, N_MOD * D), 0.02)
    norm_gains = 1.0 + nrm(ks[5], (L, 3, D), 0.05)
    ffn_wi = nrm(ks[6], (L, 2, D, 2 * D_FF), D ** -0.5)
    ffn_wo = nrm(ks[7], (L, 2, D_FF, D), D_FF ** -0.5)
    w_in = nrm(ks[8], (L, D, D_IN), D ** -0.5)
    rpb = nrm(ks[9], (L, N_HEADS_MIX, 2 * NA_WIN_ROWS - 1, 2 * NA_WIN_COLS - 1), 0.2)
    mu_rkv = jax.random.uniform(ks[10], (L, 2, 3, W_MIX), f32)
    mu_w = jax.random.uniform(ks[11], (L, 2, DECAY_LORA), f32)
    mu_a = jax.random.uniform(ks[12], (L, 2, ICLR_LORA), f32)
    w0 = nrm(ks[13], (L, 2, W_MIX), 1.5)
    w_up = nrm(ks[14], (L, 2, DECAY_LORA, W_MIX), DECAY_LORA ** -0.5)
    a0 = nrm(ks[15], (L, 2, W_MIX), 0.5)
    a_up = nrm(ks[16], (L, 2, ICLR_LORA, W_MIX), ICLR_LORA ** -0.5)
    g_up = nrm(ks[17], (L, GATE_LORA, W_MIX), GATE_LORA ** -0.5)
    k_k = 0.85 + nrm(ks[18], (L, W_MIX), 0.05)
    k_a = 1.0 + nrm(ks[19], (L, W_MIX), 0.05)
    r_k = nrm(ks[20], (L, N_HEADS_MIX, HEAD_DIM), 0.1)
    gn_w = 1.0 + nrm(ks[21], (L, W_MIX), 0.05)
    gn_b = nrm(ks[22], (L, W_MIX), 0.02)
    w_branch = nrm(ks[23], (L, N_BRANCH, W_MIX, D), W_MIX ** -0.5)
    w_out = nrm(ks[24], (L, D, D), D ** -0.5)
    final_norm = 1.0 + nrm(ks[25], (D,), 0.05)
    return {"x": x, "c": c, "positions": positions, "ada_w": ada_w, "ada_b": ada_b,
            "norm_gains": norm_gains, "ffn_wi": ffn_wi, "ffn_wo": ffn_wo, "w_in": w_in,
            "rpb": rpb, "mu_rkv": mu_rkv, "mu_w": mu_w, "mu_a": mu_a, "w0": w0, "w_up": w_up,
            "a0": a0, "a_up": a_up, "g_up": g_up, "k_k": k_k, "k_a": k_a, "r_k": r_k,
            "gn_w": gn_w, "gn_b": gn_b, "w_branch": w_branch, "w_out": w_out,
            "final_norm": final_norm}


def reference(x, c, positions, ada_w, ada_b, norm_gains, ffn_wi, ffn_wo, w_in, rpb,
              mu_rkv, mu_w, mu_a, w0, w_up, a0, a_up, g_up, k_k, k_a, r_k, gn_w, gn_b,
              w_branch, w_out, final_norm):
    cs = jax.nn.silu(c)
    for l in range(DEPTH):
        mod = cs @ ada_w[l] + ada_b[l]
        sh1, sc1, gt1, sh2, sc2, gt2, sh3, sc3, gt3 = jnp.split(mod, N_MOD, axis=-1)
        h = modulate(rmsnorm(x, norm_gains[l, 0]), sh1, sc1)
        x = x + 0.5 * gt1[:, None, :] * swiglu(h, ffn_wi[l, 0], ffn_wo[l, 0])
        h = modulate(rmsnorm(x, norm_gains[l, 1]), sh2, sc2)
        x = x + gt2[:, None, :] * hybrid_mixer(
            h, positions, w_in[l], rpb[l], mu_rkv[l], mu_w[l], mu_a[l], w0[l], w_up[l],
            a0[l], a_up[l], g_up[l], k_k[l], k_a[l], r_k[l], gn_w[l], gn_b[l],
            w_branch[l], w_out[l])
        h = modulate(rmsnorm(x, norm_gains[l, 2]), sh3, sc3)
        x = x + 0.5 * gt3[:, None, :] * swiglu(h, ffn_wi[l, 1], ffn_wo[l, 1])
    return rmsnorm(x, final_norm)
```

```python
import contextlib
import numpy as np
import concourse.bass as bass
import concourse.mybir as mybir
from concourse.bass_utils import run_bass_kernel_spmd

F32 = mybir.dt.float32
BF16 = mybir.dt.bfloat16
I32 = mybir.dt.int32
ALU = mybir.AluOpType
AF = mybir.ActivationFunctionType


class Buf:
    __slots__ = ("name", "w", "r")

    def __init__(self, name):
        self.name = name
        self.w = []
        self.r = []

    def add_writer(self, tok):
        if tok[0] == "c":
            self.w = [t for t in self.w if not (t[0] == "c" and t[1] == tok[1])]
        self.w.append(tok)
        if len(self.w) > 12:
            self.w = self.w[-12:]
        self.r = []


class Sched:
    ENG = ("pe", "act", "dve", "pool", "sp")
    NDS = 8

    def __init__(self, nc, es):
        self.nc = nc
        self.eng = {"pe": nc.tensor, "act": nc.scalar, "dve": nc.vector, "pool": nc.gpsimd, "sp": nc.sync}
        self.sem = {e: es.enter_context(nc.semaphore("s_" + e)) for e in self.ENG}
        self.cnt = {e: 0 for e in self.ENG}
        self.dsem = {q: [es.enter_context(nc.semaphore("d_%s%d" % (q, i))) for i in range(self.NDS)]
                     for q in ("sp", "pool", "act")}
        self.dcnt = {q: 0 for q in self.dsem}
        self.waited = {e: {} for e in self.ENG}
        self.ninst = 0

    def _wait(self, e, tok):
        if tok is None:
            return
        kind = tok[0]
        if kind == "c":
            _, e2, i2 = tok
            if e2 == e and e == "pe":
                return
            key = ("c", e2)
            if self.waited[e].get(key, 0) >= i2:
                return
            self.waited[e][key] = i2
            self.eng[e].wait_ge(self.sem[e2], i2)
        else:
            _, q, j = tok
            k = j % self.NDS
            val = 16 * (j // self.NDS + 1)
            key = ("d", q, k)
            if self.waited[e].get(key, 0) >= val:
                return
            self.waited[e][key] = val
            self.eng[e].wait_ge(self.dsem[q][k], val)

    def _deps(self, e, reads, writes, same_war=False):
        for b in reads:
            for t in b.w:
                self._wait(e, t)
        for b in writes:
            for t in b.w:
                self._wait(e, t)
            for t in b.r:
                if t[0] == "c" and t[1] == e:
                    continue
                self._wait(e, t)

    def op(self, e, fn, reads=(), writes=()):
        self._deps(e, reads, writes)
        ins = fn(self.eng[e])
        self.cnt[e] += 1
        ins.then_inc(self.sem[e], 1)
        tok = ("c", e, self.cnt[e])
        for b in reads:
            b.r.append(tok)
        for b in writes:
            b.add_writer(tok)
        self.ninst += 1
        return tok

    def dma(self, q, out, in_, reads=(), writes=()):
        e = q
        j = self.dcnt[q]
        if j >= self.NDS:
            self._wait(e, ("d", q, j - self.NDS))
        self._deps(e, reads, writes)
        k = j % self.NDS
        self.eng[e].dma_start(out=out, in_=in_).then_inc(self.dsem[q][k], 16)
        self.dcnt[q] += 1
        tok = ("d", q, j)
        for b in reads:
            b.r.append(tok)
        for b in writes:
            b.add_writer(tok)
        self.ninst += 1
        return tok

    def barrier(self):
        toks = [("c", e, self.cnt[e]) for e in self.ENG if self.cnt[e] > 0]
        for q in self.dsem:
            for j in range(max(0, self.dcnt[q] - self.NDS), self.dcnt[q]):
                toks.append(("d", q, j))
        for e in self.ENG:
            for t in toks:
                if t[0] == "c" and t[1] == e:
                    continue
                self._wait(e, t)

    def finish(self, e="sp"):
        self.barrier()


D = 1024
S_LEN = 2048
NCH = 8
DFF = 2816
NJ = DFF // 128
L_ALL = 4
WM = 512
RMS_EPS = 1e-6
GN_EPS = 64e-5
DECAY_SCALE = 0.6065306597126334
TWO_PI = 6.283185307179586

OFF_A_QK = 0
OFF_A_V = 2048
OFF_B_QK = 2560
OFF_B_V = 3584
OFF_C_R = 4096
OFF_C_K = 4608
OFF_C_V = 5120
OFF_C_LW = 5632
OFF_C_LA = 5760
OFF_C_G = 5888
OFF_GZ = 6016
NCOL = OFF_GZ + 3 * 1024


def pp_layout(L):
    off = {}
    n = 0
    for name, size in (("ng", L * 24), ("fn", 8), ("adab", L * 72), ("mu_rkv", L * 24), ("w0", L * 8),
                       ("a0", L * 8), ("k_k", L * 4), ("k_a", L * 4), ("gn_w", L * 4), ("gn_b", L * 4),
                       ("r_k", L * 4), ("mu_w", L), ("mu_a", L), ("invf", 1), ("rsign", 1), ("eps", 1),
                       ("one", 1), ("gneps", 1)):
        off[name] = n
        n += size
    return off, n


class Prog:
    def __init__(self, NSEQ=2, L=4, parts=("ffn", "A", "B", "C")):
        self.NSEQ, self.L, self.parts = NSEQ, L, parts
        self.ppo, self.NPP = pp_layout(L)
        self.nc = bass.Bass("TRN2", target_bir_lowering=False)
        self.es = contextlib.ExitStack()

    def dram(self, name, shape, dtype=F32, kind="ExternalInput"):
        return self.nc.dram_tensor(name, list(shape), dtype, kind=kind).ap()

    def sb(self, name, shape, dtype, es=None):
        self._uid = getattr(self, "_uid", 0) + 1
        return (es or self.es).enter_context(self.nc.sbuf_tensor("%s_%d" % (name, self._uid), list(shape), dtype))

    def ppc(self, name, idx=0, n=1):
        o = self.ppo[name] + idx
        return self.ppt[:, o:o + n]

    def pget(self, pool="g"):
        lst = self.ps_g if pool == "g" else self.ps_a
        k = self.ps_rr[pool]
        self.ps_rr[pool] = (k + 1) % len(lst)
        return lst[k]

    def wslot(self):
        k = self.ws_rr
        self.ws_rr = (k + 1) % len(self.wsl)
        return self.wsl[k]

    def mm(self, ps, pairs, reads, psb, extra_reads=()):
        S = self.S
        n = len(pairs)
        for i, (a, b) in enumerate(pairs):
            S.op("pe", lambda e, a=a, b=b, i=i: e.matmul(ps, a, b, start=(i == 0), stop=(i == n - 1)),
                 reads=reads, writes=[psb])

    def build(self):
        nc, es, NSEQ, L = self.nc, self.es, self.NSEQ, self.L
        with es:
            self._declare()
            self.S = Sched(nc, es)
            self._alloc_global()
            self._prologue()
            for s in range(NSEQ):
                self._seq(s)
            self.S.finish()
        return nc

    def _declare(self):
        NSEQ, L = self.NSEQ, self.L
        d = self.dram
        self.xT_d = d("xT", [NSEQ, 128, NCH, S_LEN])
        self.cT_d = d("cT", [128, NCH, NSEQ])
        self.pos_d = d("pos", [NSEQ, S_LEN], I32)
        self.pp_d = d("pp", [128, self.NPP])
        self.ada_w_d = d("ada_w", [L, D, 9 * D])
        self.wi_d = d("ffn_wi", [L, 2, D, 2 * DFF])
        self.wo_d = d("ffn_wo", [L, 2, DFF, D])
        self.win_d = d("w_in", [L, D, NCOL])
        self.wbr_d = d("w_branch", [L, 3, WM, D])
        self.wout_d = d("w_out", [L, D, D])
        self.ta_d = d("ta", [128, 3968])
        self.bu_d = d("bu", [32, S_LEN])
        self.bw_d = d("bw", [32, S_LEN])
        self.braw_d = d("braw", [L, 8, 64, 31 * 64])
        self.wup_d = d("w_up", [L, 2, 64, WM])
        self.aup_d = d("a_up", [L, 2, 64, WM])
        self.gup_d = d("g_up", [L, 128, WM])
        self.cst_d = d("cst", [128, 1024])
        self.cst2_d = d("cst2", [128, 512])
        self.murkv_d = d("mu_rkv", [L, 2, 3, WM])
        self.muw_d = d("mu_w", [L, 2, 64])
        self.mua_d = d("mu_a", [L, 2, 64])
        self.out_d = d("outT", [NSEQ, 128, NCH, S_LEN], kind="ExternalOutput")
        self.xs_d = d("xspill", [128, NCH, S_LEN], kind="Internal")
        self.b_xs = Buf("xs")
        self.dbg_d = d("dbg", [128, 4, S_LEN], kind="ExternalOutput") if getattr(self, "debug", False) else None

    def _alloc_global(self):
        nc, es = self.nc, self.es
        sb = self.sb
        self.xT = sb("xT_s", [128, NCH, S_LEN], F32)
        self.b_x = [[Buf("x%d_%d" % (c, g)) for g in range(4)] for c in range(NCH)]
        self.wsl = [(sb("wsl%d" % i, [128, 4096], BF16), Buf("wsl%d" % i)) for i in range(4)]
        self.ws_rr = 0
        self.ppt = sb("ppt", [128, self.NPP], F32)
        self.b_pp = Buf("pp")
        self.modT = sb("modT", [128, self.L, 72, self.NSEQ], F32)
        self.b_mod = Buf("mod")
        self.cst = sb("cstt", [128, 1024], BF16)
        self.cstf = sb("cstf", [128, 1024], F32)
        self.b_cst = Buf("cst")
        self.rotC = sb("rotC", [128, S_LEN], BF16)
        self.rotS = sb("rotS", [128, S_LEN], BF16)
        self.b_rot = Buf("rot")
        self.der = sb("der", [128, 64], F32)
        self.b_der = Buf("der")
        pst = [es.enter_context(nc.psum_tensor("ps%d" % i, [128, 512], F32)) for i in range(8)]
        self.ps_all = [(pst[i], Buf("ps%d" % i)) for i in range(8)]
        self.ps_g = self.ps_all[0:4]
        self.ps_a = self.ps_all[4:8]
        self.ps_rr = {"g": 0, "a": 0}
        self.ident = self.cst[:, 0:128]
        self.ones = self.cst[:, 128:256]

    def _prologue(self):
        S, nc = self.S, self.nc
        L, NSEQ = self.L, self.NSEQ
        S.dma("sp", self.ppt[:, :], self.pp_d[:, :], writes=[self.b_pp])
        S.dma("pool", self.cst[:, :], self.cst_d[:, :], writes=[self.b_cst])
        S.dma("sp", self.cstf[:, :], self.cst_d[:, :], writes=[self.b_cst])
        with contextlib.ExitStack() as pes:
            ct = self.sb("ct", [128, NCH, NSEQ], F32, pes)
            cb = self.sb("cb", [128, NCH, NSEQ], BF16, pes)
            b_ct, b_cb = Buf("ct"), Buf("cb")
            S.dma("sp", ct[:], self.cT_d[:, :, :], writes=[b_ct])
            S.op("act", lambda e: e.activation(out=cb[:], in_=ct[:], func=AF.Silu), reads=[b_ct], writes=[b_cb])
            for l in range(L):
                ps, psb = self.pget()
                pv = ps[:, 0:72 * NSEQ].rearrange("p (c s) -> p c s", s=NSEQ)
                for g in range(18):
                    wt, wb = self.wslot()
                    wv = wt[:, :].rearrange("p (k n) -> p k n", k=8)
                    S.dma("pool", wv, self.ada_w_d[l].rearrange("(k p) n -> p k n", p=128)[:, :, g * 512:(g + 1) * 512],
                          writes=[wb])
                    for jj in range(4):
                        ch = g * 4 + jj
                        self.mm(pv[:, ch, :], [(wv[:, k, jj * 128:(jj + 1) * 128], cb[:, k, :]) for k in range(8)],
                                [wb, b_cb], psb)
                ab = self.ppc("adab", l * 72, 72)
                S.op("dve", lambda e, l=l, pv=pv, ab=ab: e.tensor_tensor(
                    out=self.modT[:, l, :, :], in0=pv, in1=ab.unsqueeze(2).to_broadcast([128, 72, NSEQ]), op=ALU.add),
                    reads=[psb, self.b_pp], writes=[self.b_mod])
            S.barrier()

    def _seq(self, s):
        S = self.S
        for c in range(NCH):
            S.dma("sp", self.xT[:, c, :], self.xT_d[s, :, c, :], writes=self.b_x[c])
        if "A" in self.parts:
            self._rotary(s)
        for l in range(self.L):
            if "ffn" in self.parts:
                self._ffn(l, 0, s)
            if any(p in self.parts for p in "ABC"):
                self._mixer(l, s)
            if "ffn" in self.parts:
                self._ffn(l, 1, s)
        self._final(s)
        S.barrier()

    def _derive(self, l, sub, s, gate_scale):
        S = self.S
        m = self.modT
        sh = m[:, l, (3 * sub) * 8:(3 * sub) * 8 + 8, s]
        sc = m[:, l, (3 * sub + 1) * 8:(3 * sub + 1) * 8 + 8, s]
        gt = m[:, l, (3 * sub + 2) * 8:(3 * sub + 2) * 8 + 8, s]
        ng = self.ppc("ng", (l * 3 + sub) * 8, 8)
        der = self.der
        rd = [self.b_mod, self.b_pp]
        S.op("dve", lambda e: e.scalar_tensor_tensor(out=der[:, 0:8], in0=sc, scalar=1.0, in1=ng, op0=ALU.add, op1=ALU.mult),
             reads=rd, writes=[self.b_der])
        S.op("dve", lambda e: e.tensor_copy(out=der[:, 8:16], in_=sh), reads=rd, writes=[self.b_der])
        S.op("dve", lambda e: e.tensor_scalar(out=der[:, 16:24], in0=gt, scalar1=float(gate_scale), scalar2=None, op0=ALU.mult),
             reads=rd, writes=[self.b_der])

    def _norm_mod(self, g512, gs_ap, sh_ap, dst_fn, dst_bufs, tmp, extra_reads=()):
        S = self.S
        sq, b_sq, lnt, b_ln, rstd, b_rstd, tm, b_tm = tmp
        tok = slice(g512 * 512, (g512 + 1) * 512)
        ps, psb = self.pget()
        for c in range(NCH):
            k = c % 2
            S.op("act", lambda e, c=c, k=k: e.activation(out=sq[k][:, :], in_=self.xT[:, c, tok], func=AF.Square),
                 reads=[self.b_x[c][g512]], writes=[b_sq[k]])
            S.op("pe", lambda e, c=c, k=k: e.matmul(ps[:, :], self.ones, sq[k][:, :], start=(c == 0), stop=(c == NCH - 1)),
                 reads=[b_sq[k], self.b_cst], writes=[psb])
        S.op("act", lambda e: e.activation(out=lnt[:, :], in_=ps[:, :], func=AF.Ln, scale=1.0 / D, bias=self.ppc("eps")),
             reads=[psb, self.b_pp], writes=[b_ln])
        S.op("act", lambda e: e.activation(out=rstd[:, :], in_=lnt[:, :], func=AF.Exp, scale=-0.5),
             reads=[b_ln], writes=[b_rstd])
        for c in range(NCH):
            k = c % 2
            S.op("dve", lambda e, c=c, k=k: e.tensor_tensor(out=tm[k][:, :], in0=self.xT[:, c, tok], in1=rstd[:, :], op=ALU.mult),
                 reads=[self.b_x[c][g512], b_rstd], writes=[b_tm[k]])
            if sh_ap is not None:
                S.op("act", lambda e, c=c, k=k: e.activation(out=dst_fn(c), in_=tm[k][:, :], func=AF.Identity,
                                                             scale=gs_ap[:, c:c + 1], bias=sh_ap[:, c:c + 1]),
                     reads=[b_tm[k], self.b_der, self.b_pp] + list(extra_reads), writes=[dst_bufs[c]])
            else:
                S.op("act", lambda e, c=c, k=k: e.activation(out=dst_fn(c), in_=tm[k][:, :], func=AF.Copy,
                                                             scale=gs_ap[:, c:c + 1]),
                     reads=[b_tm[k], self.b_der, self.b_pp] + list(extra_reads), writes=[dst_bufs[c]])

    def _norm_tmp(self, pes):
        sq = [self.sb("sq%d" % k, [128, 512], BF16, pes) for k in range(2)]
        tm = [self.sb("tm%d" % k, [128, 512], F32, pes) for k in range(2)]
        lnt = self.sb("lnt", [128, 512], F32, pes)
        rstd = self.sb("rstd", [128, 512], F32, pes)
        return (sq, [Buf("sq0"), Buf("sq1")], lnt, Buf("ln"), rstd, Buf("rstd"), tm, [Buf("tm0"), Buf("tm1")])

    def _ffn(self, l, i, s):
        S = self.S
        sub = 0 if i == 0 else 2
        self._derive(l, sub, s, 0.5)
        der = self.der
        with contextlib.ExitStack() as pes:
            hg = self.sb("hg", [128, NCH, 1024], BF16, pes)
            b_hg = [[Buf("hg%d_%d" % (c, t)) for t in range(2)] for c in range(NCH)]
            u = self.sb("u", [128, NJ, 1024], BF16, pes)
            b_u = [[Buf("u%d_%d" % (j, t)) for t in range(2)] for j in range(NJ)]
            sg = [self.sb("sg%d" % k, [128, 512], F32, pes) for k in range(2)]
            b_sg = [Buf("sg0"), Buf("sg1")]
            tmp = self._norm_tmp(pes)
            wi_v = self.wi_d[l, i].rearrange("(k p) n -> p k n", p=128)
            wo_v = self.wo_d[l, i].rearrange("(j p) n -> p j n", p=128)
            kk = 0
            for hf in range(2):
                for t2 in range(2):
                    g512 = hf * 2 + t2
                    self._norm_mod(g512, der[:, 0:8], der[:, 8:16],
                                   lambda c, t2=t2: hg[:, c, t2 * 512:(t2 + 1) * 512], [b_hg[c][t2] for c in range(NCH)], tmp)
                for jg in range(6):
                    ncol = 512 if jg < 5 else 256
                    wg, wgb = self.wslot()
                    wu, wub = self.wslot()
                    wgv = wg[:, 0:8 * ncol].rearrange("p (k n) -> p k n", k=8)
                    wuv = wu[:, 0:8 * ncol].rearrange("p (k n) -> p k n", k=8)
                    S.dma("pool", wgv, wi_v[:, :, jg * 512:jg * 512 + ncol], writes=[wgb])
                    S.dma("pool", wuv, wi_v[:, :, DFF + jg * 512:DFF + jg * 512 + ncol], writes=[wub])
                    for jj in range(ncol // 128):
                        j = jg * 4 + jj
                        for t2 in range(2):
                            tk = slice(t2 * 512, (t2 + 1) * 512)
                            pg, pgb = self.pget()
                            pu, pub = self.pget()
                            rd = [b_hg[c][t2] for c in range(NCH)]
                            self.mm(pg[:, :], [(wgv[:, k, jj * 128:(jj + 1) * 128], hg[:, k, tk]) for k in range(8)], rd + [wgb], pgb)
                            self.mm(pu[:, :], [(wuv[:, k, jj * 128:(jj + 1) * 128], hg[:, k, tk]) for k in range(8)], rd + [wub], pub)
                            q = kk % 2
                            kk += 1
                            S.op("act", lambda e, q=q, pg=pg: e.activation(out=sg[q][:, :], in_=pg[:, :], func=AF.Silu),
                                 reads=[pgb], writes=[b_sg[q]])
                            S.op("dve", lambda e, q=q, pu=pu, j=j, tk=tk: e.tensor_tensor(out=u[:, j, tk], in0=sg[q][:, :], in1=pu[:, :], op=ALU.mult),
                                 reads=[b_sg[q], pub], writes=[b_u[j][t2]])
                for chh in range(2):
                    tiles = []
                    for (j0, nj) in ((0, 8), (8, 8), (16, 6)):
                        wt, wb = self.wslot()
                        wv = wt[:, 0:nj * 512].rearrange("p (j n) -> p j n", j=nj)
                        S.dma("pool", wv, wo_v[:, j0:j0 + nj, chh * 512:(chh + 1) * 512], writes=[wb])
                        tiles.append((wv, wb, j0, nj))
                    for t2 in range(2):
                        g512 = hf * 2 + t2
                        tk = slice(t2 * 512, (t2 + 1) * 512)
                        tok = slice(g512 * 512, (g512 + 1) * 512)
                        for cc in range(4):
                            c = chh * 4 + cc
                            pz, pzb = self.pget()
                            pairs = []
                            for (wv, wb, j0, nj) in tiles:
                                for jj in range(nj):
                                    pairs.append((wv[:, jj, cc * 128:(cc + 1) * 128], u[:, j0 + jj, tk]))
                            self.mm(pz[:, :], pairs, [b_u[j][t2] for j in range(NJ)] + [t[1] for t in tiles], pzb)
                            S.op("dve", lambda e, pz=pz, c=c, tok=tok: e.scalar_tensor_tensor(
                                out=self.xT[:, c, tok], in0=pz[:, :], scalar=der[:, 16 + c:17 + c], in1=self.xT[:, c, tok],
                                op0=ALU.mult, op1=ALU.add), reads=[pzb, self.b_der, self.b_x[c][g512]], writes=[self.b_x[c][g512]])
            S.barrier()

    def _final(self, s):
        S = self.S
        with contextlib.ExitStack() as pes:
            tmp = self._norm_tmp(pes)
            ob = [self.sb("ob%d" % k, [128, 512], F32, pes) for k in range(2)]
            b_ob = [Buf("ob0"), Buf("ob1")]
            fn = self.ppc("fn", 0, 8)
            for g in range(4):
                tok = slice(g * 512, (g + 1) * 512)
                cnt = [0]

                def dst(c):
                    return ob[c % 2][:, :]
                sq, b_sq, lnt, b_ln, rstd, b_rstd, tm, b_tm = tmp
                ps, psb = self.pget()
                for c in range(NCH):
                    k = c % 2
                    S.op("act", lambda e, c=c, k=k: e.activation(out=sq[k][:, :], in_=self.xT[:, c, tok], func=AF.Square),
                         reads=[self.b_x[c][g]], writes=[b_sq[k]])
                    S.op("pe", lambda e, c=c, k=k: e.matmul(ps[:, :], self.ones, sq[k][:, :], start=(c == 0), stop=(c == NCH - 1)),
                         reads=[b_sq[k], self.b_cst], writes=[psb])
                S.op("act", lambda e: e.activation(out=lnt[:, :], in_=ps[:, :], func=AF.Ln, scale=1.0 / D, bias=self.ppc("eps")),
                     reads=[psb, self.b_pp], writes=[b_ln])
                S.op("act", lambda e: e.activation(out=rstd[:, :], in_=lnt[:, :], func=AF.Exp, scale=-0.5),
                     reads=[b_ln], writes=[b_rstd])
                for c in range(NCH):
                    k = c % 2
                    S.op("dve", lambda e, c=c, k=k: e.scalar_tensor_tensor(
                        out=ob[k][:, :], in0=self.xT[:, c, tok], scalar=fn[:, c:c + 1], in1=rstd[:, :], op0=ALU.mult, op1=ALU.mult),
                        reads=[self.b_x[c][g], b_rstd, self.b_pp], writes=[b_ob[k]])
                    S.dma("sp", self.out_d[s, :, c, tok], ob[k][:, :], reads=[b_ob[k]])
            S.barrier()

    def _rotary(self, s):
        S = self.S
        with contextlib.ExitStack() as pes:
            posi = self.sb("posi", [128, S_LEN], I32, pes)
            posf = self.sb("posf", [128, S_LEN], F32, pes)
            u = self.sb("ru", [128, S_LEN], F32, pes)
            ui = self.sb("rui", [128, S_LEN], I32, pes)
            uf = self.sb("ruf", [128, S_LEN], F32, pes)
            b = [Buf("r%d" % i) for i in range(5)]
            S.dma("sp", posi[:, :], self.pos_d[s:s + 1, :].to_broadcast([128, S_LEN]), writes=[b[0]])
            S.op("dve", lambda e: e.tensor_copy(out=posf[:, :], in_=posi[:, :]), reads=[b[0]], writes=[b[1]])
            for dst, add, sgn in ((self.rotS, 0.0, True), (self.rotC, 0.25, False)):
                S.op("dve", lambda e, add=add: e.tensor_scalar(out=u[:, :], in0=posf[:, :], scalar1=self.ppc("invf"), scalar2=float(add),
                                                               op0=ALU.mult, op1=ALU.add), reads=[b[1], self.b_pp], writes=[b[2]])
                S.op("dve", lambda e: e.tensor_copy(out=ui[:, :], in_=u[:, :]), reads=[b[2]], writes=[b[3]])
                S.op("dve", lambda e: e.tensor_copy(out=uf[:, :], in_=ui[:, :]), reads=[b[3]], writes=[b[4]])
                S.op("dve", lambda e: e.tensor_tensor(out=u[:, :], in0=u[:, :], in1=uf[:, :], op=ALU.subtract), reads=[b[2], b[4]], writes=[b[2]])
                S.op("act", lambda e: e.activation(out=uf[:, :], in_=u[:, :], func=AF.Sin, scale=TWO_PI), reads=[b[2]], writes=[b[4]])
                if sgn:
                    S.op("dve", lambda e, dst=dst: e.tensor_scalar(out=dst[:, :], in0=uf[:, :], scalar1=self.ppc("rsign"), scalar2=None, op0=ALU.mult),
                         reads=[b[4], self.b_pp], writes=[self.b_rot])
                else:
                    S.op("dve", lambda e, dst=dst: e.tensor_copy(out=dst[:, :], in_=uf[:, :]), reads=[b[4]], writes=[self.b_rot])
            S.barrier()

    def _mixer(self, l, s):
        S = self.S
        self._derive(l, 1, s, 1.0)
        der = self.der
        win_v = self.win_d[l].rearrange("(k p) n -> p k n", p=128)
        self.win_v = win_v
        with contextlib.ExitStack() as pes:
            hT = self.sb("hT", [128, NCH, S_LEN], BF16, pes)
            mT = self.sb("mT", [128, NCH, S_LEN], BF16, pes)
            b_h = [[Buf("h%d_%d" % (c, g)) for g in range(4)] for c in range(NCH)]
            b_m = [[Buf("m%d_%d" % (c, g)) for g in range(4)] for c in range(NCH)]
            self.hT, self.b_h = hT, b_h
            with contextlib.ExitStack() as p2:
                tmp = self._norm_tmp(p2)
                for g in range(4):
                    self._norm_mod(g, der[:, 0:8], der[:, 8:16], lambda c, g=g: hT[:, c, g * 512:(g + 1) * 512],
                                   [b_h[c][g] for c in range(NCH)], tmp)
                for c in range(NCH):
                    S.dma("sp", self.xs_d[:, c, :], self.xT[:, c, :], reads=self.b_x[c], writes=[self.b_xs])
                S.barrier()
            xf = self.xT[:, :, :].rearrange("p c t -> p (c t)")
            xb = xf.bitcast(BF16)
            self.al_f, self.al_b = xf, xb
            oT = xb[:, 0:8192].rearrange("p (h t) -> p h t", h=4)
            b_o = [[Buf("o%d_%d" % (hp, g)) for g in range(4)] for hp in range(4)]
            first = True
            for bi, name in enumerate("ABC"):
                if name not in self.parts:
                    continue
                with contextlib.ExitStack() as p3:
                    getattr(self, "_mix_" + name)(l, s, oT, b_o, p3)
                    if self.dbg_d is not None and name == "C" and l == 0 and s == 0:
                        S.dma("pool", self.dbg_d[:, :, :], oT, reads=[b for bb in b_o for b in bb])
                    self._branch(l, bi, oT, b_o, mT, b_m, first, p3)
                    S.barrier()
                first = False
            for c in range(NCH):
                S.dma("sp", self.xT[:, c, :], self.xs_d[:, c, :], reads=[self.b_xs], writes=self.b_x[c])
            tiles = []
            wo_v = self.wout_d[l].rearrange("(k p) n -> p k n", p=128)
            for chh in range(2):
                wt, wb = self.wslot()
                wv = wt[:, :].rearrange("p (k n) -> p k n", k=8)
                S.dma("pool", wv, wo_v[:, :, chh * 512:(chh + 1) * 512], writes=[wb])
                tiles.append((wv, wb))
            for g in range(4):
                tok = slice(g * 512, (g + 1) * 512)
                for c in range(NCH):
                    wv, wb = tiles[c // 4]
                    cc = c % 4
                    pz, pzb = self.pget()
                    self.mm(pz[:, :], [(wv[:, k, cc * 128:(cc + 1) * 128], mT[:, k, tok]) for k in range(8)],
                            [b_m[k][g] for k in range(8)] + [wb], pzb)
                    S.op("dve", lambda e, pz=pz, c=c, tok=tok: e.scalar_tensor_tensor(
                        out=self.xT[:, c, tok], in0=pz[:, :], scalar=der[:, 16 + c:17 + c], in1=self.xT[:, c, tok],
                        op0=ALU.mult, op1=ALU.add), reads=[pzb, self.b_der, self.b_x[c][g]], writes=[self.b_x[c][g]])
            S.barrier()

    def _branch(self, l, bi, oT, b_o, mT, b_m, first, pes):
        S = self.S
        hT, b_h = self.hT, self.b_h
        sgt = [self.sb("bsg%d" % k, [128, 512], F32, pes) for k in range(2)]
        b_sgt = [Buf("bsg0"), Buf("bsg1")]
        tmm = [self.sb("btm%d" % k, [128, 512], F32, pes) for k in range(2)]
        b_tmm = [Buf("btm0"), Buf("btm1")]
        wbt, wbb = self.wslot()
        wbv = wbt[:, :].rearrange("p (h n) -> p h n", h=4)
        S.dma("pool", wbv, self.wbr_d[l, bi].rearrange("(h p) n -> p h n", p=128), writes=[wbb])
        kk = 0
        for chh in range(2):
            wt, wb = self.wslot()
            wv = wt[:, :].rearrange("p (k n) -> p k n", k=8)
            c0 = OFF_GZ + bi * 1024 + chh * 512
            S.dma("pool", wv, self.win_v[:, :, c0:c0 + 512], writes=[wb])
            for g in range(4):
                tok = slice(g * 512, (g + 1) * 512)
                for cc in range(4):
                    c = chh * 4 + cc
                    py, pyb = self.pget()
                    pg, pgb = self.pget()
                    self.mm(py[:, :], [(wbv[:, hp, c * 128:(c + 1) * 128], oT[:, hp, tok]) for hp in range(4)],
                            [b_o[hp][g] for hp in range(4)] + [wbb], pyb)
                    self.mm(pg[:, :], [(wv[:, k, cc * 128:(cc + 1) * 128], hT[:, k, tok]) for k in range(8)],
                            [b_h[k][g] for k in range(8)] + [wb], pgb)
                    q = kk % 2
                    kk += 1
                    S.op("act", lambda e, q=q, pg=pg: e.activation(out=sgt[q][:, :], in_=pg[:, :], func=AF.Sigmoid),
                         reads=[pgb], writes=[b_sgt[q]])
                    if first:
                        S.op("dve", lambda e, q=q, py=py, c=c, tok=tok: e.tensor_tensor(out=mT[:, c, tok], in0=py[:, :], in1=sgt[q][:, :], op=ALU.mult),
                             reads=[pyb, b_sgt[q]], writes=[b_m[c][g]])
                    else:
                        S.op("dve", lambda e, q=q, py=py: e.tensor_tensor(out=tmm[q][:, :], in0=py[:, :], in1=sgt[q][:, :], op=ALU.mult),
                             reads=[pyb, b_sgt[q]], writes=[b_tmm[q]])
                        S.op("pool", lambda e, q=q, c=c, tok=tok: e.tensor_tensor(out=mT[:, c, tok], in0=mT[:, c, tok], in1=tmm[q][:, :], op=ALU.add),
                             reads=[b_tmm[q], b_m[c][g]], writes=[b_m[c][g]])

    def _vtm(self, col0, vtm, b_v):
        S = self.S
        hT, b_h = self.hT, self.b_h
        wt, wb = self.wslot()
        wv = wt[:, :].rearrange("p (k n) -> p k n", k=8)
        S.dma("pool", wv, self.win_v[:, :, col0:col0 + 512], writes=[wb])
        for tt in range(16):
            pv, pvb = self.pget()
            g = tt // 4
            self.mm(pv[:, :], [(hT[:, k, tt * 128:(tt + 1) * 128], wv[:, k, :]) for k in range(8)],
                    [b_h[k][g] for k in range(8)] + [wb], pvb)
            S.op("act", lambda e, pv=pv, tt=tt: e.activation(out=vtm[:, tt, :], in_=pv[:, :], func=AF.Copy), reads=[pvb], writes=[b_v[tt]])

    def _attn_core(self, hp, qf, kf, b_q, b_k, vtm, b_v, oT, b_o, blocks_fn, mask_fn, mask_reads, extra_mm, pt, b_pt, rdt, b_rd):
        for hh in range(2):
            self._attn_core_one(hp, hh, qf, kf, b_q, b_k, vtm, b_v, oT, b_o, blocks_fn, mask_fn, mask_reads, extra_mm, pt, b_pt, rdt, b_rd)

    def _attn_core_one(self, hp, hh, qf, kf, b_q, b_k, vtm, b_v, oT, b_o, blocks_fn, mask_fn, mask_reads, extra_mm, pt, b_pt, rdt, b_rd):
        S = self.S
        kk = 0
        if True:
            pb = hh * 64
            h = hp * 2 + hh
            for qg in range(4):
                qs = slice(qg * 512, (qg + 1) * 512)
                pn, pnb = self.pget("a")
                pd, pdb = self.pget("a")
                kts = blocks_fn(qg)
                for idx, kt in enumerate(kts):
                    ps_s, psb = self.pget()
                    ex = extra_mm(kt, qg) if extra_mm else None
                    S.op("pe", lambda e, ps_s=ps_s, kt=kt: e.matmul(ps_s[:, :], kf[pb:pb + 64, kt * 128:(kt + 1) * 128], qf[pb:pb + 64, qs],
                                                                   start=True, stop=(ex is None)),
                         reads=[b_k[kt // 4], b_q[qg]], writes=[psb])
                    if ex is not None:
                        S.op("pe", lambda e, ps_s=ps_s, ex=ex: e.matmul(ps_s[:, :], ex[0], ex[1], start=False, stop=True),
                             reads=ex[2], writes=[psb])
                    q = kk % 3
                    kk += 1
                    S.op("act", lambda e, q=q, ps_s=ps_s: e.activation(out=pt[q][:, :], in_=ps_s[:, :], func=AF.Exp, scale=0.125),
                         reads=[psb], writes=[b_pt[q]])
                    eng = "dve" if kk % 2 == 0 else "pool"
                    mk = mask_fn(kt, qg)
                    S.op(eng, lambda e, q=q, mk=mk: e.tensor_tensor(out=pt[3 + q][:, :], in0=pt[q][:, :], in1=mk, op=ALU.mult),
                         reads=[b_pt[q]] + list(mask_reads), writes=[b_pt[3 + q]])
                    S.op("pe", lambda e, q=q, kt=kt, pn=pn, idx=idx: e.matmul(pn[0:64, :], vtm[:, kt, h * 64:(h + 1) * 64], pt[3 + q][:, :],
                                                                           start=(idx == 0), stop=(idx == len(kts) - 1)),
                         reads=[b_v[kt], b_pt[3 + q]], writes=[pnb])
                    S.op("pe", lambda e, q=q, pd=pd, idx=idx: e.matmul(pd[0:64, :], self.ones[:, 0:64], pt[3 + q][:, :],
                                                                    start=(idx == 0), stop=(idx == len(kts) - 1)),
                         reads=[self.b_cst, b_pt[3 + q]], writes=[pdb])
                S.op("dve", lambda e, pd=pd: e.reciprocal(out=rdt[:, :], in_=pd[0:64, :]), reads=[pdb], writes=[b_rd])
                S.op("dve", lambda e, pn=pn, qs=qs: e.tensor_tensor(out=oT[pb:pb + 64, hp, qs], in0=pn[0:64, :], in1=rdt[:, :], op=ALU.mult),
                     reads=[pnb, b_rd], writes=[b_o[hp][qg]])

    def _al(self, off_bytes, shape, dtype):
        n = int(np.prod(shape[1:]))
        if dtype == BF16:
            v = self.al_b[0:shape[0], off_bytes // 2:off_bytes // 2 + n]
        else:
            v = self.al_f[0:shape[0], off_bytes // 4:off_bytes // 4 + n]
        if len(shape) == 3:
            v = v.rearrange("p (a b) -> p a b", a=shape[1])
        return v

    def _mix_A(self, l, s, oT, b_o, pes):
        S = self.S
        hT, b_h = self.hT, self.b_h
        K = 1024
        vtm = self._al(16 * K, [128, 16, 512], BF16)
        b_v = [Buf("v%d" % t) for t in range(16)]
        qf = self._al(32 * K, [128, S_LEN], BF16)
        kf = self._al(36 * K, [128, S_LEN], BF16)
        ta = self._al(40 * K, [128, 3968], BF16)
        b_ta = Buf("ta")
        pt = [self._al(48 * K + i * K, [128, 512], BF16) for i in range(6)]
        b_pt = [Buf("pt%d" % i) for i in range(6)]
        t1 = [self._al(54 * K + i * 2 * K, [128, 512], F32) for i in range(2)]
        b_t1 = [Buf("t1a"), Buf("t1b")]
        rdt = self._al(58 * K, [64, 512], F32)
        b_rd = Buf("rd")
        b_q = [Buf("q%d" % g) for g in range(4)]
        b_k = [Buf("k%d" % g) for g in range(4)]
        S.dma("pool", ta, self.ta_d[:, :], writes=[b_ta])
        self._vtm(OFF_A_V, vtm, b_v)

        def blocks(qg):
            out = []
            for kt in range(16):
                dl = 128 * kt - 512 * qg
                if dl - 511 > 1024 or dl + 127 < -1024:
                    continue
                out.append(kt)
            return out

        def mask(kt, qg):
            y0 = 1920 - 128 * kt + 512 * qg
            return ta[:, y0:y0 + 512]
        for hp in range(4):
            wt, wb = self.wslot()
            wv = wt[:, :].rearrange("p (k n) -> p k n", k=8)
            S.dma("pool", wv, self.win_v[:, :, OFF_A_QK + hp * 512:OFF_A_QK + (hp + 1) * 512], writes=[wb])
            for g in range(4):
                tok = slice(g * 512, (g + 1) * 512)
                for wi_, (dst, bd) in enumerate(((qf, b_q), (kf, b_k))):
                    p1, p1b = self.pget()
                    p2, p2b = self.pget()
                    rd = [b_h[k][g] for k in range(8)] + [wb]
                    c0 = wi_ * 256
                    self.mm(p1[:, :], [(wv[:, k, c0:c0 + 128], hT[:, k, tok]) for k in range(8)], rd, p1b)
                    self.mm(p2[:, :], [(wv[:, k, c0 + 128:c0 + 256], hT[:, k, tok]) for k in range(8)], rd, p2b)
                    S.op("dve", lambda e, p1=p1, tok=tok: e.tensor_tensor(out=t1[0][:, :], in0=p1[:, :], in1=self.rotC[:, tok], op=ALU.mult),
                         reads=[p1b, self.b_rot], writes=[b_t1[0]])
                    S.op("dve", lambda e, p2=p2, tok=tok: e.tensor_tensor(out=t1[1][:, :], in0=p2[:, :], in1=self.rotS[:, tok], op=ALU.mult),
                         reads=[p2b, self.b_rot], writes=[b_t1[1]])
                    S.op("pool", lambda e, dst=dst, tok=tok: e.tensor_tensor(out=dst[:, tok], in0=t1[0][:, :], in1=t1[1][:, :], op=ALU.add),
                         reads=[b_t1[0], b_t1[1]], writes=[bd[g]])
            self._attn_core(hp, qf, kf, b_q, b_k, vtm, b_v, oT, b_o, blocks, mask, [b_ta], None, pt, b_pt, rdt, b_rd)

    def _mix_B(self, l, s, oT, b_o, pes):
        S = self.S
        hT, b_h = self.hT, self.b_h
        K = 1024
        vtm = self._al(16 * K, [128, 16, 512], BF16)
        b_v = [Buf("v%d" % t) for t in range(16)]
        qf = self._al(32 * K, [128, S_LEN], BF16)
        kf = self._al(36 * K, [128, S_LEN], BF16)
        bu = self._al(40 * K, [128, S_LEN], BF16)
        bw = self._al(44 * K, [128, S_LEN], BF16)
        b_bb = Buf("bb")
        pt = [self._al(48 * K + i * K, [128, 512], BF16) for i in range(6)]
        b_pt = [Buf("pt%d" % i) for i in range(6)]
        eraw = self._al(54 * K, [128, 1920], F32)
        b_er = Buf("er")
        rdt = self._al(62 * K, [64, 512], F32)
        b_rd = Buf("rd")
        eb = self.sb("eb", [128, 1920], BF16, pes)
        b_eb = Buf("eb")
        b_q = [Buf("q%d" % g) for g in range(4)]
        b_k = [Buf("k%d" % g) for g in range(4)]
        S.dma("pool", bu[0:32, :], self.bu_d[:, :], writes=[b_bb])
        S.dma("pool", bw[0:32, :], self.bw_d[:, :], writes=[b_bb])
        self._vtm(OFF_B_V, vtm, b_v)
        rows = 32
        rs = np.clip(np.arange(rows) - 4, 0, rows - 8)

        def blocks(qg):
            a = 8 * qg
            out = []
            for i in range(16):
                ok = False
                for qr in range(a, a + 8):
                    for kr in (2 * i, 2 * i + 1):
                        if rs[qr] <= kr < rs[qr] + 8:
                            ok = True
                if ok:
                    j0 = a - 2 * i + 14
                    assert 0 <= j0 <= 22, (a, i, j0)
                    out.append(i)
            return out

        def mask(i, qg):
            j0 = 8 * qg - 2 * i + 14
            return eb[:, j0 * 64:j0 * 64 + 512]

        def extra(i, qg):
            return (bu[0:32, i * 128:(i + 1) * 128], bw[0:32, qg * 512:(qg + 1) * 512], [b_bb])
        for hp in range(4):
            wt, wb = self.wslot()
            wv = wt[:, 0:2048].rearrange("p (k n) -> p k n", k=8)
            S.dma("pool", wv, self.win_v[:, :, OFF_B_QK + hp * 256:OFF_B_QK + (hp + 1) * 256], writes=[wb])
            for g in range(4):
                tok = slice(g * 512, (g + 1) * 512)
                for wi_, (dst, bd) in enumerate(((qf, b_q), (kf, b_k))):
                    p1, p1b = self.pget()
                    rd = [b_h[k][g] for k in range(8)] + [wb]
                    self.mm(p1[:, :], [(wv[:, k, wi_ * 128:(wi_ + 1) * 128], hT[:, k, tok]) for k in range(8)], rd, p1b)
                    S.op("act", lambda e, p1=p1, dst=dst, tok=tok: e.activation(out=dst[:, tok], in_=p1[:, :], func=AF.Copy),
                         reads=[p1b], writes=[bd[g]])
            for hh in range(2):
                pass
            self._attn_core_B(l, hp, qf, kf, b_q, b_k, vtm, b_v, oT, b_o, blocks, mask, extra, pt, b_pt, rdt, b_rd, eraw, b_er, eb, b_eb)

    def _attn_core_B(self, l, hp, qf, kf, b_q, b_k, vtm, b_v, oT, b_o, blocks, mask, extra, pt, b_pt, rdt, b_rd, eraw, b_er, eb, b_eb):
        S = self.S
        for hh in range(2):
            h = hp * 2 + hh
            S.dma("sp", eraw[0:64, :], self.braw_d[l, h, :, 64:31 * 64], writes=[b_er])
            S.dma("sp", eraw[64:128, :], self.braw_d[l, h, :, 0:30 * 64], writes=[b_er])
            S.op("act", lambda e: e.activation(out=eb[:, :], in_=eraw[:, :], func=AF.Exp), reads=[b_er], writes=[b_eb])
            self._attn_core_one(hp, hh, qf, kf, b_q, b_k, vtm, b_v, oT, b_o, blocks, mask, [b_eb], extra, pt, b_pt, rdt, b_rd)

    def _mix_C(self, l, s, oT, b_o, pes):
        S = self.S
        hT, b_h = self.hT, self.b_h
        K = 1024
        cdec = DECAY_SCALE
        cur = [16 * K]

        def T(shape, dt, name):
            n = int(np.prod(shape[1:])) * (2 if dt == BF16 else 4)
            off = cur[0]
            cur[0] += n
            assert cur[0] <= 64 * K, cur[0]
            return self._al(off, [128] + list(shape[1:]), dt), Buf(name)
        B = [128, 512]
        rS, b_rS = T(B, F32, "rS")
        kS, b_kS = T(B, F32, "kS")
        sgm, b_sgm = T(B, F32, "sgm")
        ai, b_ai = T(B, F32, "ai")
        pre, b_pre = T(B, F32, "pre")
        ein, b_ein = T(B, F32, "ein")
        eex, b_eex = T(B, F32, "eex")
        rem, b_rem = T(B, F32, "rem")
        kkn, b_kkn = T(B, F32, "kkn")
        kmod, b_kmod = T(B, F32, "kmod")
        kb, b_kb = T(B, F32, "kb")
        t0, b_t0 = T(B, F32, "t0")
        t1, b_t1 = T(B, F32, "t1")
        cm, b_cm = self.sb("cm", B, F32, pes), Buf("cm")
        acc, b_acc = T([128, S_LEN], BF16, "acc")
        th, b_th = T(B, BF16, "th")
        xa, b_xa = T(B, BF16, "xa")
        aF, b_aF = T(B, BF16, "aF")
        bF, b_bF = T(B, BF16, "bF")
        kF, b_kF = T(B, BF16, "kF")
        rF, b_rF = T(B, BF16, "rF")
        bgF, b_bgF = T(B, BF16, "bgF")
        kgF, b_kgF = T(B, BF16, "kgF")
        vF, b_vF = T(B, BF16, "vF")
        sqb, b_sqb = T(B, BF16, "sqb")
        vT, b_vT = T([128, 4, 128], BF16, "vT")
        bgT, b_bgT = T([128, 4, 128], BF16, "bgT")
        kgT, b_kgT = T([128, 4, 128], BF16, "kgT")
        sq = [T([128, 128], BF16, "sqm%d" % i) for i in range(4)]
        AkT, b_AkT = T([128, 128], BF16, "AkT")
        MrbT, b_MrbT = T([128, 128], BF16, "MrbT")
        MrkT, b_MrkT = T([128, 128], BF16, "MrkT")
        ynT, b_ynT = T([128, 128], BF16, "ynT")
        X = [T([128, 64], BF16, "X%d" % i) for i in range(2)]
        H, b_H = T([128, 64], BF16, "H")
        tH, b_tH = T([128, 64], F32, "tH")
        gC, b_gC = T([128, 4], F32, "gC")
        st6, b_st6 = T([128, 6], F32, "st6")
        mv, b_mv = T([128, 2], F32, "mv")
        murow, b_mu = self.sb("murow", B, F32, pes), Buf("murow")
        gupt, b_gup = T(B, BF16, "gupt")
        cst = self.cst
        LT, LE, GT, GE, BLK = (cst[:, 256:384], cst[:, 384:512], cst[:, 512:640], cst[:, 640:768], cst[:, 768:896])
        S.dma("sp", cm[:, :], self.cst2_d[:, :], writes=[b_cm])
        S.dma("pool", gupt, self.gup_d[l], writes=[b_gup])
        wlt, b_wl = self.sb("wlt", [128, 2048], BF16, pes), Buf("wl")
        wl = wlt[0:64, 0:2048].rearrange("p (a n) -> p a n", a=4)
        for d in range(2):
            S.dma("pool", wl[:, d * 2, :], self.wup_d[l, d], writes=[b_wl])
            S.dma("pool", wl[:, d * 2 + 1, :], self.aup_d[l, d], writes=[b_wl])
        wgt, b_wg = self.sb("wgt", [128, 1024], BF16, pes), Buf("wg")
        wg = wgt[:, 0:1024].rearrange("p (k n) -> p k n", k=8)
        S.dma("pool", wg, self.win_v[:, :, OFF_C_G:OFF_C_G + 128], writes=[b_wg])
        mode = getattr(self, "cmode", None)
        for hp in range(4):
            for d in range(2):
                if mode in ("acc_d0", "yn_d0", "bonus_d0") and d == 1:
                    continue
                if mode == "acc_d1" and d == 0:
                    continue
                wrt, b_wr = self.wslot()
                w1t, b_w1 = self.wslot()
                wr = wrt[:, :].rearrange("p (k n) -> p k n", k=8)
                w1 = w1t[:, :].rearrange("p (k n) -> p k n", k=8)
                for j, c0 in enumerate((OFF_C_R + hp * 128, OFF_C_K + hp * 128, OFF_C_V + hp * 128)):
                    S.dma("pool", wr[:, :, j * 128:(j + 1) * 128], self.win_v[:, :, c0:c0 + 128], writes=[b_wr])
                    S.dma("sp", murow[:, j * 128:(j + 1) * 128],
                          self.murkv_d[l, d, j:j + 1, hp * 128:(hp + 1) * 128].to_broadcast([128, 128]), writes=[b_mu])
                S.dma("pool", wr[:, :, 384:448], self.win_v[:, :, OFF_C_LW + d * 64:OFF_C_LW + d * 64 + 64], writes=[b_wr])
                S.dma("pool", wr[:, :, 448:512], self.win_v[:, :, OFF_C_LA + d * 64:OFF_C_LA + d * 64 + 64], writes=[b_wr])
                S.dma("sp", murow[:, 384:448], self.muw_d[l, d:d + 1, :].to_broadcast([128, 64]), writes=[b_mu])
                S.dma("sp", murow[:, 448:512], self.mua_d[l, d:d + 1, :].to_broadcast([128, 64]), writes=[b_mu])
                mub = murow[:, :].unsqueeze(1).to_broadcast([128, 8, 512])
                S.op("dve", lambda e, w1=w1, wr=wr, mub=mub: e.tensor_tensor(out=w1, in0=wr, in1=mub, op=ALU.mult), reads=[b_wr, b_mu], writes=[b_w1])
                S.op("dve", lambda e, w1=w1, wr=wr: e.tensor_tensor(out=wr, in0=wr, in1=w1, op=ALU.subtract), reads=[b_wr, b_w1], writes=[b_wr])
                S.op("pool", lambda e: e.memset(H, 0.0), writes=[b_H])
                strict_st, strict_ts, incl_st = (LT, GT, LE) if d == 0 else (GT, LT, GE)
                w0c = self.ppc("w0", (l * 2 + d) * 4 + hp)
                a0c = self.ppc("a0", (l * 2 + d) * 4 + hp)
                for bi_ in range(4):
                    bi = bi_ if d == 0 else 3 - bi_
                    t0_ = bi * 512
                    tok = slice(t0_, t0_ + 512)
                    if d == 0:
                        lo, hi = (1 if bi == 0 else 0), 512
                        nb = slice(t0_ + lo - 1, t0_ + 511)
                    else:
                        lo, hi = 0, (511 if bi == 3 else 512)
                        nb = slice(t0_ + 1, t0_ + hi + 1)
                    rdh = [b_h[k][bi] for k in range(8)]
                    if d == 0 and bi > 0:
                        rdh += [b_h[k][bi - 1] for k in range(8)]
                    if d == 1 and bi < 3:
                        rdh += [b_h[k][bi + 1] for k in range(8)]

                    def proj(c0, n, ps, psb):
                        m = ps[0:n, :]
                        for k in range(8):
                            S.op("pe", lambda e, k=k: e.matmul(m, wr[:, k, c0:c0 + n], hT[:, k, tok], start=(k == 0), stop=False),
                                 reads=rdh + [b_wr], writes=[psb])
                        for k in range(8):
                            S.op("pe", lambda e, k=k: e.matmul(ps[0:n, lo:hi], w1[:, k, c0:c0 + n], hT[:, k, nb], start=False, stop=(k == 7)),
                                 reads=rdh + [b_w1], writes=[psb])
                    ps, psb = self.pget()
                    proj(0, 128, ps, psb)
                    S.op("act", lambda e, ps=ps: e.activation(out=rS, in_=ps[:, :], func=AF.Copy), reads=[psb], writes=[b_rS])
                    ps, psb = self.pget()
                    proj(128, 128, ps, psb)
                    S.op("act", lambda e, ps=ps: e.activation(out=kS, in_=ps[:, :], func=AF.Copy), reads=[psb], writes=[b_kS])
                    ps, psb = self.pget()
                    proj(256, 128, ps, psb)
                    S.op("act", lambda e, ps=ps: e.activation(out=vF, in_=ps[:, :], func=AF.Copy), reads=[psb], writes=[b_vF])
                    ps, psb = self.pget()
                    proj(384, 64, ps, psb)
                    S.op("act", lambda e, ps=ps: e.activation(out=th[0:64, :], in_=ps[0:64, :], func=AF.Tanh), reads=[psb], writes=[b_th])
                    ps, psb = self.pget()
                    proj(448, 64, ps, psb)
                    S.op("act", lambda e, ps=ps: e.activation(out=xa[0:64, :], in_=ps[0:64, :], func=AF.Copy), reads=[psb], writes=[b_xa])
                    ps, psb = self.pget()
                    self.mm(ps[:, :], [(wl[:, d * 2, hp * 128:(hp + 1) * 128], th[0:64, :])], [b_wl, b_th], psb)
                    S.op("act", lambda e, ps=ps: e.activation(out=sgm, in_=ps[:, :], func=AF.Sigmoid, bias=w0c), reads=[psb, self.b_pp], writes=[b_sgm])
                    ps, psb = self.pget()
                    self.mm(ps[:, :], [(wl[:, d * 2 + 1, hp * 128:(hp + 1) * 128], xa[0:64, :])], [b_wl, b_xa], psb)
                    S.op("act", lambda e, ps=ps: e.activation(out=ai, in_=ps[:, :], func=AF.Sigmoid, bias=a0c), reads=[psb, self.b_pp], writes=[b_ai])
                    S.op("dve", lambda e: e.tensor_tensor_scan(out=pre, data0=cm[:, :], data1=sgm, initial=0.0, op0=ALU.mult, op1=ALU.add),
                         reads=[b_cm, b_sgm], writes=[b_pre])
                    pre3 = pre.rearrange("p (a b) -> p a b", a=4)
                    etb = pre3[:, :, 127:128].to_broadcast([128, 4, 128])
                    v3 = lambda x: x.rearrange("p (a b) -> p a b", a=4)
                    if d == 0:
                        S.op("dve", lambda e: e.tensor_copy(out=ein, in_=pre), reads=[b_pre], writes=[b_ein])
                        S.op("dve", lambda e: e.tensor_tensor(out=eex, in0=pre, in1=sgm, op=ALU.subtract), reads=[b_pre, b_sgm], writes=[b_eex])
                        S.op("dve", lambda e: e.tensor_tensor(out=v3(rem), in0=etb, in1=pre3, op=ALU.subtract), reads=[b_pre], writes=[b_rem])
                    else:
                        S.op("dve", lambda e: e.tensor_tensor(out=v3(eex), in0=etb, in1=pre3, op=ALU.subtract), reads=[b_pre], writes=[b_eex])
                        S.op("dve", lambda e: e.tensor_tensor(out=ein, in0=eex, in1=sgm, op=ALU.add), reads=[b_eex, b_sgm], writes=[b_ein])
                        S.op("dve", lambda e: e.tensor_tensor(out=rem, in0=pre, in1=sgm, op=ALU.subtract), reads=[b_pre, b_sgm], writes=[b_rem])
                    S.op("act", lambda e: e.activation(out=gC, in_=pre3[:, :, 127], func=AF.Exp, scale=-cdec), reads=[b_pre], writes=[b_gC])
                    S.op("dve", lambda e: e.tensor_scalar(out=t0, in0=kS, scalar1=self.ppc("k_k", l * 4 + hp), scalar2=None, op0=ALU.mult),
                         reads=[b_kS, self.b_pp], writes=[b_t0])
                    S.op("act", lambda e: e.activation(out=sqb, in_=t0, func=AF.Square), reads=[b_t0], writes=[b_sqb])
                    ps, psb = self.pget()
                    self.mm(ps[:, :], [(BLK, sqb)], [self.b_cst, b_sqb], psb)
                    S.op("act", lambda e, ps=ps: e.activation(out=t1, in_=ps[:, :], func=AF.Ln, bias=self.ppc("eps")), reads=[psb, self.b_pp], writes=[b_t1])
                    S.op("act", lambda e: e.activation(out=t1, in_=t1, func=AF.Exp, scale=-0.5), reads=[b_t1], writes=[b_t1])
                    S.op("dve", lambda e: e.tensor_tensor(out=kkn, in0=t0, in1=t1, op=ALU.mult), reads=[b_t0, b_t1], writes=[b_kkn])
                    S.op("dve", lambda e: e.tensor_scalar(out=t0, in0=ai, scalar1=-1.0, scalar2=self.ppc("k_a", l * 4 + hp), op0=ALU.add, op1=ALU.mult),
                         reads=[b_ai, self.b_pp], writes=[b_t0])
                    S.op("dve", lambda e: e.scalar_tensor_tensor(out=kmod, in0=t0, scalar=1.0, in1=kS, op0=ALU.add, op1=ALU.mult),
                         reads=[b_t0, b_kS], writes=[b_kmod])
                    S.op("dve", lambda e: e.tensor_tensor(out=kb, in0=kkn, in1=ai, op=ALU.mult), reads=[b_kkn, b_ai], writes=[b_kb])
                    S.op("act", lambda e: e.activation(out=t1, in_=eex, func=AF.Exp, scale=-cdec), reads=[b_eex], writes=[b_t1])
                    S.op("dve", lambda e: e.scalar_tensor_tensor(out=aF, in0=kkn, scalar=-1.0, in1=t1, op0=ALU.mult, op1=ALU.mult),
                         reads=[b_kkn, b_t1], writes=[b_aF])
                    S.op("act", lambda e: e.activation(out=t1, in_=ein, func=AF.Exp, scale=cdec), reads=[b_ein], writes=[b_t1])
                    S.op("dve", lambda e: e.tensor_tensor(out=bF, in0=kb, in1=t1, op=ALU.mult), reads=[b_kb, b_t1], writes=[b_bF])
                    S.op("dve", lambda e: e.tensor_tensor(out=kF, in0=kmod, in1=t1, op=ALU.mult), reads=[b_kmod, b_t1], writes=[b_kF])
                    S.op("act", lambda e: e.activation(out=t1, in_=ein, func=AF.Exp, scale=-cdec), reads=[b_ein], writes=[b_t1])
                    S.op("dve", lambda e: e.tensor_tensor(out=rF, in0=rS, in1=t1, op=ALU.mult), reads=[b_rS, b_t1], writes=[b_rF])
                    S.op("act", lambda e: e.activation(out=t1, in_=rem, func=AF.Exp, scale=-cdec), reads=[b_rem], writes=[b_t1])
                    S.op("dve", lambda e: e.tensor_tensor(out=bgF, in0=kb, in1=t1, op=ALU.mult), reads=[b_kb, b_t1], writes=[b_bgF])
                    S.op("dve", lambda e: e.tensor_tensor(out=kgF, in0=kmod, in1=t1, op=ALU.mult), reads=[b_kmod, b_t1], writes=[b_kgF])
                    S.op("dve", lambda e: e.scalar_tensor_tensor(out=sqb, in0=rS, scalar=self.ppc("r_k", l * 4 + hp), in1=kmod, op0=ALU.mult, op1=ALU.mult),
                         reads=[b_rS, b_kmod, self.b_pp], writes=[b_sqb])
                    ps, psb = self.pget()
                    self.mm(ps[:, :], [(BLK, sqb)], [self.b_cst, b_sqb], psb)
                    S.op("dve", lambda e, ps=ps: e.tensor_tensor(out=kb, in0=ps[:, :], in1=vF, op=ALU.mult), reads=[psb, b_vF, b_bgF], writes=[b_kb])
                    bonus, b_bonus = kb, b_kb
                    for cb in range(4):
                        cs = slice(cb * 128, (cb + 1) * 128)
                        for src, bs, dst, bd in ((vF, b_vF, vT, b_vT), (bgF, b_bgF, bgT, b_bgT), (kgF, b_kgF, kgT, b_kgT)):
                            ps, psb = self.pget()
                            self.mm(ps[:, 0:128], [(src[:, cs], self.ident)], [bs, self.b_cst], psb)
                            S.op("act", lambda e, ps=ps, dst=dst, cb=cb: e.activation(out=dst[:, cb, :], in_=ps[:, 0:128], func=AF.Copy), reads=[psb], writes=[bd])
                    for cb_ in range(4):
                        cb = cb_ if d == 0 else 3 - cb_
                        cs = slice(cb * 128, (cb + 1) * 128)
                        for hh in range(2):
                            ph = slice(hh * 64, hh * 64 + 64)
                            hc = slice(hh * 64, hh * 64 + 64)
                            (Nm, b_N), (Am, b_A), (N2, b_N2), (A2, b_A2) = sq

                            def prod(dst, bd, lhs, bl, rhs, br, mask):
                                ps, psb = self.pget()
                                self.mm(ps[:, 0:128], [(lhs[ph, cs], rhs[ph, cs])], [bl, br], psb)
                                S.op("dve", lambda e, ps=ps: e.tensor_tensor(out=dst, in0=ps[:, 0:128], in1=mask, op=ALU.mult), reads=[psb, self.b_cst], writes=[bd])
                            prod(Nm, b_N, bF, b_bF, aF, b_aF, strict_st)
                            prod(Am, b_A, aF, b_aF, bF, b_bF, strict_ts)
                            prod(AkT, b_AkT, kF, b_kF, aF, b_aF, strict_st)
                            prod(MrbT, b_MrbT, bF, b_bF, rF, b_rF, incl_st)
                            prod(MrkT, b_MrkT, kF, b_kF, rF, b_rF, incl_st)
                            ps, psb = self.pget()
                            self.mm(ps[:, 0:64], [(aF[ph, cs], H[ph, :]), (AkT, vT[:, cb, hc])], [b_aF, b_H, b_AkT, b_vT], psb)
                            (X0, b_X0), (X1, b_X1) = X
                            S.op("act", lambda e, ps=ps: e.activation(out=X0, in_=ps[:, 0:64], func=AF.Copy), reads=[psb], writes=[b_X0])
                            cur_ = (Nm, b_N, Am, b_A)
                            oth_ = (N2, b_N2, A2, b_A2)
                            xs = [(X0, b_X0), (X1, b_X1)]
                            for j in range(7):
                                Nj, bNj, Aj, bAj = cur_
                                xin, bxin = xs[j % 2]
                                xout, bxout = xs[(j + 1) % 2]
                                ps, psb = self.pget()
                                self.mm(ps[:, 0:64], [(Nj, xin)], [bNj, bxin], psb)
                                S.op("dve", lambda e, ps=ps, xin=xin, xout=xout: e.tensor_tensor(out=xout, in0=ps[:, 0:64], in1=xin, op=ALU.add),
                                     reads=[psb, bxin], writes=[bxout])
                                if j < 6:
                                    Nn, bNn, An, bAn = oth_
                                    ps, psb = self.pget()
                                    self.mm(ps[:, 0:128], [(Aj, Nj)], [bNj, bAj], psb)
                                    S.op("act", lambda e, ps=ps, Nn=Nn: e.activation(out=Nn, in_=ps[:, 0:128], func=AF.Copy), reads=[psb], writes=[bNn])
                                    ps, psb = self.pget()
                                    self.mm(ps[:, 0:128], [(Nj, Aj)], [bNj, bAj], psb)
                                    S.op("act", lambda e, ps=ps, An=An: e.activation(out=An, in_=ps[:, 0:128], func=AF.Copy), reads=[psb], writes=[bAn])
                                    cur_, oth_ = oth_, cur_
                            U, b_U = xs[7 % 2]
                            ps, psb = self.pget()
                            self.mm(ps[:, 0:64], [(rF[ph, cs], H[ph, :]), (MrbT, U), (MrkT, vT[:, cb, hc])],
                                    [b_rF, b_H, b_MrbT, b_U, b_MrkT, b_vT], psb)
                            S.op("dve", lambda e, ps=ps: e.bn_stats(out=st6, in_=ps[:, 0:64]), reads=[psb], writes=[b_st6])
                            S.op("dve", lambda e: e.bn_aggr(out=mv, in_=st6), reads=[b_st6], writes=[b_mv])
                            S.op("act", lambda e: e.activation(out=mv[:, 1:2], in_=mv[:, 1:2], func=AF.Ln, bias=self.ppc("gneps")), reads=[b_mv, self.b_pp], writes=[b_mv])
                            S.op("act", lambda e: e.activation(out=mv[:, 1:2], in_=mv[:, 1:2], func=AF.Exp, scale=-0.5), reads=[b_mv], writes=[b_mv])
                            S.op("dve", lambda e, ps=ps, hc=hc: e.tensor_scalar(out=ynT[:, hc], in0=ps[:, 0:64], scalar1=mv[:, 0:1], scalar2=mv[:, 1:2],
                                                                              op0=ALU.subtract, op1=ALU.mult), reads=[psb, b_mv], writes=[b_ynT])
                            ps, psb = self.pget()
                            self.mm(ps[0:64, 0:64], [(bgT[:, cb, hc], U), (kgT[:, cb, hc], vT[:, cb, hc])], [b_bgT, b_U, b_kgT, b_vT], psb)
                            S.op("act", lambda e, ps=ps, ph=ph: e.activation(out=tH[ph, :], in_=ps[0:64, 0:64], func=AF.Copy), reads=[psb], writes=[b_tH])
                            S.op("dve", lambda e, ph=ph, cb=cb: e.scalar_tensor_tensor(out=H[ph, :], in0=H[ph, :], scalar=gC[ph, cb:cb + 1], in1=tH[ph, :],
                                                                                    op0=ALU.mult, op1=ALU.add), reads=[b_H, b_gC, b_tH], writes=[b_H])
                        ps, psb = self.pget()
                        self.mm(ps[:, 0:128], [(ynT, self.ident)], [b_ynT, self.b_cst], psb)
                        S.op("act", lambda e, ps=ps, cs=cs: e.activation(out=t0[:, cs], in_=ps[:, 0:128], func=AF.Identity,
                                                                        scale=self.ppc("gn_w", l * 4 + hp), bias=self.ppc("gn_b", l * 4 + hp)),
                             reads=[psb, self.b_pp], writes=[b_t0])
                    if mode == "yn_d0":
                        S.op("dve", lambda e, tok=tok: e.tensor_copy(out=acc[:, tok], in_=t0), reads=[b_t0], writes=[b_acc])
                    elif mode == "bonus_d0":
                        S.op("dve", lambda e, tok=tok: e.tensor_copy(out=acc[:, tok], in_=bonus), reads=[b_bonus], writes=[b_acc])
                    elif d == 0 or mode == "acc_d1":
                        S.op("dve", lambda e, tok=tok: e.tensor_tensor(out=acc[:, tok], in0=t0, in1=bonus, op=ALU.add), reads=[b_t0, b_bonus], writes=[b_acc])
                    else:
                        S.op("dve", lambda e: e.tensor_tensor(out=t0, in0=t0, in1=bonus, op=ALU.add), reads=[b_t0, b_bonus], writes=[b_t0])
                        S.op("pool", lambda e, tok=tok: e.tensor_tensor(out=acc[:, tok], in0=acc[:, tok], in1=t0, op=ALU.add), reads=[b_t0, b_acc], writes=[b_acc])
            for g in range(4):
                tok = slice(g * 512, (g + 1) * 512)
                ps, psb = self.pget()
                self.mm(ps[:, :], [(wg[:, k, :], hT[:, k, tok]) for k in range(8)], [b_h[k][g] for k in range(8)] + [b_wg], psb)
                S.op("act", lambda e, ps=ps: e.activation(out=sqb, in_=ps[:, :], func=AF.Sigmoid), reads=[psb], writes=[b_sqb])
                ps, psb = self.pget()
                self.mm(ps[:, :], [(gupt[:, hp * 128:(hp + 1) * 128], sqb)], [b_gup, b_sqb], psb)
                if mode == "gate":
                    S.op("dve", lambda e, ps=ps, tok=tok: e.tensor_copy(out=oT[:, hp, tok], in_=ps[:, :]), reads=[psb], writes=[b_o[hp][g]])
                elif mode is not None:
                    S.op("dve", lambda e, ps=ps, tok=tok: e.tensor_copy(out=oT[:, hp, tok], in_=acc[:, tok]), reads=[psb, b_acc], writes=[b_o[hp][g]])
                else:
                    S.op("dve", lambda e, ps=ps, tok=tok: e.tensor_tensor(out=oT[:, hp, tok], in0=ps[:, :], in1=acc[:, tok], op=ALU.mult),
                         reads=[psb, b_acc], writes=[b_o[hp][g]])

def _consts():
    cst = np.zeros((128, 1024), np.float32)
    cst[:, 0:128] = np.eye(128, dtype=np.float32)
    cst[:, 128:256] = 1.0
    p = np.arange(128)[:, None]
    f = np.arange(128)[None, :]
    cst[:, 256:384] = (p < f)
    cst[:, 384:512] = (p <= f)
    cst[:, 512:640] = (p > f)
    cst[:, 640:768] = (p >= f)
    cst[:, 768:896] = ((p // 64) == (f // 64))
    return cst


def _consts2():
    c2 = np.ones((128, 512), np.float32)
    c2[:, 0::128] = 0.0
    return c2


def _pp_static(pp, off):
    d = np.arange(128) % 64
    invf = np.where(d < 16, 500000.0 ** (-((d % 8) * 2.0) / 16.0), 0.0)
    pp[:, off["invf"]] = (invf / TWO_PI).astype(np.float32)
    pp[:, off["rsign"]] = np.where(d < 8, -1.0, np.where(d < 16, 1.0, 0.0))
    pp[:, off["eps"]] = RMS_EPS
    pp[:, off["one"]] = 1.0
    pp[:, off["gneps"]] = GN_EPS


def build_w_in_ext(w_in):
    L = w_in.shape[0]
    cols = []
    base = {"aq": 0, "ak": 512, "av": 1024, "nq": 1536, "nk": 2048, "nv": 2560, "pr": 3072, "pk": 3584, "pv": 4096,
            "pwf": 4608, "pwb": 4672, "paf": 4736, "pab": 4800, "pg": 4864, "gz0": 4992, "gz1": 6016, "gz2": 7040}
    perm64 = np.arange(64)
    perm64[0:8] = np.arange(8, 16)
    perm64[8:16] = np.arange(0, 8)
    for hp in range(4):
        idx = np.arange(hp * 128, (hp + 1) * 128)
        pidx = np.concatenate([hp * 128 + perm64, hp * 128 + 64 + perm64])
        cols += [base["aq"] + idx, base["aq"] + pidx, base["ak"] + idx, base["ak"] + pidx]
    cols.append(base["av"] + np.arange(512))
    for hp in range(4):
        idx = np.arange(hp * 128, (hp + 1) * 128)
        cols += [base["nq"] + idx, base["nk"] + idx]
    cols.append(base["nv"] + np.arange(512))
    cols.append(base["pr"] + np.arange(512))
    cols.append(base["pk"] + np.arange(512))
    cols.append(base["pv"] + np.arange(512))
    cols.append(base["pwf"] + np.arange(128))
    cols.append(base["paf"] + np.arange(128))
    cols.append(base["pg"] + np.arange(128))
    cols.append(base["gz0"] + np.arange(3072))
    cols = np.concatenate(cols)
    assert cols.shape[0] == NCOL
    return np.ascontiguousarray(w_in[:, :, cols])


def chunk_cols(v):
    sh = v.shape
    n = sh[-1] // 128
    return np.moveaxis(v.reshape(sh[:-1] + (n, 128)), -1, 0)


def prep_shared(inp, L):
    off, NPP = pp_layout(L)
    pp = np.zeros((128, NPP), np.float32)
    _pp_static(pp, off)

    def put(name, arr):
        arr = np.asarray(arr, np.float32).reshape(128, -1)
        pp[:, off[name]:off[name] + arr.shape[1]] = arr
    put("ng", chunk_cols(inp["norm_gains"][:L]))
    put("fn", chunk_cols(inp["final_norm"]))
    put("adab", chunk_cols(inp["ada_b"][:L]))
    put("mu_rkv", chunk_cols(inp["mu_rkv"][:L]))
    put("w0", chunk_cols(inp["w0"][:L]))
    put("a0", chunk_cols(inp["a0"][:L]))
    for nm in ("k_k", "k_a", "gn_w", "gn_b"):
        put(nm, chunk_cols(inp[nm][:L]))
    put("r_k", chunk_cols(inp["r_k"][:L].reshape(L, 512)))
    put("mu_w", np.moveaxis(inp["mu_w"][:L].reshape(L, 128), -1, 0))
    put("mu_a", np.moveaxis(inp["mu_a"][:L].reshape(L, 128), -1, 0))
    sh = {"pp": pp, "ada_w": np.ascontiguousarray(inp["ada_w"][:L]),
          "ffn_wi": np.ascontiguousarray(inp["ffn_wi"][:L]), "ffn_wo": np.ascontiguousarray(inp["ffn_wo"][:L]),
          "w_in": build_w_in_ext(inp["w_in"][:L]), "w_branch": np.ascontiguousarray(inp["w_branch"][:L]),
          "w_out": np.ascontiguousarray(inp["w_out"][:L]),
          "w_up": np.ascontiguousarray(inp["w_up"][:L]), "a_up": np.ascontiguousarray(inp["a_up"][:L]),
          "g_up": np.ascontiguousarray(inp["g_up"][:L]), "cst": _consts(), "cst2": _consts2(),
          "mu_rkv": np.ascontiguousarray(inp["mu_rkv"][:L]), "mu_w": np.ascontiguousarray(inp["mu_w"][:L]),
          "mu_a": np.ascontiguousarray(inp["mu_a"][:L])}
    sh.update(attn_tables(inp["rpb"][:L]))
    return sh


def attn_tables(rpb):
    L = rpb.shape[0]
    dd = np.arange(128)[:, None] - (np.arange(3968)[None, :] - 1920)
    ad = np.abs(dd)
    ta = ((ad <= 64).astype(np.float32) + ((dd % 4 == 0) & (ad <= 256)) + ((dd % 16 == 0) & (ad <= 1024))).astype(np.float32)
    rows = 32
    rs = np.clip(np.arange(rows) - 4, 0, rows - 8)
    valid = (np.arange(rows)[None, :] >= rs[:, None]) & (np.arange(rows)[None, :] < rs[:, None] + 8)
    tokrow = np.arange(S_LEN) // 64
    bu = (tokrow[None, :] == np.arange(32)[:, None]).astype(np.float32)
    bw = np.where(valid[tokrow, :].T, 0.0, -240000.0).astype(np.float32)
    kc = np.arange(64)[:, None, None]
    jj = np.arange(31)[None, :, None]
    qc = np.arange(64)[None, None, :]
    drp = jj - 15
    cs = np.clip(qc - 8, 0, 48)
    colv = (kc >= cs) & (kc < cs + 16)
    ok = colv & (np.abs(drp) <= 7)
    roff = np.clip(-drp + 7, 0, 14)
    coff = np.clip(kc - qc, -15, 15) + 15
    g = rpb[:, :, roff, coff]
    braw = np.where(ok[None, None], g, np.float32(-30000.0)).astype(np.float32)
    return {"ta": ta, "bu": bu, "bw": bw, "braw": np.ascontiguousarray(braw.reshape(L, 8, 64, 31 * 64))}


def prep_core(inp, seqs):
    x = inp["x"][seqs]
    n = len(seqs)
    xT = np.ascontiguousarray(x.reshape(n, S_LEN, NCH, 128).transpose(0, 3, 2, 1))
    cT = np.ascontiguousarray(inp["c"][seqs].reshape(n, NCH, 128).transpose(2, 1, 0))
    pos = np.ascontiguousarray(inp["positions"][seqs]).astype(np.int32)
    return {"xT": xT, "cT": cT, "pos": pos}


def run_model(inp, n_cores=8, NSEQ=2, L=4, parts=("ffn", "A", "B", "C"), seq_lists=None, trace=False, debug=False, cmode=None):
    prog = Prog(NSEQ=NSEQ, L=L, parts=parts)
    prog.debug = debug
    prog.cmode = cmode
    nc = prog.build()
    shared = prep_shared(inp, L)
    if seq_lists is None:
        seq_lists = [list(range(i * NSEQ, (i + 1) * NSEQ)) for i in range(n_cores)]
    in_maps = []
    for sl in seq_lists:
        m = dict(shared)
        m.update(prep_core(inp, sl))
        in_maps.append(m)
    res = run_bass_kernel_spmd(nc, in_maps, core_ids=list(range(len(seq_lists))), trace=trace)
    prog.dbg_out = res.results[0].get("dbg") if debug else None
    outs = []
    for r in res.results:
        o = r["outT"]
        outs.append(o.transpose(0, 3, 2, 1).reshape(o.shape[0], S_LEN, D))
    return np.concatenate(outs, axis=0), res, prog


def kernel(**inputs):
    inp = {k: np.asarray(v) for k, v in inputs.items()}
    out, _, _ = run_model(inp)
    return np.ascontiguousarray(out.astype(np.float32))
```

```python
import contextlib
import numpy as np
import concourse.bass as bass
import concourse.mybir as mybir
from concourse.bass_utils import run_bass_kernel_spmd

F32 = mybir.dt.float32
BF16 = mybir.dt.bfloat16
I32 = mybir.dt.int32
ALU = mybir.AluOpType
AF = mybir.ActivationFunctionType


class Buf:
    __slots__ = ("name", "w", "r")

    def __init__(self, name):
        self.name = name
        self.w = []
        self.r = []

    def add_writer(self, tok):
        if tok[0] == "c":
            self.w = [t for t in self.w if not (t[0] == "c" and t[1] == tok[1])]
        self.w.append(tok)
        if len(self.w) > 12:
            self.w = self.w[-12:]
        self.r = []


class Sched:
    ENG = ("pe", "act", "dve", "pool", "sp")
    NDS = 8

    def __init__(self, nc, es):
        self.nc = nc
        self.eng = {"pe": nc.tensor, "act": nc.scalar, "dve": nc.vector, "pool": nc.gpsimd, "sp": nc.sync}
        self.sem = {e: es.enter_context(nc.semaphore("s_" + e)) for e in self.ENG}
        self.cnt = {e: 0 for e in self.ENG}
        self.dsem = {q: [es.enter_context(nc.semaphore("d_%s%d" % (q, i))) for i in range(self.NDS)]
                     for q in ("sp", "pool", "act")}
        self.dcnt = {q: 0 for q in self.dsem}
        self.waited = {e: {} for e in self.ENG}
        self.ninst = 0

    def _wait(self, e, tok):
        if tok is None:
            return
        kind = tok[0]
        if kind == "c":
            _, e2, i2 = tok
            if e2 == e and e == "pe":
                return
            key = ("c", e2)
            if self.waited[e].get(key, 0) >= i2:
                return
            self.waited[e][key] = i2
            self.eng[e].wait_ge(self.sem[e2], i2)
        else:
            _, q, j = tok
            k = j % self.NDS
            val = 16 * (j // self.NDS + 1)
            key = ("d", q, k)
            if self.waited[e].get(key, 0) >= val:
                return
            self.waited[e][key] = val
            self.eng[e].wait_ge(self.dsem[q][k], val)

    def _deps(self, e, reads, writes, same_war=False):
        for b in reads:
            for t in b.w:
                self._wait(e, t)
        for b in writes:
            for t in b.w:
                self._wait(e, t)
            for t in b.r:
                if t[0] == "c" and t[1] == e:
                    continue
                self._wait(e, t)

    def op(self, e, fn, reads=(), writes=()):
        self._deps(e, reads, writes)
        ins = fn(self.eng[e])
        self.cnt[e] += 1
        ins.then_inc(self.sem[e], 1)
        tok = ("c", e, self.cnt[e])
        for b in reads:
            b.r.append(tok)
        for b in writes:
            b.add_writer(tok)
        self.ninst += 1
        return tok

    def dma(self, q, out, in_, reads=(), writes=()):
        e = q
        j = self.dcnt[q]
        if j >= self.NDS:
            self._wait(e, ("d", q, j - self.NDS))
        self._deps(e, reads, writes)
        k = j % self.NDS
        self.eng[e].dma_start(out=out, in_=in_).then_inc(self.dsem[q][k], 16)
        self.dcnt[q] += 1
        tok = ("d", q, j)
        for b in reads:
            b.r.append(tok)
        for b in writes:
            b.add_writer(tok)
        self.ninst += 1
        return tok

    def barrier(self):
        toks = [("c", e, self.cnt[e]) for e in self.ENG if self.cnt[e] > 0]
        for q in self.dsem:
            for j in range(max(0, self.dcnt[q] - self.NDS), self.dcnt[q]):
                toks.append(("d", q, j))
        for e in self.ENG:
            for t in toks:
                if t[0] == "c" and t[1] == e:
                    continue
                self._wait(e, t)

    def finish(self, e="sp"):
        self.barrier()


D = 1024
S_LEN = 2048
NCH = 8
DFF = 2816
NJ = DFF // 128
L_ALL = 4
WM = 512
RMS_EPS = 1e-6
GN_EPS = 64e-5
DECAY_SCALE = 0.6065306597126334
TWO_PI = 6.283185307179586

OFF_A_QK = 0
OFF_A_V = 2048
OFF_B_QK = 2560
OFF_B_V = 3584
OFF_C_R = 4096
OFF_C_K = 4608
OFF_C_V = 5120
OFF_C_LW = 5632
OFF_C_LA = 5760
OFF_C_G = 5888
OFF_GZ = 6016
NCOL = OFF_GZ + 3 * 1024


def pp_layout(L):
    off = {}
    n = 0
    for name, size in (("ng", L * 24), ("fn", 8), ("adab", L * 72), ("mu_rkv", L * 24), ("w0", L * 8),
                       ("a0", L * 8), ("k_k", L * 4), ("k_a", L * 4), ("gn_w", L * 4), ("gn_b", L * 4),
                       ("r_k", L * 4), ("mu_w", L), ("mu_a", L), ("invf", 1), ("rsign", 1), ("eps", 1),
                       ("one", 1), ("gneps", 1)):
        off[name] = n
        n += size
    return off, n


class Prog:
    def __init__(self, NSEQ=2, L=4, parts=("ffn", "A", "B", "C")):
        self.NSEQ, self.L, self.parts = NSEQ, L, parts
        self.ppo, self.NPP = pp_layout(L)
        self.nc = bass.Bass("TRN2", target_bir_lowering=False)
        self.es = contextlib.ExitStack()

    def dram(self, name, shape, dtype=F32, kind="ExternalInput"):
        return self.nc.dram_tensor(name, list(shape), dtype, kind=kind).ap()

    def sb(self, name, shape, dtype, es=None):
        self._uid = getattr(self, "_uid", 0) + 1
        return (es or self.es).enter_context(self.nc.sbuf_tensor("%s_%d" % (name, self._uid), list(shape), dtype))

    def ppc(self, name, idx=0, n=1):
        o = self.ppo[name] + idx
        return self.ppt[:, o:o + n]

    def pget(self, pool="g"):
        lst = self.ps_g if pool == "g" else self.ps_a
        k = self.ps_rr[pool]
        self.ps_rr[pool] = (k + 1) % len(lst)
        return lst[k]

    def wslot(self):
        k = self.ws_rr
        self.ws_rr = (k + 1) % len(self.wsl)
        return self.wsl[k]

    def mm(self, ps, pairs, reads, psb, extra_reads=()):
        S = self.S
        n = len(pairs)
        for i, (a, b) in enumerate(pairs):
            S.op("pe", lambda e, a=a, b=b, i=i: e.matmul(ps, a, b, start=(i == 0), stop=(i == n - 1)),
                 reads=reads, writes=[psb])

    def build(self):
        nc, es, NSEQ, L = self.nc, self.es, self.NSEQ, self.L
        with es:
            self._declare()
            self.S = Sched(nc, es)
            self._alloc_global()
            self._prologue()
            for s in range(NSEQ):
                self._seq(s)
            self.S.finish()
        return nc

    def _declare(self):
        NSEQ, L = self.NSEQ, self.L
        d = self.dram
        self.xT_d = d("xT", [NSEQ, 128, NCH, S_LEN])
        self.cT_d = d("cT", [128, NCH, NSEQ])
        self.pos_d = d("pos", [NSEQ, S_LEN], I32)
        self.pp_d = d("pp", [128, self.NPP])
        self.ada_w_d = d("ada_w", [L, D, 9 * D])
        self.wi_d = d("ffn_wi", [L, 2, D, 2 * DFF])
        self.wo_d = d("ffn_wo", [L, 2, DFF, D])
        self.win_d = d("w_in", [L, D, NCOL])
        self.wbr_d = d("w_branch", [L, 3, WM, D])
        self.wout_d = d("w_out", [L, D, D])
        self.ta_d = d("ta", [128, 3968])
        self.bu_d = d("bu", [32, S_LEN])
        self.bw_d = d("bw", [32, S_LEN])
        self.braw_d = d("braw", [L, 8, 64, 31 * 64])
        self.wup_d = d("w_up", [L, 2, 64, WM])
        self.aup_d = d("a_up", [L, 2, 64, WM])
        self.gup_d = d("g_up", [L, 128, WM])
        self.cst_d = d("cst", [128, 1024])
        self.cst2_d = d("cst2", [128, 512])
        self.murkv_d = d("mu_rkv", [L, 2, 3, WM])
        self.muw_d = d("mu_w", [L, 2, 64])
        self.mua_d = d("mu_a", [L, 2, 64])
        self.out_d = d("outT", [NSEQ, 128, NCH, S_LEN], kind="ExternalOutput")
        self.xs_d = d("xspill", [128, NCH, S_LEN], kind="Internal")
        self.b_xs = Buf("xs")
        self.dbg_d = d("dbg", [128, 4, S_LEN], kind="ExternalOutput") if getattr(self, "debug", False) else None

    def _alloc_global(self):
        nc, es = self.nc, self.es
        sb = self.sb
        self.xT = sb("xT_s", [128, NCH, S_LEN], F32)
        self.b_x = [[Buf("x%d_%d" % (c, g)) for g in range(4)] for c in range(NCH)]
        self.wsl = [(sb("wsl%d" % i, [128, 4096], BF16), Buf("wsl%d" % i)) for i in range(4)]
        self.ws_rr = 0
        self.ppt = sb("ppt", [128, self.NPP], F32)
        self.b_pp = Buf("pp")
        self.modT = sb("modT", [128, self.L, 72, self.NSEQ], F32)
        self.b_mod = Buf("mod")
        self.cst = sb("cstt", [128, 1024], BF16)
        self.cstf = sb("cstf", [128, 1024], F32)
        self.b_cst = Buf("cst")
        self.rotC = sb("rotC", [128, S_LEN], BF16)
        self.rotS = sb("rotS", [128, S_LEN], BF16)
        self.b_rot = Buf("rot")
        self.der = sb("der", [128, 64], F32)
        self.b_der = Buf("der")
        pst = [es.enter_context(nc.psum_tensor("ps%d" % i, [128, 512], F32)) for i in range(8)]
        self.ps_all = [(pst[i], Buf("ps%d" % i)) for i in range(8)]
        self.ps_g = self.ps_all[0:4]
        self.ps_a = self.ps_all[4:8]
        self.ps_rr = {"g": 0, "a": 0}
        self.ident = self.cst[:, 0:128]
        self.ones = self.cst[:, 128:256]

    def _prologue(self):
        S, nc = self.S, self.nc
        L, NSEQ = self.L, self.NSEQ
        S.dma("sp", self.ppt[:, :], self.pp_d[:, :], writes=[self.b_pp])
        S.dma("pool", self.cst[:, :], self.cst_d[:, :], writes=[self.b_cst])
        S.dma("sp", self.cstf[:, :], self.cst_d[:, :], writes=[self.b_cst])
        with contextlib.ExitStack() as pes:
            ct = self.sb("ct", [128, NCH, NSEQ], F32, pes)
            cb = self.sb("cb", [128, NCH, NSEQ], BF16, pes)
            b_ct, b_cb = Buf("ct"), Buf("cb")
            S.dma("sp", ct[:], self.cT_d[:, :, :], writes=[b_ct])
            S.op("act", lambda e: e.activation(out=cb[:], in_=ct[:], func=AF.Silu), reads=[b_ct], writes=[b_cb])
            for l in range(L):
                ps, psb = self.pget()
                pv = ps[:, 0:72 * NSEQ].rearrange("p (c s) -> p c s", s=NSEQ)
                for g in range(18):
                    wt, wb = self.wslot()
                    wv = wt[:, :].rearrange("p (k n) -> p k n", k=8)
                    S.dma("pool", wv, self.ada_w_d[l].rearrange("(k p) n -> p k n", p=128)[:, :, g * 512:(g + 1) * 512],
                          writes=[wb])
                    for jj in range(4):
                        ch = g * 4 + jj
                        self.mm(pv[:, ch, :], [(wv[:, k, jj * 128:(jj + 1) * 128], cb[:, k, :]) for k in range(8)],
                                [wb, b_cb], psb)
                ab = self.ppc("adab", l * 72, 72)
                S.op("dve", lambda e, l=l, pv=pv, ab=ab: e.tensor_tensor(
                    out=self.modT[:, l, :, :], in0=pv, in1=ab.unsqueeze(2).to_broadcast([128, 72, NSEQ]), op=ALU.add),
                    reads=[psb, self.b_pp], writes=[self.b_mod])
            S.barrier()

    def _seq(self, s):
        S = self.S
        for c in range(NCH):
            S.dma("sp", self.xT[:, c, :], self.xT_d[s, :, c, :], writes=self.b_x[c])
        if "A" in self.parts:
            self._rotary(s)
        for l in range(self.L):
            if "ffn" in self.parts:
                self._ffn(l, 0, s)
            if any(p in self.parts for p in "ABC"):
                self._mixer(l, s)
            if "ffn" in self.parts:
                self._ffn(l, 1, s)
        self._final(s)
        S.barrier()

    def _derive(self, l, sub, s, gate_scale):
        S = self.S
        m = self.modT
        sh = m[:, l, (3 * sub) * 8:(3 * sub) * 8 + 8, s]
        sc = m[:, l, (3 * sub + 1) * 8:(3 * sub + 1) * 8 + 8, s]
        gt = m[:, l, (3 * sub + 2) * 8:(3 * sub + 2) * 8 + 8, s]
        ng = self.ppc("ng", (l * 3 + sub) * 8, 8)
        der = self.der
        rd = [self.b_mod, self.b_pp]
        S.op("dve", lambda e: e.scalar_tensor_tensor(out=der[:, 0:8], in0=sc, scalar=1.0, in1=ng, op0=ALU.add, op1=ALU.mult),
             reads=rd, writes=[self.b_der])
        S.op("dve", lambda e: e.tensor_copy(out=der[:, 8:16], in_=sh), reads=rd, writes=[self.b_der])
        S.op("dve", lambda e: e.tensor_scalar(out=der[:, 16:24], in0=gt, scalar1=float(gate_scale), scalar2=None, op0=ALU.mult),
             reads=rd, writes=[self.b_der])

    def _norm_mod(self, g512, gs_ap, sh_ap, dst_fn, dst_bufs, tmp, extra_reads=()):
        S = self.S
        sq, b_sq, lnt, b_ln, rstd, b_rstd, tm, b_tm = tmp
        tok = slice(g512 * 512, (g512 + 1) * 512)
        ps, psb = self.pget()
        for c in range(NCH):
            k = c % 2
            S.op("act", lambda e, c=c, k=k: e.activation(out=sq[k][:, :], in_=self.xT[:, c, tok], func=AF.Square),
                 reads=[self.b_x[c][g512]], writes=[b_sq[k]])
            S.op("pe", lambda e, c=c, k=k: e.matmul(ps[:, :], self.ones, sq[k][:, :], start=(c == 0), stop=(c == NCH - 1)),
                 reads=[b_sq[k], self.b_cst], writes=[psb])
        S.op("act", lambda e: e.activation(out=lnt[:, :], in_=ps[:, :], func=AF.Ln, scale=1.0 / D, bias=self.ppc("eps")),
             reads=[psb, self.b_pp], writes=[b_ln])
        S.op("act", lambda e: e.activation(out=rstd[:, :], in_=lnt[:, :], func=AF.Exp, scale=-0.5),
             reads=[b_ln], writes=[b_rstd])
        for c in range(NCH):
            k = c % 2
            S.op("dve", lambda e, c=c, k=k: e.tensor_tensor(out=tm[k][:, :], in0=self.xT[:, c, tok], in1=rstd[:, :], op=ALU.mult),
                 reads=[self.b_x[c][g512], b_rstd], writes=[b_tm[k]])
            if sh_ap is not None:
                S.op("act", lambda e, c=c, k=k: e.activation(out=dst_fn(c), in_=tm[k][:, :], func=AF.Identity,
                                                             scale=gs_ap[:, c:c + 1], bias=sh_ap[:, c:c + 1]),
                     reads=[b_tm[k], self.b_der, self.b_pp] + list(extra_reads), writes=[dst_bufs[c]])
            else:
                S.op("act", lambda e, c=c, k=k: e.activation(out=dst_fn(c), in_=tm[k][:, :], func=AF.Copy,
                                                             scale=gs_ap[:, c:c + 1]),
                     reads=[b_tm[k], self.b_der, self.b_pp] + list(extra_reads), writes=[dst_bufs[c]])

    def _norm_tmp(self, pes):
        sq = [self.sb("sq%d" % k, [128, 512], BF16, pes) for k in range(2)]
        tm = [self.sb("tm%d" % k, [128, 512], F32, pes) for k in range(2)]
        lnt = self.sb("lnt", [128, 512], F32, pes)
        rstd = self.sb("rstd", [128, 512], F32, pes)
        return (sq, [Buf("sq0"), Buf("sq1")], lnt, Buf("ln"), rstd, Buf("rstd"), tm, [Buf("tm0"), Buf("tm1")])

    def _ffn(self, l, i, s):
        S = self.S
        sub = 0 if i == 0 else 2
        self._derive(l, sub, s, 0.5)
        der = self.der
        with contextlib.ExitStack() as pes:
            hg = self.sb("hg", [128, NCH, 1024], BF16, pes)
            b_hg = [[Buf("hg%d_%d" % (c, t)) for t in range(2)] for c in range(NCH)]
            u = self.sb("u", [128, NJ, 1024], BF16, pes)
            b_u = [[Buf("u%d_%d" % (j, t)) for t in range(2)] for j in range(NJ)]
            sg = [self.sb("sg%d" % k, [128, 512], F32, pes) for k in range(2)]
            b_sg = [Buf("sg0"), Buf("sg1")]
            tmp = self._norm_tmp(pes)
            wi_v = self.wi_d[l, i].rearrange("(k p) n -> p k n", p=128)
            wo_v = self.wo_d[l, i].rearrange("(j p) n -> p j n", p=128)
            kk = 0
            for hf in range(2):
                for t2 in range(2):
                    g512 = hf * 2 + t2
                    self._norm_mod(g512, der[:, 0:8], der[:, 8:16],
                                   lambda c, t2=t2: hg[:, c, t2 * 512:(t2 + 1) * 512], [b_hg[c][t2] for c in range(NCH)], tmp)
                for jg in range(6):
                    ncol = 512 if jg < 5 else 256
                    wg, wgb = self.wslot()
                    wu, wub = self.wslot()
                    wgv = wg[:, 0:8 * ncol].rearrange("p (k n) -> p k n", k=8)
                    wuv = wu[:, 0:8 * ncol].rearrange("p (k n) -> p k n", k=8)
                    S.dma("pool", wgv, wi_v[:, :, jg * 512:jg * 512 + ncol], writes=[wgb])
                    S.dma("pool", wuv, wi_v[:, :, DFF + jg * 512:DFF + jg * 512 + ncol], writes=[wub])
                    for jj in range(ncol // 128):
                        j = jg * 4 + jj
                        for t2 in range(2):
                            tk = slice(t2 * 512, (t2 + 1) * 512)
                            pg, pgb = self.pget()
                            pu, pub = self.pget()
                            rd = [b_hg[c][t2] for c in range(NCH)]
                            self.mm(pg[:, :], [(wgv[:, k, jj * 128:(jj + 1) * 128], hg[:, k, tk]) for k in range(8)], rd + [wgb], pgb)
                            self.mm(pu[:, :], [(wuv[:, k, jj * 128:(jj + 1) * 128], hg[:, k, tk]) for k in range(8)], rd + [wub], pub)
                            q = kk % 2
                            kk += 1
                            S.op("act", lambda e, q=q, pg=pg: e.activation(out=sg[q][:, :], in_=pg[:, :], func=AF.Silu),
                                 reads=[pgb], writes=[b_sg[q]])
                            S.op("dve", lambda e, q=q, pu=pu, j=j, tk=tk: e.tensor_tensor(out=u[:, j, tk], in0=sg[q][:, :], in1=pu[:, :], op=ALU.mult),
                                 reads=[b_sg[q], pub], writes=[b_u[j][t2]])
                for chh in range(2):
                    tiles = []
                    for (j0, nj) in ((0, 8), (8, 8), (16, 6)):
                        wt, wb = self.wslot()
                        wv = wt[:, 0:nj * 512].rearrange("p (j n) -> p j n", j=nj)
                        S.dma("pool", wv, wo_v[:, j0:j0 + nj, chh * 512:(chh + 1) * 512], writes=[wb])
                        tiles.append((wv, wb, j0, nj))
                    for t2 in range(2):
                        g512 = hf * 2 + t2
                        tk = slice(t2 * 512, (t2 + 1) * 512)
                        tok = slice(g512 * 512, (g512 + 1) * 512)
                        for cc in range(4):
                            c = chh * 4 + cc
                            pz, pzb = self.pget()
                            pairs = []
                            for (wv, wb, j0, nj) in tiles:
                                for jj in range(nj):
                                    pairs.append((wv[:, jj, cc * 128:(cc + 1) * 128], u[:, j0 + jj, tk]))
                            self.mm(pz[:, :], pairs, [b_u[j][t2] for j in range(NJ)] + [t[1] for t in tiles], pzb)
                            S.op("dve", lambda e, pz=pz, c=c, tok=tok: e.scalar_tensor_tensor(
                                out=self.xT[:, c, tok], in0=pz[:, :], scalar=der[:, 16 + c:17 + c], in1=self.xT[:, c, tok],
                                op0=ALU.mult, op1=ALU.add), reads=[pzb, self.b_der, self.b_x[c][g512]], writes=[self.b_x[c][g512]])
            S.barrier()

    def _final(self, s):
        S = self.S
        with contextlib.ExitStack() as pes:
            tmp = self._norm_tmp(pes)
            ob = [self.sb("ob%d" % k, [128, 512], F32, pes) for k in range(2)]
            b_ob = [Buf("ob0"), Buf("ob1")]
            fn = self.ppc("fn", 0, 8)
            for g in range(4):
                tok = slice(g * 512, (g + 1) * 512)
                cnt = [0]

                def dst(c):
                    return ob[c % 2][:, :]
                sq, b_sq, lnt, b_ln, rstd, b_rstd, tm, b_tm = tmp
                ps, psb = self.pget()
                for c in range(NCH):
                    k = c % 2
                    S.op("act", lambda e, c=c, k=k: e.activation(out=sq[k][:, :], in_=self.xT[:, c, tok], func=AF.Square),
                         reads=[self.b_x[c][g]], writes=[b_sq[k]])
                    S.op("pe", lambda e, c=c, k=k: e.matmul(ps[:, :], self.ones, sq[k][:, :], start=(c == 0), stop=(c == NCH - 1)),
                         reads=[b_sq[k], self.b_cst], writes=[psb])
                S.op("act", lambda e: e.activation(out=lnt[:, :], in_=ps[:, :], func=AF.Ln, scale=1.0 / D, bias=self.ppc("eps")),
                     reads=[psb, self.b_pp], writes=[b_ln])
                S.op("act", lambda e: e.activation(out=rstd[:, :], in_=lnt[:, :], func=AF.Exp, scale=-0.5),
                     reads=[b_ln], writes=[b_rstd])
                for c in range(NCH):
                    k = c % 2
                    S.op("dve", lambda e, c=c, k=k: e.scalar_tensor_tensor(
                        out=ob[k][:, :], in0=self.xT[:, c, tok], scalar=fn[:, c:c + 1], in1=rstd[:, :], op0=ALU.mult, op1=ALU.mult),
                        reads=[self.b_x[c][g], b_rstd, self.b_pp], writes=[b_ob[k]])
                    S.dma("sp", self.out_d[s, :, c, tok], ob[k][:, :], reads=[b_ob[k]])
            S.barrier()

    def _rotary(self, s):
        S = self.S
        with contextlib.ExitStack() as pes:
            posi = self.sb("posi", [128, S_LEN], I32, pes)
            posf = self.sb("posf", [128, S_LEN], F32, pes)
            u = self.sb("ru", [128, S_LEN], F32, pes)
            ui = self.sb("rui", [128, S_LEN], I32, pes)
            uf = self.sb("ruf", [128, S_LEN], F32, pes)
            b = [Buf("r%d" % i) for i in range(5)]
            S.dma("sp", posi[:, :], self.pos_d[s:s + 1, :].to_broadcast([128, S_LEN]), writes=[b[0]])
            S.op("dve", lambda e: e.tensor_copy(out=posf[:, :], in_=posi[:, :]), reads=[b[0]], writes=[b[1]])
            for dst, add, sgn in ((self.rotS, 0.0, True), (self.rotC, 0.25, False)):
                S.op("dve", lambda e, add=add: e.tensor_scalar(out=u[:, :], in0=posf[:, :], scalar1=self.ppc("invf"), scalar2=float(add),
                                                               op0=ALU.mult, op1=ALU.add), reads=[b[1], self.b_pp], writes=[b[2]])
                S.op("dve", lambda e: e.tensor_copy(out=ui[:, :], in_=u[:, :]), reads=[b[2]], writes=[b[3]])
                S.op("dve", lambda e: e.tensor_copy(out=uf[:, :], in_=ui[:, :]), reads=[b[3]], writes=[b[4]])
                S.op("dve", lambda e: e.tensor_tensor(out=u[:, :], in0=u[:, :], in1=uf[:, :], op=ALU.subtract), reads=[b[2], b[4]], writes=[b[2]])
                S.op("act", lambda e: e.activation(out=uf[:, :], in_=u[:, :], func=AF.Sin, scale=TWO_PI), reads=[b[2]], writes=[b[4]])
                if sgn:
                    S.op("dve", lambda e, dst=dst: e.tensor_scalar(out=dst[:, :], in0=uf[:, :], scalar1=self.ppc("rsign"), scalar2=None, op0=ALU.mult),
                         reads=[b[4], self.b_pp], writes=[self.b_rot])
                else:
                    S.op("dve", lambda e, dst=dst: e.tensor_copy(out=dst[:, :], in_=uf[:, :]), reads=[b[4]], writes=[self.b_rot])
            S.barrier()

    def _mixer(self, l, s):
        S = self.S
        self._derive(l, 1, s, 1.0)
        der = self.der
        win_v = self.win_d[l].rearrange("(k p) n -> p k n", p=128)
        self.win_v = win_v
        with contextlib.ExitStack() as pes:
            hT = self.sb("hT", [128, NCH, S_LEN], BF16, pes)
            mT = self.sb("mT", [128, NCH, S_LEN], BF16, pes)
            b_h = [[Buf("h%d_%d" % (c, g)) for g in range(4)] for c in range(NCH)]
            b_m = [[Buf("m%d_%d" % (c, g)) for g in range(4)] for c in range(NCH)]
            self.hT, self.b_h = hT, b_h
            with contextlib.ExitStack() as p2:
                tmp = self._norm_tmp(p2)
                for g in range(4):
                    self._norm_mod(g, der[:, 0:8], der[:, 8:16], lambda c, g=g: hT[:, c, g * 512:(g + 1) * 512],
                                   [b_h[c][g] for c in range(NCH)], tmp)
                for c in range(NCH):
                    S.dma("sp", self.xs_d[:, c, :], self.xT[:, c, :], reads=self.b_x[c], writes=[self.b_xs])
                S.barrier()
            xf = self.xT[:, :, :].rearrange("p c t -> p (c t)")
            xb = xf.bitcast(BF16)
            self.al_f, self.al_b = xf, xb
            oT = xb[:, 0:8192].rearrange("p (h t) -> p h t", h=4)
            b_o = [[Buf("o%d_%d" % (hp, g)) for g in range(4)] for hp in range(4)]
            first = True
            for bi, name in enumerate("ABC"):
                if name not in self.parts:
                    continue
                with contextlib.ExitStack() as p3:
                    getattr(self, "_mix_" + name)(l, s, oT, b_o, p3)
                    if self.dbg_d is not None and name == "C" and l == 0 and s == 0:
                        S.dma("pool", self.dbg_d[:, :, :], oT, reads=[b for bb in b_o for b in bb])
                    self._branch(l, bi, oT, b_o, mT, b_m, first, p3)
                    S.barrier()
                first = False
            for c in range(NCH):
                S.dma("sp", self.xT[:, c, :], self.xs_d[:, c, :], reads=[self.b_xs], writes=self.b_x[c])
            tiles = []
            wo_v = self.wout_d[l].rearrange("(k p) n -> p k n", p=128)
            for chh in range(2):
                wt, wb = self.wslot()
                wv = wt[:, :].rearrange("p (k n) -> p k n", k=8)
                S.dma("pool", wv, wo_v[:, :, chh * 512:(chh + 1) * 512], writes=[wb])
                tiles.append((wv, wb))
            for g in range(4):
                tok = slice(g * 512, (g + 1) * 512)
                for c in range(NCH):
                    wv, wb = tiles[c // 4]
                    cc = c % 4
                    pz, pzb = self.pget()
                    self.mm(pz[:, :], [(wv[:, k, cc * 128:(cc + 1) * 128], mT[:, k, tok]) for k in range(8)],
                            [b_m[k][g] for k in range(8)] + [wb], pzb)
                    S.op("dve", lambda e, pz=pz, c=c, tok=tok: e.scalar_tensor_tensor(
                        out=self.xT[:, c, tok], in0=pz[:, :], scalar=der[:, 16 + c:17 + c], in1=self.xT[:, c, tok],
                        op0=ALU.mult, op1=ALU.add), reads=[pzb, self.b_der, self.b_x[c][g]], writes=[self.b_x[c][g]])
            S.barrier()

    def _branch(self, l, bi, oT, b_o, mT, b_m, first, pes):
        S = self.S
        hT, b_h = self.hT, self.b_h
        sgt = [self.sb("bsg%d" % k, [128, 512], F32, pes) for k in range(2)]
        b_sgt = [Buf("bsg0"), Buf("bsg1")]
        tmm = [self.sb("btm%d" % k, [128, 512], F32, pes) for k in range(2)]
        b_tmm = [Buf("btm0"), Buf("btm1")]
        wbt, wbb = self.wslot()
        wbv = wbt[:, :].rearrange("p (h n) -> p h n", h=4)
        S.dma("pool", wbv, self.wbr_d[l, bi].rearrange("(h p) n -> p h n", p=128), writes=[wbb])
        kk = 0
        for chh in range(2):
            wt, wb = self.wslot()
            wv = wt[:, :].rearrange("p (k n) -> p k n", k=8)
            c0 = OFF_GZ + bi * 1024 + chh * 512
            S.dma("pool", wv, self.win_v[:, :, c0:c0 + 512], writes=[wb])
            for g in range(4):
                tok = slice(g * 512, (g + 1) * 512)
                for cc in range(4):
                    c = chh * 4 + cc
                    py, pyb = self.pget()
                    pg, pgb = self.pget()
                    self.mm(py[:, :], [(wbv[:, hp, c * 128:(c + 1) * 128], oT[:, hp, tok]) for hp in range(4)],
                            [b_o[hp][g] for hp in range(4)] + [wbb], pyb)
                    self.mm(pg[:, :], [(wv[:, k, cc * 128:(cc + 1) * 128], hT[:, k, tok]) for k in range(8)],
                            [b_h[k][g] for k in range(8)] + [wb], pgb)
                    q = kk % 2
                    kk += 1
                    S.op("act", lambda e, q=q, pg=pg: e.activation(out=sgt[q][:, :], in_=pg[:, :], func=AF.Sigmoid),
                         reads=[pgb], writes=[b_sgt[q]])
                    if first:
                        S.op("dve", lambda e, q=q, py=py, c=c, tok=tok: e.tensor_tensor(out=mT[:, c, tok], in0=py[:, :], in1=sgt[q][:, :], op=ALU.mult),
                             reads=[pyb, b_sgt[q]], writes=[b_m[c][g]])
                    else:
                        S.op("dve", lambda e, q=q, py=py: e.tensor_tensor(out=tmm[q][:, :], in0=py[:, :], in1=sgt[q][:, :], op=ALU.mult),
                             reads=[pyb, b_sgt[q]], writes=[b_tmm[q]])
                        S.op("pool", lambda e, q=q, c=c, tok=tok: e.tensor_tensor(out=mT[:, c, tok], in0=mT[:, c, tok], in1=tmm[q][:, :], op=ALU.add),
                             reads=[b_tmm[q], b_m[c][g]], writes=[b_m[c][g]])

    def _vtm(self, col0, vtm, b_v):
        S = self.S
        hT, b_h = self.hT, self.b_h
        wt, wb = self.wslot()
        wv = wt[:, :].rearrange("p (k n) -> p k n", k=8)
        S.dma("pool", wv, self.win_v[:, :, col0:col0 + 512], writes=[wb])
        for tt in range(16):
            pv, pvb = self.pget()
            g = tt // 4
            self.mm(pv[:, :], [(hT[:, k, tt * 128:(tt + 1) * 128], wv[:, k, :]) for k in range(8)],
                    [b_h[k][g] for k in range(8)] + [wb], pvb)
            S.op("act", lambda e, pv=pv, tt=tt: e.activation(out=vtm[:, tt, :], in_=pv[:, :], func=AF.Copy), reads=[pvb], writes=[b_v[tt]])

    def _attn_core(self, hp, qf, kf, b_q, b_k, vtm, b_v, oT, b_o, blocks_fn, mask_fn, mask_reads, extra_mm, pt, b_pt, rdt, b_rd):
        for hh in range(2):
            self._attn_core_one(hp, hh, qf, kf, b_q, b_k, vtm, b_v, oT, b_o, blocks_fn, mask_fn, mask_reads, extra_mm, pt, b_pt, rdt, b_rd)

    def _attn_core_one(self, hp, hh, qf, kf, b_q, b_k, vtm, b_v, oT, b_o, blocks_fn, mask_fn, mask_reads, extra_mm, pt, b_pt, rdt, b_rd):
        S = self.S
        kk = 0
        if True:
            pb = hh * 64
            h = hp * 2 + hh
            for qg in range(4):
                qs = slice(qg * 512, (qg + 1) * 512)
                pn, pnb = self.pget("a")
                pd, pdb = self.pget("a")
                kts = blocks_fn(qg)
                n = len(kts)
                DK = 2
                stage = {}
                for i in range(n + DK):
                    if i < n:
                        kt = kts[i]
                        ps_s, psb = self.pget()
                        ex = extra_mm(kt, qg) if extra_mm else None
                        S.op("pe", lambda e, ps_s=ps_s, kt=kt: e.matmul(ps_s[:, :], kf[pb:pb + 64, kt * 128:(kt + 1) * 128], qf[pb:pb + 64, qs],
                                                                       start=True, stop=(ex is None)),
                             reads=[b_k[kt // 4], b_q[qg]], writes=[psb])
                        if ex is not None:
                            S.op("pe", lambda e, ps_s=ps_s, ex=ex: e.matmul(ps_s[:, :], ex[0], ex[1], start=False, stop=True),
                                 reads=ex[2], writes=[psb])
                        q = kk % 3
                        kk += 1
                        S.op("act", lambda e, q=q, ps_s=ps_s: e.activation(out=pt[q][:, :], in_=ps_s[:, :], func=AF.Exp, scale=0.125),
                             reads=[psb], writes=[b_pt[q]])
                        eng = "dve" if kk % 2 == 0 else "pool"
                        mk = mask_fn(kt, qg)
                        S.op(eng, lambda e, q=q, mk=mk: e.tensor_tensor(out=pt[3 + q][:, :], in0=pt[q][:, :], in1=mk, op=ALU.mult),
                             reads=[b_pt[q]] + list(mask_reads), writes=[b_pt[3 + q]])
                        stage[i] = (q, kt)
                    j = i - DK
                    if j >= 0:
                        q, kt = stage[j]
                        S.op("pe", lambda e, q=q, kt=kt, pn=pn, j=j: e.matmul(pn[0:64, :], vtm[:, kt, h * 64:(h + 1) * 64], pt[3 + q][:, :],
                                                                           start=(j == 0), stop=(j == n - 1)),
                             reads=[b_v[kt], b_pt[3 + q]], writes=[pnb])
                        S.op("pe", lambda e, q=q, pd=pd, j=j: e.matmul(pd[0:64, :], self.ones[:, 0:64], pt[3 + q][:, :],
                                                                    start=(j == 0), stop=(j == n - 1)),
                             reads=[self.b_cst, b_pt[3 + q]], writes=[pdb])
                S.op("dve", lambda e, pd=pd: e.reciprocal(out=rdt[:, :], in_=pd[0:64, :]), reads=[pdb], writes=[b_rd])
                S.op("dve", lambda e, pn=pn, qs=qs: e.tensor_tensor(out=oT[pb:pb + 64, hp, qs], in0=pn[0:64, :], in1=rdt[:, :], op=ALU.mult),
                     reads=[pnb, b_rd], writes=[b_o[hp][qg]])

    def _al(self, off_bytes, shape, dtype):
        n = int(np.prod(shape[1:]))
        if dtype == BF16:
            v = self.al_b[0:shape[0], off_bytes // 2:off_bytes // 2 + n]
        else:
            v = self.al_f[0:shape[0], off_bytes // 4:off_bytes // 4 + n]
        if len(shape) == 3:
            v = v.rearrange("p (a b) -> p a b", a=shape[1])
        return v

    def _mix_A(self, l, s, oT, b_o, pes):
        S = self.S
        hT, b_h = self.hT, self.b_h
        K = 1024
        vtm = self._al(16 * K, [128, 16, 512], BF16)
        b_v = [Buf("v%d" % t) for t in range(16)]
        qf = self._al(32 * K, [128, S_LEN], BF16)
        kf = self._al(36 * K, [128, S_LEN], BF16)
        ta = self._al(40 * K, [128, 3968], BF16)
        b_ta = Buf("ta")
        pt = [self._al(48 * K + i * K, [128, 512], BF16) for i in range(6)]
        b_pt = [Buf("pt%d" % i) for i in range(6)]
        t1 = [self._al(54 * K + i * 2 * K, [128, 512], F32) for i in range(2)]
        b_t1 = [Buf("t1a"), Buf("t1b")]
        rdt = self._al(58 * K, [64, 512], F32)
        b_rd = Buf("rd")
        b_q = [Buf("q%d" % g) for g in range(4)]
        b_k = [Buf("k%d" % g) for g in range(4)]
        S.dma("pool", ta, self.ta_d[:, :], writes=[b_ta])
        self._vtm(OFF_A_V, vtm, b_v)

        def blocks(qg):
            out = []
            for kt in range(16):
                dl = 128 * kt - 512 * qg
                if dl - 511 > 1024 or dl + 127 < -1024:
                    continue
                out.append(kt)
            return out

        def mask(kt, qg):
            y0 = 1920 - 128 * kt + 512 * qg
            return ta[:, y0:y0 + 512]
        for hp in range(4):
            wt, wb = self.wslot()
            wv = wt[:, :].rearrange("p (k n) -> p k n", k=8)
            S.dma("pool", wv, self.win_v[:, :, OFF_A_QK + hp * 512:OFF_A_QK + (hp + 1) * 512], writes=[wb])
            for g in range(4):
                tok = slice(g * 512, (g + 1) * 512)
                for wi_, (dst, bd) in enumerate(((qf, b_q), (kf, b_k))):
                    p1, p1b = self.pget()
                    p2, p2b = self.pget()
                    rd = [b_h[k][g] for k in range(8)] + [wb]
                    c0 = wi_ * 256
                    self.mm(p1[:, :], [(wv[:, k, c0:c0 + 128], hT[:, k, tok]) for k in range(8)], rd, p1b)
                    self.mm(p2[:, :], [(wv[:, k, c0 + 128:c0 + 256], hT[:, k, tok]) for k in range(8)], rd, p2b)
                    S.op("dve", lambda e, p1=p1, tok=tok: e.tensor_tensor(out=t1[0][:, :], in0=p1[:, :], in1=self.rotC[:, tok], op=ALU.mult),
                         reads=[p1b, self.b_rot], writes=[b_t1[0]])
                    S.op("dve", lambda e, p2=p2, tok=tok: e.tensor_tensor(out=t1[1][:, :], in0=p2[:, :], in1=self.rotS[:, tok], op=ALU.mult),
                         reads=[p2b, self.b_rot], writes=[b_t1[1]])
                    S.op("pool", lambda e, dst=dst, tok=tok: e.tensor_tensor(out=dst[:, tok], in0=t1[0][:, :], in1=t1[1][:, :], op=ALU.add),
                         reads=[b_t1[0], b_t1[1]], writes=[bd[g]])
            self._attn_core(hp, qf, kf, b_q, b_k, vtm, b_v, oT, b_o, blocks, mask, [b_ta], None, pt, b_pt, rdt, b_rd)

    def _mix_B(self, l, s, oT, b_o, pes):
        S = self.S
        hT, b_h = self.hT, self.b_h
        K = 1024
        vtm = self._al(16 * K, [128, 16, 512], BF16)
        b_v = [Buf("v%d" % t) for t in range(16)]
        qf = self._al(32 * K, [128, S_LEN], BF16)
        kf = self._al(36 * K, [128, S_LEN], BF16)
        bu = self._al(40 * K, [128, S_LEN], BF16)
        bw = self._al(44 * K, [128, S_LEN], BF16)
        b_bb = Buf("bb")
        pt = [self._al(48 * K + i * K, [128, 512], BF16) for i in range(6)]
        b_pt = [Buf("pt%d" % i) for i in range(6)]
        eraw = self._al(54 * K, [128, 1920], F32)
        b_er = Buf("er")
        rdt = self._al(62 * K, [64, 512], F32)
        b_rd = Buf("rd")
        eb = self.sb("eb", [128, 1920], BF16, pes)
        b_eb = Buf("eb")
        b_q = [Buf("q%d" % g) for g in range(4)]
        b_k = [Buf("k%d" % g) for g in range(4)]
        S.dma("pool", bu[0:32, :], self.bu_d[:, :], writes=[b_bb])
        S.dma("pool", bw[0:32, :], self.bw_d[:, :], writes=[b_bb])
        self._vtm(OFF_B_V, vtm, b_v)
        rows = 32
        rs = np.clip(np.arange(rows) - 4, 0, rows - 8)

        def blocks(qg):
            a = 8 * qg
            out = []
            for i in range(16):
                ok = False
                for qr in range(a, a + 8):
                    for kr in (2 * i, 2 * i + 1):
                        if rs[qr] <= kr < rs[qr] + 8:
                            ok = True
                if ok:
                    j0 = a - 2 * i + 14
                    assert 0 <= j0 <= 22, (a, i, j0)
                    out.append(i)
            return out

        def mask(i, qg):
            j0 = 8 * qg - 2 * i + 14
            return eb[:, j0 * 64:j0 * 64 + 512]

        def extra(i, qg):
            return (bu[0:32, i * 128:(i + 1) * 128], bw[0:32, qg * 512:(qg + 1) * 512], [b_bb])
        for hp in range(4):
            wt, wb = self.wslot()
            wv = wt[:, 0:2048].rearrange("p (k n) -> p k n", k=8)
            S.dma("pool", wv, self.win_v[:, :, OFF_B_QK + hp * 256:OFF_B_QK + (hp + 1) * 256], writes=[wb])
            for g in range(4):
                tok = slice(g * 512, (g + 1) * 512)
                for wi_, (dst, bd) in enumerate(((qf, b_q), (kf, b_k))):
                    p1, p1b = self.pget()
                    rd = [b_h[k][g] for k in range(8)] + [wb]
                    self.mm(p1[:, :], [(wv[:, k, wi_ * 128:(wi_ + 1) * 128], hT[:, k, tok]) for k in range(8)], rd, p1b)
                    S.op("act", lambda e, p1=p1, dst=dst, tok=tok: e.activation(out=dst[:, tok], in_=p1[:, :], func=AF.Copy),
                         reads=[p1b], writes=[bd[g]])
            for hh in range(2):
                pass
            self._attn_core_B(l, hp, qf, kf, b_q, b_k, vtm, b_v, oT, b_o, blocks, mask, extra, pt, b_pt, rdt, b_rd, eraw, b_er, eb, b_eb)

    def _attn_core_B(self, l, hp, qf, kf, b_q, b_k, vtm, b_v, oT, b_o, blocks, mask, extra, pt, b_pt, rdt, b_rd, eraw, b_er, eb, b_eb):
        S = self.S
        for hh in range(2):
            h = hp * 2 + hh
            S.dma("sp", eraw[0:64, :], self.braw_d[l, h, :, 64:31 * 64], writes=[b_er])
            S.dma("sp", eraw[64:128, :], self.braw_d[l, h, :, 0:30 * 64], writes=[b_er])
            S.op("act", lambda e: e.activation(out=eb[:, :], in_=eraw[:, :], func=AF.Exp), reads=[b_er], writes=[b_eb])
            self._attn_core_one(hp, hh, qf, kf, b_q, b_k, vtm, b_v, oT, b_o, blocks, mask, [b_eb], extra, pt, b_pt, rdt, b_rd)

    def _mix_C(self, l, s, oT, b_o, pes):
        S = self.S
        hT, b_h = self.hT, self.b_h
        K = 1024
        cdec = DECAY_SCALE
        cur = [16 * K]

        def T(shape, dt, name):
            n = int(np.prod(shape[1:])) * (2 if dt == BF16 else 4)
            off = cur[0]
            cur[0] += n
            assert cur[0] <= 64 * K, cur[0]
            return self._al(off, [128] + list(shape[1:]), dt), Buf(name)
        B = [128, 512]
        rS, b_rS = T(B, F32, "rS")
        kS, b_kS = T(B, F32, "kS")
        sgm, b_sgm = T(B, F32, "sgm")
        ai, b_ai = T(B, F32, "ai")
        pre, b_pre = T(B, F32, "pre")
        ein, b_ein = T(B, F32, "ein")
        eex, b_eex = T(B, F32, "eex")
        rem, b_rem = T(B, F32, "rem")
        kkn, b_kkn = T(B, F32, "kkn")
        kmod, b_kmod = T(B, F32, "kmod")
        kb, b_kb = T(B, F32, "kb")
        t0, b_t0 = T(B, F32, "t0")
        t1, b_t1 = T(B, F32, "t1")
        cm, b_cm = self.sb("cm", B, F32, pes), Buf("cm")
        acc, b_acc = T([128, S_LEN], BF16, "acc")
        th, b_th = T(B, BF16, "th")
        xa, b_xa = T(B, BF16, "xa")
        aF, b_aF = T(B, BF16, "aF")
        bF, b_bF = T(B, BF16, "bF")
        kF, b_kF = T(B, BF16, "kF")
        rF, b_rF = T(B, BF16, "rF")
        bgF, b_bgF = T(B, BF16, "bgF")
        kgF, b_kgF = T(B, BF16, "kgF")
        vF, b_vF = T(B, BF16, "vF")
        sqb, b_sqb = T(B, BF16, "sqb")
        vT, b_vT = T([128, 4, 128], BF16, "vT")
        bgT, b_bgT = T([128, 4, 128], BF16, "bgT")
        kgT, b_kgT = T([128, 4, 128], BF16, "kgT")
        sq = [T([128, 128], BF16, "sqm%d" % i) for i in range(4)]
        AkT, b_AkT = T([128, 128], BF16, "AkT")
        MrbT, b_MrbT = T([128, 128], BF16, "MrbT")
        MrkT, b_MrkT = T([128, 128], BF16, "MrkT")
        ynT, b_ynT = T([128, 128], BF16, "ynT")
        X = [T([128, 64], BF16, "X%d" % i) for i in range(2)]
        H, b_H = T([128, 64], BF16, "H")
        tH, b_tH = T([128, 64], F32, "tH")
        gC, b_gC = T([128, 4], F32, "gC")
        st6, b_st6 = T([128, 6], F32, "st6")
        mv, b_mv = T([128, 2], F32, "mv")
        murow, b_mu = self.sb("murow", B, F32, pes), Buf("murow")
        gupt, b_gup = T(B, BF16, "gupt")
        b_Hh = [Buf("H0"), Buf("H1")]
        b_tHh = [Buf("tH0"), Buf("tH1")]
        self.ps_g = self.ps_all
        self.ps_rr["g"] = 0
        TS0 = (sq, AkT, b_AkT, MrbT, b_MrbT, MrkT, b_MrkT, X, st6, b_st6, mv, b_mv)
        t2 = self.sb("c2m", [128, 7, 128], BF16, pes)
        t2x = self.sb("c2x", [128, 2, 64], BF16, pes)
        t2s = self.sb("c2s", [128, 8], F32, pes)
        TS1 = ([(t2[:, i, :], Buf("sq2_%d" % i)) for i in range(4)], t2[:, 4, :], Buf("AkT2"), t2[:, 5, :], Buf("MrbT2"),
               t2[:, 6, :], Buf("MrkT2"), [(t2x[:, i, :], Buf("X2_%d" % i)) for i in range(2)], t2s[:, 0:6], Buf("st62"),
               t2s[:, 6:8], Buf("mv2"))
        cst = self.cst
        LT, LE, GT, GE, BLK = (cst[:, 256:384], cst[:, 384:512], cst[:, 512:640], cst[:, 640:768], cst[:, 768:896])
        S.dma("sp", cm[:, :], self.cst2_d[:, :], writes=[b_cm])
        S.dma("pool", gupt, self.gup_d[l], writes=[b_gup])
        wlt, b_wl = self.sb("wlt", [128, 2048], BF16, pes), Buf("wl")
        wl = wlt[0:64, 0:2048].rearrange("p (a n) -> p a n", a=4)
        for d in range(2):
            S.dma("pool", wl[:, d * 2, :], self.wup_d[l, d], writes=[b_wl])
            S.dma("pool", wl[:, d * 2 + 1, :], self.aup_d[l, d], writes=[b_wl])
        wgt, b_wg = self.sb("wgt", [128, 1024], BF16, pes), Buf("wg")
        wg = wgt[:, 0:1024].rearrange("p (k n) -> p k n", k=8)
        S.dma("pool", wg, self.win_v[:, :, OFF_C_G:OFF_C_G + 128], writes=[b_wg])
        mode = getattr(self, "cmode", None)
        for hp in range(4):
            for d in range(2):
                if mode in ("acc_d0", "yn_d0", "bonus_d0") and d == 1:
                    continue
                if mode == "acc_d1" and d == 0:
                    continue
                wrt, b_wr = self.wslot()
                w1t, b_w1 = self.wslot()
                wr = wrt[:, :].rearrange("p (k n) -> p k n", k=8)
                w1 = w1t[:, :].rearrange("p (k n) -> p k n", k=8)
                for j, c0 in enumerate((OFF_C_R + hp * 128, OFF_C_K + hp * 128, OFF_C_V + hp * 128)):
                    S.dma("pool", wr[:, :, j * 128:(j + 1) * 128], self.win_v[:, :, c0:c0 + 128], writes=[b_wr])
                    S.dma("sp", murow[:, j * 128:(j + 1) * 128],
                          self.murkv_d[l, d, j:j + 1, hp * 128:(hp + 1) * 128].to_broadcast([128, 128]), writes=[b_mu])
                S.dma("pool", wr[:, :, 384:448], self.win_v[:, :, OFF_C_LW + d * 64:OFF_C_LW + d * 64 + 64], writes=[b_wr])
                S.dma("pool", wr[:, :, 448:512], self.win_v[:, :, OFF_C_LA + d * 64:OFF_C_LA + d * 64 + 64], writes=[b_wr])
                S.dma("sp", murow[:, 384:448], self.muw_d[l, d:d + 1, :].to_broadcast([128, 64]), writes=[b_mu])
                S.dma("sp", murow[:, 448:512], self.mua_d[l, d:d + 1, :].to_broadcast([128, 64]), writes=[b_mu])
                mub = murow[:, :].unsqueeze(1).to_broadcast([128, 8, 512])
                S.op("dve", lambda e, w1=w1, wr=wr, mub=mub: e.tensor_tensor(out=w1, in0=wr, in1=mub, op=ALU.mult), reads=[b_wr, b_mu], writes=[b_w1])
                S.op("dve", lambda e, w1=w1, wr=wr: e.tensor_tensor(out=wr, in0=wr, in1=w1, op=ALU.subtract), reads=[b_wr, b_w1], writes=[b_wr])
                S.op("pool", lambda e: e.memset(H, 0.0), writes=b_Hh)
                strict_st, strict_ts, incl_st = (LT, GT, LE) if d == 0 else (GT, LT, GE)
                w0c = self.ppc("w0", (l * 2 + d) * 4 + hp)
                a0c = self.ppc("a0", (l * 2 + d) * 4 + hp)
                for bi_ in range(4):
                    bi = bi_ if d == 0 else 3 - bi_
                    t0_ = bi * 512
                    tok = slice(t0_, t0_ + 512)
                    if d == 0:
                        lo, hi = (1 if bi == 0 else 0), 512
                        nb = slice(t0_ + lo - 1, t0_ + 511)
                    else:
                        lo, hi = 0, (511 if bi == 3 else 512)
                        nb = slice(t0_ + 1, t0_ + hi + 1)
                    rdh = [b_h[k][bi] for k in range(8)]
                    if d == 0 and bi > 0:
                        rdh += [b_h[k][bi - 1] for k in range(8)]
                    if d == 1 and bi < 3:
                        rdh += [b_h[k][bi + 1] for k in range(8)]

                    def proj(c0, n, ps, psb):
                        m = ps[0:n, :]
                        for k in range(8):
                            S.op("pe", lambda e, k=k: e.matmul(m, wr[:, k, c0:c0 + n], hT[:, k, tok], start=(k == 0), stop=False),
                                 reads=rdh + [b_wr], writes=[psb])
                        for k in range(8):
                            S.op("pe", lambda e, k=k: e.matmul(ps[0:n, lo:hi], w1[:, k, c0:c0 + n], hT[:, k, nb], start=False, stop=(k == 7)),
                                 reads=rdh + [b_w1], writes=[psb])
                    ps, psb = self.pget()
                    proj(0, 128, ps, psb)
                    S.op("act", lambda e, ps=ps: e.activation(out=rS, in_=ps[:, :], func=AF.Copy), reads=[psb], writes=[b_rS])
                    ps, psb = self.pget()
                    proj(128, 128, ps, psb)
                    S.op("act", lambda e, ps=ps: e.activation(out=kS, in_=ps[:, :], func=AF.Copy), reads=[psb], writes=[b_kS])
                    ps, psb = self.pget()
                    proj(256, 128, ps, psb)
                    S.op("act", lambda e, ps=ps: e.activation(out=vF, in_=ps[:, :], func=AF.Copy), reads=[psb], writes=[b_vF])
                    ps, psb = self.pget()
                    proj(384, 64, ps, psb)
                    S.op("act", lambda e, ps=ps: e.activation(out=th[0:64, :], in_=ps[0:64, :], func=AF.Tanh), reads=[psb], writes=[b_th])
                    ps, psb = self.pget()
                    proj(448, 64, ps, psb)
                    S.op("act", lambda e, ps=ps: e.activation(out=xa[0:64, :], in_=ps[0:64, :], func=AF.Copy), reads=[psb], writes=[b_xa])
                    ps, psb = self.pget()
                    self.mm(ps[:, :], [(wl[:, d * 2, hp * 128:(hp + 1) * 128], th[0:64, :])], [b_wl, b_th], psb)
                    S.op("act", lambda e, ps=ps: e.activation(out=sgm, in_=ps[:, :], func=AF.Sigmoid, bias=w0c), reads=[psb, self.b_pp], writes=[b_sgm])
                    ps, psb = self.pget()
                    self.mm(ps[:, :], [(wl[:, d * 2 + 1, hp * 128:(hp + 1) * 128], xa[0:64, :])], [b_wl, b_xa], psb)
                    S.op("act", lambda e, ps=ps: e.activation(out=ai, in_=ps[:, :], func=AF.Sigmoid, bias=a0c), reads=[psb, self.b_pp], writes=[b_ai])
                    S.op("dve", lambda e: e.tensor_tensor_scan(out=pre, data0=cm[:, :], data1=sgm, initial=0.0, op0=ALU.mult, op1=ALU.add),
                         reads=[b_cm, b_sgm], writes=[b_pre])
                    pre3 = pre.rearrange("p (a b) -> p a b", a=4)
                    etb = pre3[:, :, 127:128].to_broadcast([128, 4, 128])
                    v3 = lambda x: x.rearrange("p (a b) -> p a b", a=4)
                    if d == 0:
                        S.op("dve", lambda e: e.tensor_copy(out=ein, in_=pre), reads=[b_pre], writes=[b_ein])
                        S.op("dve", lambda e: e.tensor_tensor(out=eex, in0=pre, in1=sgm, op=ALU.subtract), reads=[b_pre, b_sgm], writes=[b_eex])
                        S.op("dve", lambda e: e.tensor_tensor(out=v3(rem), in0=etb, in1=pre3, op=ALU.subtract), reads=[b_pre], writes=[b_rem])
                    else:
                        S.op("dve", lambda e: e.tensor_tensor(out=v3(eex), in0=etb, in1=pre3, op=ALU.subtract), reads=[b_pre], writes=[b_eex])
                        S.op("dve", lambda e: e.tensor_tensor(out=ein, in0=eex, in1=sgm, op=ALU.add), reads=[b_eex, b_sgm], writes=[b_ein])
                        S.op("dve", lambda e: e.tensor_tensor(out=rem, in0=pre, in1=sgm, op=ALU.subtract), reads=[b_pre, b_sgm], writes=[b_rem])
                    S.op("act", lambda e: e.activation(out=gC, in_=pre3[:, :, 127], func=AF.Exp, scale=-cdec), reads=[b_pre], writes=[b_gC])
                    S.op("dve", lambda e: e.tensor_scalar(out=t0, in0=kS, scalar1=self.ppc("k_k", l * 4 + hp), scalar2=None, op0=ALU.mult),
                         reads=[b_kS, self.b_pp], writes=[b_t0])
                    S.op("act", lambda e: e.activation(out=sqb, in_=t0, func=AF.Square), reads=[b_t0], writes=[b_sqb])
                    ps, psb = self.pget()
                    self.mm(ps[:, :], [(BLK, sqb)], [self.b_cst, b_sqb], psb)
                    S.op("act", lambda e, ps=ps: e.activation(out=t1, in_=ps[:, :], func=AF.Ln, bias=self.ppc("eps")), reads=[psb, self.b_pp], writes=[b_t1])
                    S.op("act", lambda e: e.activation(out=t1, in_=t1, func=AF.Exp, scale=-0.5), reads=[b_t1], writes=[b_t1])
                    S.op("dve", lambda e: e.tensor_tensor(out=kkn, in0=t0, in1=t1, op=ALU.mult), reads=[b_t0, b_t1], writes=[b_kkn])
                    S.op("dve", lambda e: e.tensor_scalar(out=t0, in0=ai, scalar1=-1.0, scalar2=self.ppc("k_a", l * 4 + hp), op0=ALU.add, op1=ALU.mult),
                         reads=[b_ai, self.b_pp], writes=[b_t0])
                    S.op("dve", lambda e: e.scalar_tensor_tensor(out=kmod, in0=t0, scalar=1.0, in1=kS, op0=ALU.add, op1=ALU.mult),
                         reads=[b_t0, b_kS], writes=[b_kmod])
                    S.op("dve", lambda e: e.tensor_tensor(out=kb, in0=kkn, in1=ai, op=ALU.mult), reads=[b_kkn, b_ai], writes=[b_kb])
                    S.op("act", lambda e: e.activation(out=t1, in_=eex, func=AF.Exp, scale=-cdec), reads=[b_eex], writes=[b_t1])
                    S.op("dve", lambda e: e.scalar_tensor_tensor(out=aF, in0=kkn, scalar=-1.0, in1=t1, op0=ALU.mult, op1=ALU.mult),
                         reads=[b_kkn, b_t1], writes=[b_aF])
                    S.op("act", lambda e: e.activation(out=t1, in_=ein, func=AF.Exp, scale=cdec), reads=[b_ein], writes=[b_t1])
                    S.op("dve", lambda e: e.tensor_tensor(out=bF, in0=kb, in1=t1, op=ALU.mult), reads=[b_kb, b_t1], writes=[b_bF])
                    S.op("dve", lambda e: e.tensor_tensor(out=kF, in0=kmod, in1=t1, op=ALU.mult), reads=[b_kmod, b_t1], writes=[b_kF])
                    S.op("act", lambda e: e.activation(out=t1, in_=ein, func=AF.Exp, scale=-cdec), reads=[b_ein], writes=[b_t1])
                    S.op("dve", lambda e: e.tensor_tensor(out=rF, in0=rS, in1=t1, op=ALU.mult), reads=[b_rS, b_t1], writes=[b_rF])
                    S.op("act", lambda e: e.activation(out=t1, in_=rem, func=AF.Exp, scale=-cdec), reads=[b_rem], writes=[b_t1])
                    S.op("dve", lambda e: e.tensor_tensor(out=bgF, in0=kb, in1=t1, op=ALU.mult), reads=[b_kb, b_t1], writes=[b_bgF])
                    S.op("dve", lambda e: e.tensor_tensor(out=kgF, in0=kmod, in1=t1, op=ALU.mult), reads=[b_kmod, b_t1], writes=[b_kgF])
                    S.op("dve", lambda e: e.scalar_tensor_tensor(out=sqb, in0=rS, scalar=self.ppc("r_k", l * 4 + hp), in1=kmod, op0=ALU.mult, op1=ALU.mult),
                         reads=[b_rS, b_kmod, self.b_pp], writes=[b_sqb])
                    ps, psb = self.pget()
                    self.mm(ps[:, :], [(BLK, sqb)], [self.b_cst, b_sqb], psb)
                    S.op("dve", lambda e, ps=ps: e.tensor_tensor(out=kb, in0=ps[:, :], in1=vF, op=ALU.mult), reads=[psb, b_vF, b_bgF], writes=[b_kb])
                    bonus, b_bonus = kb, b_kb
                    for cb in range(4):
                        cs = slice(cb * 128, (cb + 1) * 128)
                        for src, bs, dst, bd in ((vF, b_vF, vT, b_vT), (bgF, b_bgF, bgT, b_bgT), (kgF, b_kgF, kgT, b_kgT)):
                            ps, psb = self.pget()
                            self.mm(ps[:, 0:128], [(src[:, cs], self.ident)], [bs, self.b_cst], psb)
                            S.op("act", lambda e, ps=ps, dst=dst, cb=cb: e.activation(out=dst[:, cb, :], in_=ps[:, 0:128], func=AF.Copy), reads=[psb], writes=[bd])
                    for cb_ in range(4):
                        cb = cb_ if d == 0 else 3 - cb_
                        cs = slice(cb * 128, (cb + 1) * 128)
                        def chain(hh, TS):
                            sq_, AkT, b_AkT, MrbT, b_MrbT, MrkT, b_MrkT, X_, st6, b_st6, mv, b_mv = TS
                            ph = slice(hh * 64, hh * 64 + 64)
                            hc = slice(hh * 64, hh * 64 + 64)
                            (Nm, b_N), (Am, b_A), (N2, b_N2), (A2, b_A2) = sq_

                            def prod(dst, bd, lhs, bl, rhs, br, mask):
                                ps, psb = self.pget()
                                self.mm(ps[:, 0:128], [(lhs[ph, cs], rhs[ph, cs])], [bl, br], psb)
                                S.op("dve", lambda e, ps=ps: e.tensor_tensor(out=dst, in0=ps[:, 0:128], in1=mask, op=ALU.mult), reads=[psb, self.b_cst], writes=[bd])
                            prod(Nm, b_N, bF, b_bF, aF, b_aF, strict_st)
                            prod(Am, b_A, aF, b_aF, bF, b_bF, strict_ts)
                            yield
                            prod(AkT, b_AkT, kF, b_kF, aF, b_aF, strict_st)
                            prod(MrbT, b_MrbT, bF, b_bF, rF, b_rF, incl_st)
                            prod(MrkT, b_MrkT, kF, b_kF, rF, b_rF, incl_st)
                            yield
                            ps, psb = self.pget()
                            self.mm(ps[:, 0:64], [(aF[ph, cs], H[ph, :]), (AkT, vT[:, cb, hc])], [b_aF, b_Hh[hh], b_AkT, b_vT], psb)
                            (X0, b_X0), (X1, b_X1) = X_
                            S.op("act", lambda e, ps=ps: e.activation(out=X0, in_=ps[:, 0:64], func=AF.Copy), reads=[psb], writes=[b_X0])
                            yield
                            cur_ = (Nm, b_N, Am, b_A)
                            oth_ = (N2, b_N2, A2, b_A2)
                            xs = [(X0, b_X0), (X1, b_X1)]
                            for j in range(7):
                                Nj, bNj, Aj, bAj = cur_
                                xin, bxin = xs[j % 2]
                                xout, bxout = xs[(j + 1) % 2]
                                ps, psb = self.pget()
                                self.mm(ps[:, 0:64], [(Nj, xin)], [bNj, bxin], psb)
                                S.op("dve", lambda e, ps=ps, xin=xin, xout=xout: e.tensor_tensor(out=xout, in0=ps[:, 0:64], in1=xin, op=ALU.add),
                                     reads=[psb, bxin], writes=[bxout])
                                if j < 6:
                                    Nn, bNn, An, bAn = oth_
                                    ps, psb = self.pget()
                                    self.mm(ps[:, 0:128], [(Aj, Nj)], [bNj, bAj], psb)
                                    S.op("act", lambda e, ps=ps, Nn=Nn: e.activation(out=Nn, in_=ps[:, 0:128], func=AF.Copy), reads=[psb], writes=[bNn])
                                    ps, psb = self.pget()
                                    self.mm(ps[:, 0:128], [(Nj, Aj)], [bNj, bAj], psb)
                                    S.op("act", lambda e, ps=ps, An=An: e.activation(out=An, in_=ps[:, 0:128], func=AF.Copy), reads=[psb], writes=[bAn])
                                    cur_, oth_ = oth_, cur_
                                yield
                            U, b_U = xs[7 % 2]
                            ps, psb = self.pget()
                            self.mm(ps[:, 0:64], [(rF[ph, cs], H[ph, :]), (MrbT, U), (MrkT, vT[:, cb, hc])],
                                    [b_rF, b_Hh[hh], b_MrbT, b_U, b_MrkT, b_vT], psb)
                            S.op("dve", lambda e, ps=ps: e.bn_stats(out=st6, in_=ps[:, 0:64]), reads=[psb], writes=[b_st6])
                            S.op("dve", lambda e: e.bn_aggr(out=mv, in_=st6), reads=[b_st6], writes=[b_mv])
                            S.op("act", lambda e: e.activation(out=mv[:, 1:2], in_=mv[:, 1:2], func=AF.Ln, bias=self.ppc("gneps")), reads=[b_mv, self.b_pp], writes=[b_mv])
                            S.op("act", lambda e: e.activation(out=mv[:, 1:2], in_=mv[:, 1:2], func=AF.Exp, scale=-0.5), reads=[b_mv], writes=[b_mv])
                            S.op("dve", lambda e, ps=ps, hc=hc: e.tensor_scalar(out=ynT[:, hc], in0=ps[:, 0:64], scalar1=mv[:, 0:1], scalar2=mv[:, 1:2],
                                                                              op0=ALU.subtract, op1=ALU.mult), reads=[psb, b_mv], writes=[b_ynT])
                            yield
                            ps, psb = self.pget()
                            self.mm(ps[0:64, 0:64], [(bgT[:, cb, hc], U), (kgT[:, cb, hc], vT[:, cb, hc])], [b_bgT, b_U, b_kgT, b_vT], psb)
                            S.op("act", lambda e, ps=ps, ph=ph: e.activation(out=tH[ph, :], in_=ps[0:64, 0:64], func=AF.Copy), reads=[psb], writes=[b_tHh[hh]])
                            S.op("dve", lambda e, ph=ph, cb=cb: e.scalar_tensor_tensor(out=H[ph, :], in0=H[ph, :], scalar=gC[ph, cb:cb + 1], in1=tH[ph, :],
                                                                                    op0=ALU.mult, op1=ALU.add), reads=[b_Hh[hh], b_gC, b_tHh[hh]], writes=[b_Hh[hh]])

                        gens = [chain(0, TS0), chain(1, TS1)]
                        alive = [True, True]
                        while any(alive):
                            for gi in range(2):
                                if alive[gi]:
                                    try:
                                        next(gens[gi])
                                    except StopIteration:
                                        alive[gi] = False
                        ps, psb = self.pget()
                        self.mm(ps[:, 0:128], [(ynT, self.ident)], [b_ynT, self.b_cst], psb)
                        S.op("act", lambda e, ps=ps, cs=cs: e.activation(out=t0[:, cs], in_=ps[:, 0:128], func=AF.Identity,
                                                                        scale=self.ppc("gn_w", l * 4 + hp), bias=self.ppc("gn_b", l * 4 + hp)),
                             reads=[psb, self.b_pp], writes=[b_t0])
                    if mode == "yn_d0":
                        S.op("dve", lambda e, tok=tok: e.tensor_copy(out=acc[:, tok], in_=t0), reads=[b_t0], writes=[b_acc])
                    elif mode == "bonus_d0":
                        S.op("dve", lambda e, tok=tok: e.tensor_copy(out=acc[:, tok], in_=bonus), reads=[b_bonus], writes=[b_acc])
                    elif d == 0 or mode == "acc_d1":
                        S.op("dve", lambda e, tok=tok: e.tensor_tensor(out=acc[:, tok], in0=t0, in1=bonus, op=ALU.add), reads=[b_t0, b_bonus], writes=[b_acc])
                    else:
                        S.op("dve", lambda e: e.tensor_tensor(out=t0, in0=t0, in1=bonus, op=ALU.add), reads=[b_t0, b_bonus], writes=[b_t0])
                        S.op("pool", lambda e, tok=tok: e.tensor_tensor(out=acc[:, tok], in0=acc[:, tok], in1=t0, op=ALU.add), reads=[b_t0, b_acc], writes=[b_acc])
            for g in range(4):
                tok = slice(g * 512, (g + 1) * 512)
                ps, psb = self.pget()
                self.mm(ps[:, :], [(wg[:, k, :], hT[:, k, tok]) for k in range(8)], [b_h[k][g] for k in range(8)] + [b_wg], psb)
                S.op("act", lambda e, ps=ps: e.activation(out=sqb, in_=ps[:, :], func=AF.Sigmoid), reads=[psb], writes=[b_sqb])
                ps, psb = self.pget()
                self.mm(ps[:, :], [(gupt[:, hp * 128:(hp + 1) * 128], sqb)], [b_gup, b_sqb], psb)
                if mode == "gate":
                    S.op("dve", lambda e, ps=ps, tok=tok: e.tensor_copy(out=oT[:, hp, tok], in_=ps[:, :]), reads=[psb], writes=[b_o[hp][g]])
                elif mode is not None:
                    S.op("dve", lambda e, ps=ps, tok=tok: e.tensor_copy(out=oT[:, hp, tok], in_=acc[:, tok]), reads=[psb, b_acc], writes=[b_o[hp][g]])
                else:
                    S.op("dve", lambda e, ps=ps, tok=tok: e.tensor_tensor(out=oT[:, hp, tok], in0=ps[:, :], in1=acc[:, tok], op=ALU.mult),
                         reads=[psb, b_acc], writes=[b_o[hp][g]])
        self.ps_g = self.ps_all[0:4]
        self.ps_rr["g"] = 0


def _consts():
    cst = np.zeros((128, 1024), np.float32)
    cst[:, 0:128] = np.eye(128, dtype=np.float32)
    cst[:, 128:256] = 1.0
    p = np.arange(128)[:, None]
    f = np.arange(128)[None, :]
    cst[:, 256:384] = (p < f)
    cst[:, 384:512] = (p <= f)
    cst[:, 512:640] = (p > f)
    cst[:, 640:768] = (p >= f)
    cst[:, 768:896] = ((p // 64) == (f // 64))
    return cst


def _consts2():
    c2 = np.ones((128, 512), np.float32)
    c2[:, 0::128] = 0.0
    return c2


def _pp_static(pp, off):
    d = np.arange(128) % 64
    invf = np.where(d < 16, 500000.0 ** (-((d % 8) * 2.0) / 16.0), 0.0)
    pp[:, off["invf"]] = (invf / TWO_PI).astype(np.float32)
    pp[:, off["rsign"]] = np.where(d < 8, -1.0, np.where(d < 16, 1.0, 0.0))
    pp[:, off["eps"]] = RMS_EPS
    pp[:, off["one"]] = 1.0
    pp[:, off["gneps"]] = GN_EPS


def build_w_in_ext(w_in):
    L = w_in.shape[0]
    cols = []
    base = {"aq": 0, "ak": 512, "av": 1024, "nq": 1536, "nk": 2048, "nv": 2560, "pr": 3072, "pk": 3584, "pv": 4096,
            "pwf": 4608, "pwb": 4672, "paf": 4736, "pab": 4800, "pg": 4864, "gz0": 4992, "gz1": 6016, "gz2": 7040}
    perm64 = np.arange(64)
    perm64[0:8] = np.arange(8, 16)
    perm64[8:16] = np.arange(0, 8)
    for hp in range(4):
        idx = np.arange(hp * 128, (hp + 1) * 128)
        pidx = np.concatenate([hp * 128 + perm64, hp * 128 + 64 + perm64])
        cols += [base["aq"] + idx, base["aq"] + pidx, base["ak"] + idx, base["ak"] + pidx]
    cols.append(base["av"] + np.arange(512))
    for hp in range(4):
        idx = np.arange(hp * 128, (hp + 1) * 128)
        cols += [base["nq"] + idx, base["nk"] + idx]
    cols.append(base["nv"] + np.arange(512))
    cols.append(base["pr"] + np.arange(512))
    cols.append(base["pk"] + np.arange(512))
    cols.append(base["pv"] + np.arange(512))
    cols.append(base["pwf"] + np.arange(128))
    cols.append(base["paf"] + np.arange(128))
    cols.append(base["pg"] + np.arange(128))
    cols.append(base["gz0"] + np.arange(3072))
    cols = np.concatenate(cols)
    assert cols.shape[0] == NCOL
    return np.ascontiguousarray(w_in[:, :, cols])


def chunk_cols(v):
    sh = v.shape
    n = sh[-1] // 128
    return np.moveaxis(v.reshape(sh[:-1] + (n, 128)), -1, 0)


def prep_shared(inp, L):
    off, NPP = pp_layout(L)
    pp = np.zeros((128, NPP), np.float32)
    _pp_static(pp, off)

    def put(name, arr):
        arr = np.asarray(arr, np.float32).reshape(128, -1)
        pp[:, off[name]:off[name] + arr.shape[1]] = arr
    put("ng", chunk_cols(inp["norm_gains"][:L]))
    put("fn", chunk_cols(inp["final_norm"]))
    put("adab", chunk_cols(inp["ada_b"][:L]))
    put("mu_rkv", chunk_cols(inp["mu_rkv"][:L]))
    put("w0", chunk_cols(inp["w0"][:L]))
    put("a0", chunk_cols(inp["a0"][:L]))
    for nm in ("k_k", "k_a", "gn_w", "gn_b"):
        put(nm, chunk_cols(inp[nm][:L]))
    put("r_k", chunk_cols(inp["r_k"][:L].reshape(L, 512)))
    put("mu_w", np.moveaxis(inp["mu_w"][:L].reshape(L, 128), -1, 0))
    put("mu_a", np.moveaxis(inp["mu_a"][:L].reshape(L, 128), -1, 0))
    sh = {"pp": pp, "ada_w": np.ascontiguousarray(inp["ada_w"][:L]),
          "ffn_wi": np.ascontiguousarray(inp["ffn_wi"][:L]), "ffn_wo": np.ascontiguousarray(inp["ffn_wo"][:L]),
          "w_in": build_w_in_ext(inp["w_in"][:L]), "w_branch": np.ascontiguousarray(inp["w_branch"][:L]),
          "w_out": np.ascontiguousarray(inp["w_out"][:L]),
          "w_up": np.ascontiguousarray(inp["w_up"][:L]), "a_up": np.ascontiguousarray(inp["a_up"][:L]),
          "g_up": np.ascontiguousarray(inp["g_up"][:L]), "cst": _consts(), "cst2": _consts2(),
          "mu_rkv": np.ascontiguousarray(inp["mu_rkv"][:L]), "mu_w": np.ascontiguousarray(inp["mu_w"][:L]),
          "mu_a": np.ascontiguousarray(inp["mu_a"][:L])}
    sh.update(attn_tables(inp["rpb"][:L]))
    return sh


def attn_tables(rpb):
    L = rpb.shape[0]
    dd = np.arange(128)[:, None] - (np.arange(3968)[None, :] - 1920)
    ad = np.abs(dd)
    ta = ((ad <= 64).astype(np.float32) + ((dd % 4 == 0) & (ad <= 256)) + ((dd % 16 == 0) & (ad <= 1024))).astype(np.float32)
    rows = 32
    rs = np.clip(np.arange(rows) - 4, 0, rows - 8)
    valid = (np.arange(rows)[None, :] >= rs[:, None]) & (np.arange(rows)[None, :] < rs[:, None] + 8)
    tokrow = np.arange(S_LEN) // 64
    bu = (tokrow[None, :] == np.arange(32)[:, None]).astype(np.float32)
    bw = np.where(valid[tokrow, :].T, 0.0, -240000.0).astype(np.float32)
    kc = np.arange(64)[:, None, None]
    jj = np.arange(31)[None, :, None]
    qc = np.arange(64)[None, None, :]
    drp = jj - 15
    cs = np.clip(qc - 8, 0, 48)
    colv = (kc >= cs) & (kc < cs + 16)
    ok = colv & (np.abs(drp) <= 7)
    roff = np.clip(-drp + 7, 0, 14)
    coff = np.clip(kc - qc, -15, 15) + 15
    g = rpb[:, :, roff, coff]
    braw = np.where(ok[None, None], g, np.float32(-30000.0)).astype(np.float32)
    return {"ta": ta, "bu": bu, "bw": bw, "braw": np.ascontiguousarray(braw.reshape(L, 8, 64, 31 * 64))}


def prep_core(inp, seqs):
    x = inp["x"][seqs]
    n = len(seqs)
    xT = np.ascontiguousarray(x.reshape(n, S_LEN, NCH, 128).transpose(0, 3, 2, 1))
    cT = np.ascontiguousarray(inp["c"][seqs].reshape(n, NCH, 128).transpose(2, 1, 0))
    pos = np.ascontiguousarray(inp["positions"][seqs]).astype(np.int32)
    return {"xT": xT, "cT": cT, "pos": pos}


def run_model(inp, n_cores=8, NSEQ=2, L=4, parts=("ffn", "A", "B", "C"), seq_lists=None, trace=False, debug=False, cmode=None):
    prog = Prog(NSEQ=NSEQ, L=L, parts=parts)
    prog.debug = debug
    prog.cmode = cmode
    nc = prog.build()
    shared = prep_shared(inp, L)
    if seq_lists is None:
        seq_lists = [list(range(i * NSEQ, (i + 1) * NSEQ)) for i in range(n_cores)]
    in_maps = []
    for sl in seq_lists:
        m = dict(shared)
        m.update(prep_core(inp, sl))
        in_maps.append(m)
    res = run_bass_kernel_spmd(nc, in_maps, core_ids=list(range(len(seq_lists))), trace=trace)
    prog.dbg_out = res.results[0].get("dbg") if debug else None
    outs = []
    for r in res.results:
        o = r["outT"]
        outs.append(o.transpose(0, 3, 2, 1).reshape(o.shape[0], S_LEN, D))
    return np.concatenate(outs, axis=0), res, prog


def kernel(**inputs):
    inp = {k: np.asarray(v) for k, v in inputs.items()}
    out, _, _ = run_model(inp)
    return np.ascontiguousarray(out.astype(np.float32))
```

```python
import contextlib
import numpy as np
import concourse.bass as bass
import concourse.mybir as mybir
from concourse.bass_utils import run_bass_kernel_spmd

F32 = mybir.dt.float32
BF16 = mybir.dt.bfloat16
I32 = mybir.dt.int32
ALU = mybir.AluOpType
AF = mybir.ActivationFunctionType


class Buf:
    __slots__ = ("name", "w", "r")

    def __init__(self, name):
        self.name = name
        self.w = []
        self.r = []

    def add_writer(self, tok):
        if tok[0] == "c":
            self.w = [t for t in self.w if not (t[0] == "c" and t[1] == tok[1])]
        self.w.append(tok)
        if len(self.w) > 12:
            self.w = self.w[-12:]
        self.r = []


class Sched:
    ENG = ("pe", "act", "dve", "pool", "sp")
    NDS = 8

    def __init__(self, nc, es):
        self.nc = nc
        self.eng = {"pe": nc.tensor, "act": nc.scalar, "dve": nc.vector, "pool": nc.gpsimd, "sp": nc.sync}
        self.sem = {e: es.enter_context(nc.semaphore("s_" + e)) for e in self.ENG}
        self.cnt = {e: 0 for e in self.ENG}
        self.dsem = {q: [es.enter_context(nc.semaphore("d_%s%d" % (q, i))) for i in range(self.NDS)]
                     for q in ("sp", "pool", "act")}
        self.dcnt = {q: 0 for q in self.dsem}
        self.waited = {e: {} for e in self.ENG}
        self.ninst = 0

    def _wait(self, e, tok):
        if tok is None:
            return
        kind = tok[0]
        if kind == "c":
            _, e2, i2 = tok
            if e2 == e and e == "pe":
                return
            key = ("c", e2)
            if self.waited[e].get(key, 0) >= i2:
                return
            self.waited[e][key] = i2
            self.eng[e].wait_ge(self.sem[e2], i2)
        else:
            _, q, j = tok
            k = j % self.NDS
            val = 16 * (j // self.NDS + 1)
            key = ("d", q, k)
            if self.waited[e].get(key, 0) >= val:
                return
            self.waited[e][key] = val
            self.eng[e].wait_ge(self.dsem[q][k], val)

    def _deps(self, e, reads, writes, same_war=False):
        for b in reads:
            for t in b.w:
                self._wait(e, t)
        for b in writes:
            for t in b.w:
                self._wait(e, t)
            for t in b.r:
                self._wait(e, t)

    def op(self, e, fn, reads=(), writes=()):
        self._deps(e, reads, writes)
        ins = fn(self.eng[e])
        self.cnt[e] += 1
        ins.then_inc(self.sem[e], 1)
        tok = ("c", e, self.cnt[e])
        for b in reads:
            b.r.append(tok)
        for b in writes:
            b.add_writer(tok)
        self.ninst += 1
        return tok

    def dma(self, q, out, in_, reads=(), writes=()):
        e = q
        j = self.dcnt[q]
        if j >= self.NDS:
            self._wait(e, ("d", q, j - self.NDS))
        self._deps(e, reads, writes)
        k = j % self.NDS
        self.eng[e].dma_start(out=out, in_=in_).then_inc(self.dsem[q][k], 16)
        self.dcnt[q] += 1
        tok = ("d", q, j)
        for b in reads:
            b.r.append(tok)
        for b in writes:
            b.add_writer(tok)
        self.ninst += 1
        return tok

    def barrier(self):
        toks = [("c", e, self.cnt[e]) for e in self.ENG if self.cnt[e] > 0]
        for q in self.dsem:
            for j in range(max(0, self.dcnt[q] - self.NDS), self.dcnt[q]):
                toks.append(("d", q, j))
        for e in self.ENG:
            for t in toks:
                self._wait(e, t)

    def finish(self, e="sp"):
        self.barrier()


D = 1024
S_LEN = 2048
NCH = 8
DFF = 2816
NJ = DFF // 128
L_ALL = 4
WM = 512
RMS_EPS = 1e-6
GN_EPS = 64e-5
DECAY_SCALE = 0.6065306597126334
TWO_PI = 6.283185307179586

OFF_A_QK = 0
OFF_A_V = 2048
OFF_B_QK = 2560
OFF_B_V = 3584
OFF_C_R = 4096
OFF_C_K = 4608
OFF_C_V = 5120
OFF_C_LW = 5632
OFF_C_LA = 5760
OFF_C_G = 5888
OFF_GZ = 6016
NCOL = OFF_GZ + 3 * 1024


def pp_layout(L):
    off = {}
    n = 0
    for name, size in (("ng", L * 24), ("fn", 8), ("adab", L * 72), ("mu_rkv", L * 24), ("w0", L * 8),
                       ("a0", L * 8), ("k_k", L * 4), ("k_a", L * 4), ("gn_w", L * 4), ("gn_b", L * 4),
                       ("r_k", L * 4), ("mu_w", L), ("mu_a", L), ("invf", 1), ("rsign", 1), ("eps", 1),
                       ("one", 1), ("gneps", 1)):
        off[name] = n
        n += size
    return off, n


class Prog:
    def __init__(self, NSEQ=2, L=4, parts=("ffn", "A", "B", "C")):
        self.NSEQ, self.L, self.parts = NSEQ, L, parts
        self.ppo, self.NPP = pp_layout(L)
        self.nc = bass.Bass("TRN2", target_bir_lowering=False)
        self.es = contextlib.ExitStack()

    def dram(self, name, shape, dtype=F32, kind="ExternalInput"):
        return self.nc.dram_tensor(name, list(shape), dtype, kind=kind).ap()

    def sb(self, name, shape, dtype, es=None):
        self._uid = getattr(self, "_uid", 0) + 1
        return (es or self.es).enter_context(self.nc.sbuf_tensor("%s_%d" % (name, self._uid), list(shape), dtype))

    def ppc(self, name, idx=0, n=1):
        o = self.ppo[name] + idx
        return self.ppt[:, o:o + n]

    def pget(self, pool="g"):
        lst = self.ps_g if pool == "g" else self.ps_a
        k = self.ps_rr[pool]
        self.ps_rr[pool] = (k + 1) % len(lst)
        return lst[k]

    def wslot(self):
        k = self.ws_rr
        self.ws_rr = (k + 1) % len(self.wsl)
        return self.wsl[k]

    def mm(self, ps, pairs, reads, psb, extra_reads=()):
        S = self.S
        n = len(pairs)
        for i, (a, b) in enumerate(pairs):
            S.op("pe", lambda e, a=a, b=b, i=i: e.matmul(ps, a, b, start=(i == 0), stop=(i == n - 1)),
                 reads=reads, writes=[psb])

    def build(self):
        nc, es, NSEQ, L = self.nc, self.es, self.NSEQ, self.L
        with es:
            self._declare()
            self.S = Sched(nc, es)
            self._alloc_global()
            self._prologue()
            for s in range(NSEQ):
                self._seq(s)
            self.S.finish()
        return nc

    def _declare(self):
        NSEQ, L = self.NSEQ, self.L
        d = self.dram
        self.xT_d = d("xT", [NSEQ, 128, NCH, S_LEN])
        self.cT_d = d("cT", [128, NCH, NSEQ])
        self.pos_d = d("pos", [NSEQ, S_LEN], I32)
        self.pp_d = d("pp", [128, self.NPP])
        self.ada_w_d = d("ada_w", [L, D, 9 * D])
        self.wi_d = d("ffn_wi", [L, 2, D, 2 * DFF])
        self.wo_d = d("ffn_wo", [L, 2, DFF, D])
        self.win_d = d("w_in", [L, D, NCOL])
        self.wbr_d = d("w_branch", [L, 3, WM, D])
        self.wout_d = d("w_out", [L, D, D])
        self.ta_d = d("ta", [128, 3968])
        self.bu_d = d("bu", [32, S_LEN])
        self.bw_d = d("bw", [32, S_LEN])
        self.braw_d = d("braw", [L, 8, 64, 31 * 64])
        self.wup_d = d("w_up", [L, 2, 64, WM])
        self.aup_d = d("a_up", [L, 2, 64, WM])
        self.gup_d = d("g_up", [L, 128, WM])
        self.cst_d = d("cst", [128, 1024])
        self.cst2_d = d("cst2", [128, 512])
        self.murkv_d = d("mu_rkv", [L, 2, 3, WM])
        self.muw_d = d("mu_w", [L, 2, 64])
        self.mua_d = d("mu_a", [L, 2, 64])
        self.out_d = d("outT", [NSEQ, 128, NCH, S_LEN], kind="ExternalOutput")
        self.xs_d = d("xspill", [128, NCH, S_LEN], kind="Internal")
        self.b_xs = Buf("xs")
        self.dbg_d = d("dbg", [128, 4, S_LEN], kind="ExternalOutput") if getattr(self, "debug", False) else None

    def _alloc_global(self):
        nc, es = self.nc, self.es
        sb = self.sb
        self.xT = sb("xT_s", [128, NCH, S_LEN], F32)
        self.b_x = [[Buf("x%d_%d" % (c, g)) for g in range(4)] for c in range(NCH)]
        self.wsl = [(sb("wsl%d" % i, [128, 4096], BF16), Buf("wsl%d" % i)) for i in range(4)]
        self.ws_rr = 0
        self.ppt = sb("ppt", [128, self.NPP], F32)
        self.b_pp = Buf("pp")
        self.modT = sb("modT", [128, self.L, 72, self.NSEQ], F32)
        self.b_mod = Buf("mod")
        self.cst = sb("cstt", [128, 1024], BF16)
        self.cstf = sb("cstf", [128, 1024], F32)
        self.b_cst = Buf("cst")
        self.rotC = sb("rotC", [128, S_LEN], BF16)
        self.rotS = sb("rotS", [128, S_LEN], BF16)
        self.b_rot = Buf("rot")
        self.der = sb("der", [128, 64], F32)
        self.b_der = Buf("der")
        pst = [es.enter_context(nc.psum_tensor("ps%d" % i, [128, 512], F32)) for i in range(8)]
        self.ps_all = [(pst[i], Buf("ps%d" % i)) for i in range(8)]
        self.ps_g = self.ps_all[0:4]
        self.ps_a = self.ps_all[4:8]
        self.ps_rr = {"g": 0, "a": 0}
        self.ident = self.cst[:, 0:128]
        self.ones = self.cst[:, 128:256]

    def _prologue(self):
        S, nc = self.S, self.nc
        L, NSEQ = self.L, self.NSEQ
        S.dma("sp", self.ppt[:, :], self.pp_d[:, :], writes=[self.b_pp])
        S.dma("pool", self.cst[:, :], self.cst_d[:, :], writes=[self.b_cst])
        S.dma("sp", self.cstf[:, :], self.cst_d[:, :], writes=[self.b_cst])
        with contextlib.ExitStack() as pes:
            ct = self.sb("ct", [128, NCH, NSEQ], F32, pes)
            cb = self.sb("cb", [128, NCH, NSEQ], BF16, pes)
            b_ct, b_cb = Buf("ct"), Buf("cb")
            S.dma("sp", ct[:], self.cT_d[:, :, :], writes=[b_ct])
            S.op("act", lambda e: e.activation(out=cb[:], in_=ct[:], func=AF.Silu), reads=[b_ct], writes=[b_cb])
            for l in range(L):
                ps, psb = self.pget()
                pv = ps[:, 0:72 * NSEQ].rearrange("p (c s) -> p c s", s=NSEQ)
                for g in range(18):
                    wt, wb = self.wslot()
                    wv = wt[:, :].rearrange("p (k n) -> p k n", k=8)
                    S.dma("pool", wv, self.ada_w_d[l].rearrange("(k p) n -> p k n", p=128)[:, :, g * 512:(g + 1) * 512],
                          writes=[wb])
                    for jj in range(4):
                        ch = g * 4 + jj
                        self.mm(pv[:, ch, :], [(wv[:, k, jj * 128:(jj + 1) * 128], cb[:, k, :]) for k in range(8)],
                                [wb, b_cb], psb)
                ab = self.ppc("adab", l * 72, 72)
                S.op("dve", lambda e, l=l, pv=pv, ab=ab: e.tensor_tensor(
                    out=self.modT[:, l, :, :], in0=pv, in1=ab.unsqueeze(2).to_broadcast([128, 72, NSEQ]), op=ALU.add),
                    reads=[psb, self.b_pp], writes=[self.b_mod])
            S.barrier()

    def _seq(self, s):
        S = self.S
        for c in range(NCH):
            S.dma("sp", self.xT[:, c, :], self.xT_d[s, :, c, :], writes=self.b_x[c])
        if "A" in self.parts:
            self._rotary(s)
        for l in range(self.L):
            if "ffn" in self.parts:
                self._ffn(l, 0, s)
            if any(p in self.parts for p in "ABC"):
                self._mixer(l, s)
            if "ffn" in self.parts:
                self._ffn(l, 1, s)
        self._final(s)
        S.barrier()

    def _derive(self, l, sub, s, gate_scale):
        S = self.S
        m = self.modT
        sh = m[:, l, (3 * sub) * 8:(3 * sub) * 8 + 8, s]
        sc = m[:, l, (3 * sub + 1) * 8:(3 * sub + 1) * 8 + 8, s]
        gt = m[:, l, (3 * sub + 2) * 8:(3 * sub + 2) * 8 + 8, s]
        ng = self.ppc("ng", (l * 3 + sub) * 8, 8)
        der = self.der
        rd = [self.b_mod, self.b_pp]
        S.op("dve", lambda e: e.scalar_tensor_tensor(out=der[:, 0:8], in0=sc, scalar=1.0, in1=ng, op0=ALU.add, op1=ALU.mult),
             reads=rd, writes=[self.b_der])
        S.op("dve", lambda e: e.tensor_copy(out=der[:, 8:16], in_=sh), reads=rd, writes=[self.b_der])
        S.op("dve", lambda e: e.tensor_scalar(out=der[:, 16:24], in0=gt, scalar1=float(gate_scale), scalar2=None, op0=ALU.mult),
             reads=rd, writes=[self.b_der])

    def _norm_mod(self, g512, gs_ap, sh_ap, dst_fn, dst_bufs, tmp, extra_reads=()):
        S = self.S
        sq, b_sq, lnt, b_ln, rstd, b_rstd, tm, b_tm = tmp
        tok = slice(g512 * 512, (g512 + 1) * 512)
        ps, psb = self.pget()
        for c in range(NCH):
            k = c % 2
            S.op("act", lambda e, c=c, k=k: e.activation(out=sq[k][:, :], in_=self.xT[:, c, tok], func=AF.Square),
                 reads=[self.b_x[c][g512]], writes=[b_sq[k]])
            S.op("pe", lambda e, c=c, k=k: e.matmul(ps[:, :], self.ones, sq[k][:, :], start=(c == 0), stop=(c == NCH - 1)),
                 reads=[b_sq[k], self.b_cst], writes=[psb])
        S.op("act", lambda e: e.activation(out=lnt[:, :], in_=ps[:, :], func=AF.Ln, scale=1.0 / D, bias=self.ppc("eps")),
             reads=[psb, self.b_pp], writes=[b_ln])
        S.op("act", lambda e: e.activation(out=rstd[:, :], in_=lnt[:, :], func=AF.Exp, scale=-0.5),
             reads=[b_ln], writes=[b_rstd])
        for c in range(NCH):
            k = c % 2
            S.op("dve", lambda e, c=c, k=k: e.tensor_tensor(out=tm[k][:, :], in0=self.xT[:, c, tok], in1=rstd[:, :], op=ALU.mult),
                 reads=[self.b_x[c][g512], b_rstd], writes=[b_tm[k]])
            if sh_ap is not None:
                S.op("act", lambda e, c=c, k=k: e.activation(out=dst_fn(c), in_=tm[k][:, :], func=AF.Identity,
                                                             scale=gs_ap[:, c:c + 1], bias=sh_ap[:, c:c + 1]),
                     reads=[b_tm[k], self.b_der, self.b_pp] + list(extra_reads), writes=[dst_bufs[c]])
            else:
                S.op("act", lambda e, c=c, k=k: e.activation(out=dst_fn(c), in_=tm[k][:, :], func=AF.Copy,
                                                             scale=gs_ap[:, c:c + 1]),
                     reads=[b_tm[k], self.b_der, self.b_pp] + list(extra_reads), writes=[dst_bufs[c]])

    def _norm_tmp(self, pes):
        sq = [self.sb("sq%d" % k, [128, 512], BF16, pes) for k in range(2)]
        tm = [self.sb("tm%d" % k, [128, 512], F32, pes) for k in range(2)]
        lnt = self.sb("lnt", [128, 512], F32, pes)
        rstd = self.sb("rstd", [128, 512], F32, pes)
        return (sq, [Buf("sq0"), Buf("sq1")], lnt, Buf("ln"), rstd, Buf("rstd"), tm, [Buf("tm0"), Buf("tm1")])

    def _ffn(self, l, i, s):
        S = self.S
        sub = 0 if i == 0 else 2
        self._derive(l, sub, s, 0.5)
        der = self.der
        with contextlib.ExitStack() as pes:
            hg = self.sb("hg", [128, NCH, 1024], BF16, pes)
            b_hg = [[Buf("hg%d_%d" % (c, t)) for t in range(2)] for c in range(NCH)]
            u = self.sb("u", [128, NJ, 1024], BF16, pes)
            b_u = [[Buf("u%d_%d" % (j, t)) for t in range(2)] for j in range(NJ)]
            sg = [self.sb("sg%d" % k, [128, 512], F32, pes) for k in range(2)]
            b_sg = [Buf("sg0"), Buf("sg1")]
            tmp = self._norm_tmp(pes)
            wi_v = self.wi_d[l, i].rearrange("(k p) n -> p k n", p=128)
            wo_v = self.wo_d[l, i].rearrange("(j p) n -> p j n", p=128)
            kk = 0
            for hf in range(2):
                for t2 in range(2):
                    g512 = hf * 2 + t2
                    self._norm_mod(g512, der[:, 0:8], der[:, 8:16],
                                   lambda c, t2=t2: hg[:, c, t2 * 512:(t2 + 1) * 512], [b_hg[c][t2] for c in range(NCH)], tmp)
                for jg in range(6):
                    ncol = 512 if jg < 5 else 256
                    wg, wgb = self.wslot()
                    wu, wub = self.wslot()
                    wgv = wg[:, 0:8 * ncol].rearrange("p (k n) -> p k n", k=8)
                    wuv = wu[:, 0:8 * ncol].rearrange("p (k n) -> p k n", k=8)
                    S.dma("pool", wgv, wi_v[:, :, jg * 512:jg * 512 + ncol], writes=[wgb])
                    S.dma("pool", wuv, wi_v[:, :, DFF + jg * 512:DFF + jg * 512 + ncol], writes=[wub])
                    for jj in range(ncol // 128):
                        j = jg * 4 + jj
                        for t2 in range(2):
                            tk = slice(t2 * 512, (t2 + 1) * 512)
                            pg, pgb = self.pget()
                            pu, pub = self.pget()
                            rd = [b_hg[c][t2] for c in range(NCH)]
                            self.mm(pg[:, :], [(wgv[:, k, jj * 128:(jj + 1) * 128], hg[:, k, tk]) for k in range(8)], rd + [wgb], pgb)
                            self.mm(pu[:, :], [(wuv[:, k, jj * 128:(jj + 1) * 128], hg[:, k, tk]) for k in range(8)], rd + [wub], pub)
                            q = kk % 2
                            kk += 1
                            S.op("act", lambda e, q=q, pg=pg: e.activation(out=sg[q][:, :], in_=pg[:, :], func=AF.Silu),
                                 reads=[pgb], writes=[b_sg[q]])
                            S.op("dve", lambda e, q=q, pu=pu, j=j, tk=tk: e.tensor_tensor(out=u[:, j, tk], in0=sg[q][:, :], in1=pu[:, :], op=ALU.mult),
                                 reads=[b_sg[q], pub], writes=[b_u[j][t2]])
                for chh in range(2):
                    tiles = []
                    for (j0, nj) in ((0, 8), (8, 8), (16, 6)):
                        wt, wb = self.wslot()
                        wv = wt[:, 0:nj * 512].rearrange("p (j n) -> p j n", j=nj)
                        S.dma("pool", wv, wo_v[:, j0:j0 + nj, chh * 512:(chh + 1) * 512], writes=[wb])
                        tiles.append((wv, wb, j0, nj))
                    for t2 in range(2):
                        g512 = hf * 2 + t2
                        tk = slice(t2 * 512, (t2 + 1) * 512)
                        tok = slice(g512 * 512, (g512 + 1) * 512)
                        for cc in range(4):
                            c = chh * 4 + cc
                            pz, pzb = self.pget()
                            pairs = []
                            for (wv, wb, j0, nj) in tiles:
                                for jj in range(nj):
                                    pairs.append((wv[:, jj, cc * 128:(cc + 1) * 128], u[:, j0 + jj, tk]))
                            self.mm(pz[:, :], pairs, [b_u[j][t2] for j in range(NJ)] + [t[1] for t in tiles], pzb)
                            S.op("dve", lambda e, pz=pz, c=c, tok=tok: e.scalar_tensor_tensor(
                                out=self.xT[:, c, tok], in0=pz[:, :], scalar=der[:, 16 + c:17 + c], in1=self.xT[:, c, tok],
                                op0=ALU.mult, op1=ALU.add), reads=[pzb, self.b_der, self.b_x[c][g512]], writes=[self.b_x[c][g512]])
            S.barrier()

    def _final(self, s):
        S = self.S
        with contextlib.ExitStack() as pes:
            tmp = self._norm_tmp(pes)
            ob = [self.sb("ob%d" % k, [128, 512], F32, pes) for k in range(2)]
            b_ob = [Buf("ob0"), Buf("ob1")]
            fn = self.ppc("fn", 0, 8)
            for g in range(4):
                tok = slice(g * 512, (g + 1) * 512)
                cnt = [0]

                def dst(c):
                    return ob[c % 2][:, :]
                sq, b_sq, lnt, b_ln, rstd, b_rstd, tm, b_tm = tmp
                ps, psb = self.pget()
                for c in range(NCH):
                    k = c % 2
                    S.op("act", lambda e, c=c, k=k: e.activation(out=sq[k][:, :], in_=self.xT[:, c, tok], func=AF.Square),
                         reads=[self.b_x[c][g]], writes=[b_sq[k]])
                    S.op("pe", lambda e, c=c, k=k: e.matmul(ps[:, :], self.ones, sq[k][:, :], start=(c == 0), stop=(c == NCH - 1)),
                         reads=[b_sq[k], self.b_cst], writes=[psb])
                S.op("act", lambda e: e.activation(out=lnt[:, :], in_=ps[:, :], func=AF.Ln, scale=1.0 / D, bias=self.ppc("eps")),
                     reads=[psb, self.b_pp], writes=[b_ln])
                S.op("act", lambda e: e.activation(out=rstd[:, :], in_=lnt[:, :], func=AF.Exp, scale=-0.5),
                     reads=[b_ln], writes=[b_rstd])
                for c in range(NCH):
                    k = c % 2
                    S.op("dve", lambda e, c=c, k=k: e.scalar_tensor_tensor(
                        out=ob[k][:, :], in0=self.xT[:, c, tok], scalar=fn[:, c:c + 1], in1=rstd[:, :], op0=ALU.mult, op1=ALU.mult),
                        reads=[self.b_x[c][g], b_rstd, self.b_pp], writes=[b_ob[k]])
                    S.dma("sp", self.out_d[s, :, c, tok], ob[k][:, :], reads=[b_ob[k]])
            S.barrier()

    def _rotary(self, s):
        S = self.S
        with contextlib.ExitStack() as pes:
            posi = self.sb("posi", [128, S_LEN], I32, pes)
            posf = self.sb("posf", [128, S_LEN], F32, pes)
            u = self.sb("ru", [128, S_LEN], F32, pes)
            ui = self.sb("rui", [128, S_LEN], I32, pes)
            uf = self.sb("ruf", [128, S_LEN], F32, pes)
            b = [Buf("r%d" % i) for i in range(5)]
            S.dma("sp", posi[:, :], self.pos_d[s:s + 1, :].to_broadcast([128, S_LEN]), writes=[b[0]])
            S.op("dve", lambda e: e.tensor_copy(out=posf[:, :], in_=posi[:, :]), reads=[b[0]], writes=[b[1]])
            for dst, add, sgn in ((self.rotS, 0.0, True), (self.rotC, 0.25, False)):
                S.op("dve", lambda e, add=add: e.tensor_scalar(out=u[:, :], in0=posf[:, :], scalar1=self.ppc("invf"), scalar2=float(add),
                                                               op0=ALU.mult, op1=ALU.add), reads=[b[1], self.b_pp], writes=[b[2]])
                S.op("dve", lambda e: e.tensor_copy(out=ui[:, :], in_=u[:, :]), reads=[b[2]], writes=[b[3]])
                S.op("dve", lambda e: e.tensor_copy(out=uf[:, :], in_=ui[:, :]), reads=[b[3]], writes=[b[4]])
                S.op("dve", lambda e: e.tensor_tensor(out=u[:, :], in0=u[:, :], in1=uf[:, :], op=ALU.subtract), reads=[b[2], b[4]], writes=[b[2]])
                S.op("act", lambda e: e.activation(out=uf[:, :], in_=u[:, :], func=AF.Sin, scale=TWO_PI), reads=[b[2]], writes=[b[4]])
                if sgn:
                    S.op("dve", lambda e, dst=dst: e.tensor_scalar(out=dst[:, :], in0=uf[:, :], scalar1=self.ppc("rsign"), scalar2=None, op0=ALU.mult),
                         reads=[b[4], self.b_pp], writes=[self.b_rot])
                else:
                    S.op("dve", lambda e, dst=dst: e.tensor_copy(out=dst[:, :], in_=uf[:, :]), reads=[b[4]], writes=[self.b_rot])
            S.barrier()

    def _mixer(self, l, s):
        S = self.S
        self._derive(l, 1, s, 1.0)
        der = self.der
        win_v = self.win_d[l].rearrange("(k p) n -> p k n", p=128)
        self.win_v = win_v
        with contextlib.ExitStack() as pes:
            hT = self.sb("hT", [128, NCH, S_LEN], BF16, pes)
            mT = self.sb("mT", [128, NCH, S_LEN], BF16, pes)
            b_h = [[Buf("h%d_%d" % (c, g)) for g in range(4)] for c in range(NCH)]
            b_m = [[Buf("m%d_%d" % (c, g)) for g in range(4)] for c in range(NCH)]
            self.hT, self.b_h = hT, b_h
            with contextlib.ExitStack() as p2:
                tmp = self._norm_tmp(p2)
                for g in range(4):
                    self._norm_mod(g, der[:, 0:8], der[:, 8:16], lambda c, g=g: hT[:, c, g * 512:(g + 1) * 512],
                                   [b_h[c][g] for c in range(NCH)], tmp)
                for c in range(NCH):
                    S.dma("sp", self.xs_d[:, c, :], self.xT[:, c, :], reads=self.b_x[c], writes=[self.b_xs])
                S.barrier()
            xf = self.xT[:, :, :].rearrange("p c t -> p (c t)")
            xb = xf.bitcast(BF16)
            self.al_f, self.al_b = xf, xb
            oT = xb[:, 0:8192].rearrange("p (h t) -> p h t", h=4)
            b_o = [[Buf("o%d_%d" % (hp, g)) for g in range(4)] for hp in range(4)]
            first = True
            for bi, name in enumerate("ABC"):
                if name not in self.parts:
                    continue
                with contextlib.ExitStack() as p3:
                    getattr(self, "_mix_" + name)(l, s, oT, b_o, p3)
                    if self.dbg_d is not None and name == "C" and l == 0 and s == 0:
                        S.dma("pool", self.dbg_d[:, :, :], oT, reads=[b for bb in b_o for b in bb])
                    self._branch(l, bi, oT, b_o, mT, b_m, first, p3)
                    S.barrier()
                first = False
            for c in range(NCH):
                S.dma("sp", self.xT[:, c, :], self.xs_d[:, c, :], reads=[self.b_xs], writes=self.b_x[c])
            tiles = []
            wo_v = self.wout_d[l].rearrange("(k p) n -> p k n", p=128)
            for chh in range(2):
                wt, wb = self.wslot()
                wv = wt[:, :].rearrange("p (k n) -> p k n", k=8)
                S.dma("pool", wv, wo_v[:, :, chh * 512:(chh + 1) * 512], writes=[wb])
                tiles.append((wv, wb))
            for g in range(4):
                tok = slice(g * 512, (g + 1) * 512)
                for c in range(NCH):
                    wv, wb = tiles[c // 4]
                    cc = c % 4
                    pz, pzb = self.pget()
                    self.mm(pz[:, :], [(wv[:, k, cc * 128:(cc + 1) * 128], mT[:, k, tok]) for k in range(8)],
                            [b_m[k][g] for k in range(8)] + [wb], pzb)
                    S.op("dve", lambda e, pz=pz, c=c, tok=tok: e.scalar_tensor_tensor(
                        out=self.xT[:, c, tok], in0=pz[:, :], scalar=der[:, 16 + c:17 + c], in1=self.xT[:, c, tok],
                        op0=ALU.mult, op1=ALU.add), reads=[pzb, self.b_der, self.b_x[c][g]], writes=[self.b_x[c][g]])
            S.barrier()

    def _branch(self, l, bi, oT, b_o, mT, b_m, first, pes):
        S = self.S
        hT, b_h = self.hT, self.b_h
        sgt = [self.sb("bsg%d" % k, [128, 512], F32, pes) for k in range(2)]
        b_sgt = [Buf("bsg0"), Buf("bsg1")]
        tmm = [self.sb("btm%d" % k, [128, 512], F32, pes) for k in range(2)]
        b_tmm = [Buf("btm0"), Buf("btm1")]
        wbt, wbb = self.wslot()
        wbv = wbt[:, :].rearrange("p (h n) -> p h n", h=4)
        S.dma("pool", wbv, self.wbr_d[l, bi].rearrange("(h p) n -> p h n", p=128), writes=[wbb])
        kk = 0
        for chh in range(2):
            wt, wb = self.wslot()
            wv = wt[:, :].rearrange("p (k n) -> p k n", k=8)
            c0 = OFF_GZ + bi * 1024 + chh * 512
            S.dma("pool", wv, self.win_v[:, :, c0:c0 + 512], writes=[wb])
            for g in range(4):
                tok = slice(g * 512, (g + 1) * 512)
                for cc in range(4):
                    c = chh * 4 + cc
                    py, pyb = self.pget()
                    pg, pgb = self.pget()
                    self.mm(py[:, :], [(wbv[:, hp, c * 128:(c + 1) * 128], oT[:, hp, tok]) for hp in range(4)],
                            [b_o[hp][g] for hp in range(4)] + [wbb], pyb)
                    self.mm(pg[:, :], [(wv[:, k, cc * 128:(cc + 1) * 128], hT[:, k, tok]) for k in range(8)],
                            [b_h[k][g] for k in range(8)] + [wb], pgb)
                    q = kk % 2
                    kk += 1
                    S.op("act", lambda e, q=q, pg=pg: e.activation(out=sgt[q][:, :], in_=pg[:, :], func=AF.Sigmoid),
                         reads=[pgb], writes=[b_sgt[q]])
                    if first:
                        S.op("dve", lambda e, q=q, py=py, c=c, tok=tok: e.tensor_tensor(out=mT[:, c, tok], in0=py[:, :], in1=sgt[q][:, :], op=ALU.mult),
                             reads=[pyb, b_sgt[q]], writes=[b_m[c][g]])
                    else:
                        S.op("dve", lambda e, q=q, py=py: e.tensor_tensor(out=tmm[q][:, :], in0=py[:, :], in1=sgt[q][:, :], op=ALU.mult),
                             reads=[pyb, b_sgt[q]], writes=[b_tmm[q]])
                        S.op("pool", lambda e, q=q, c=c, tok=tok: e.tensor_tensor(out=mT[:, c, tok], in0=mT[:, c, tok], in1=tmm[q][:, :], op=ALU.add),
                             reads=[b_tmm[q], b_m[c][g]], writes=[b_m[c][g]])

    def _vtm(self, col0, vtm, b_v):
        S = self.S
        hT, b_h = self.hT, self.b_h
        wt, wb = self.wslot()
        wv = wt[:, :].rearrange("p (k n) -> p k n", k=8)
        S.dma("pool", wv, self.win_v[:, :, col0:col0 + 512], writes=[wb])
        for tt in range(16):
            pv, pvb = self.pget()
            g = tt // 4
            self.mm(pv[:, :], [(hT[:, k, tt * 128:(tt + 1) * 128], wv[:, k, :]) for k in range(8)],
                    [b_h[k][g] for k in range(8)] + [wb], pvb)
            S.op("act", lambda e, pv=pv, tt=tt: e.activation(out=vtm[:, tt, :], in_=pv[:, :], func=AF.Copy), reads=[pvb], writes=[b_v[tt]])

    def _attn_core(self, hp, qf, kf, b_q, b_k, vtm, b_v, oT, b_o, blocks_fn, mask_fn, mask_reads, extra_mm, pt, b_pt, rdt, b_rd):
        for hh in range(2):
            self._attn_core_one(hp, hh, qf, kf, b_q, b_k, vtm, b_v, oT, b_o, blocks_fn, mask_fn, mask_reads, extra_mm, pt, b_pt, rdt, b_rd)

    def _attn_core_one(self, hp, hh, qf, kf, b_q, b_k, vtm, b_v, oT, b_o, blocks_fn, mask_fn, mask_reads, extra_mm, pt, b_pt, rdt, b_rd):
        S = self.S
        kk = 0
        if True:
            pb = hh * 64
            h = hp * 2 + hh
            for qg in range(4):
                qs = slice(qg * 512, (qg + 1) * 512)
                pn, pnb = self.pget("a")
                pd, pdb = self.pget("a")
                kts = blocks_fn(qg)
                n = len(kts)
                DK = 2
                stage = {}
                for i in range(n + DK):
                    if i < n:
                        kt = kts[i]
                        ps_s, psb = self.pget()
                        ex = extra_mm(kt, qg) if extra_mm else None
                        S.op("pe", lambda e, ps_s=ps_s, kt=kt: e.matmul(ps_s[:, :], kf[pb:pb + 64, kt * 128:(kt + 1) * 128], qf[pb:pb + 64, qs],
                                                                       start=True, stop=(ex is None)),
                             reads=[b_k[kt // 4], b_q[qg]], writes=[psb])
                        if ex is not None:
                            S.op("pe", lambda e, ps_s=ps_s, ex=ex: e.matmul(ps_s[:, :], ex[0], ex[1], start=False, stop=True),
                                 reads=ex[2], writes=[psb])
                        q = kk % 3
                        kk += 1
                        S.op("act", lambda e, q=q, ps_s=ps_s: e.activation(out=pt[q][:, :], in_=ps_s[:, :], func=AF.Exp, scale=0.125),
                             reads=[psb], writes=[b_pt[q]])
                        eng = "dve" if kk % 2 == 0 else "pool"
                        mk = mask_fn(kt, qg)
                        S.op(eng, lambda e, q=q, mk=mk: e.tensor_tensor(out=pt[3 + q][:, :], in0=pt[q][:, :], in1=mk, op=ALU.mult),
                             reads=[b_pt[q]] + list(mask_reads), writes=[b_pt[3 + q]])
                        stage[i] = (q, kt)
                    j = i - DK
                    if j >= 0:
                        q, kt = stage[j]
                        S.op("pe", lambda e, q=q, kt=kt, pn=pn, j=j: e.matmul(pn[0:64, :], vtm[:, kt, h * 64:(h + 1) * 64], pt[3 + q][:, :],
                                                                           start=(j == 0), stop=(j == n - 1)),
                             reads=[b_v[kt], b_pt[3 + q]], writes=[pnb])
                        S.op("pe", lambda e, q=q, pd=pd, j=j: e.matmul(pd[0:64, :], self.ones[:, 0:64], pt[3 + q][:, :],
                                                                    start=(j == 0), stop=(j == n - 1)),
                             reads=[self.b_cst, b_pt[3 + q]], writes=[pdb])
                S.op("dve", lambda e, pd=pd: e.reciprocal(out=rdt[:, :], in_=pd[0:64, :]), reads=[pdb], writes=[b_rd])
                S.op("dve", lambda e, pn=pn, qs=qs: e.tensor_tensor(out=oT[pb:pb + 64, hp, qs], in0=pn[0:64, :], in1=rdt[:, :], op=ALU.mult),
                     reads=[pnb, b_rd], writes=[b_o[hp][qg]])

    def _al(self, off_bytes, shape, dtype):
        n = int(np.prod(shape[1:]))
        if dtype == BF16:
            v = self.al_b[0:shape[0], off_bytes // 2:off_bytes // 2 + n]
        else:
            v = self.al_f[0:shape[0], off_bytes // 4:off_bytes // 4 + n]
        if len(shape) == 3:
            v = v.rearrange("p (a b) -> p a b", a=shape[1])
        return v

    def _mix_A(self, l, s, oT, b_o, pes):
        S = self.S
        hT, b_h = self.hT, self.b_h
        K = 1024
        vtm = self._al(16 * K, [128, 16, 512], BF16)
        b_v = [Buf("v%d" % t) for t in range(16)]
        qf = self._al(32 * K, [128, S_LEN], BF16)
        kf = self._al(36 * K, [128, S_LEN], BF16)
        ta = self._al(40 * K, [128, 3968], BF16)
        b_ta = Buf("ta")
        pt = [self._al(48 * K + i * K, [128, 512], BF16) for i in range(6)]
        b_pt = [Buf("pt%d" % i) for i in range(6)]
        t1 = [self._al(54 * K + i * 2 * K, [128, 512], F32) for i in range(2)]
        b_t1 = [Buf("t1a"), Buf("t1b")]
        rdt = self._al(58 * K, [64, 512], F32)
        b_rd = Buf("rd")
        b_q = [Buf("q%d" % g) for g in range(4)]
        b_k = [Buf("k%d" % g) for g in range(4)]
        S.dma("pool", ta, self.ta_d[:, :], writes=[b_ta])
        self._vtm(OFF_A_V, vtm, b_v)

        def blocks(qg):
            out = []
            for kt in range(16):
                dl = 128 * kt - 512 * qg
                if dl - 511 > 1024 or dl + 127 < -1024:
                    continue
                out.append(kt)
            return out

        def mask(kt, qg):
            y0 = 1920 - 128 * kt + 512 * qg
            return ta[:, y0:y0 + 512]
        for hp in range(4):
            wt, wb = self.wslot()
            wv = wt[:, :].rearrange("p (k n) -> p k n", k=8)
            S.dma("pool", wv, self.win_v[:, :, OFF_A_QK + hp * 512:OFF_A_QK + (hp + 1) * 512], writes=[wb])
            for g in range(4):
                tok = slice(g * 512, (g + 1) * 512)
                for wi_, (dst, bd) in enumerate(((qf, b_q), (kf, b_k))):
                    p1, p1b = self.pget()
                    p2, p2b = self.pget()
                    rd = [b_h[k][g] for k in range(8)] + [wb]
                    c0 = wi_ * 256
                    self.mm(p1[:, :], [(wv[:, k, c0:c0 + 128], hT[:, k, tok]) for k in range(8)], rd, p1b)
                    self.mm(p2[:, :], [(wv[:, k, c0 + 128:c0 + 256], hT[:, k, tok]) for k in range(8)], rd, p2b)
                    S.op("dve", lambda e, p1=p1, tok=tok: e.tensor_tensor(out=t1[0][:, :], in0=p1[:, :], in1=self.rotC[:, tok], op=ALU.mult),
                         reads=[p1b, self.b_rot], writes=[b_t1[0]])
                    S.op("dve", lambda e, p2=p2, tok=tok: e.tensor_tensor(out=t1[1][:, :], in0=p2[:, :], in1=self.rotS[:, tok], op=ALU.mult),
                         reads=[p2b, self.b_rot], writes=[b_t1[1]])
                    S.op("pool", lambda e, dst=dst, tok=tok: e.tensor_tensor(out=dst[:, tok], in0=t1[0][:, :], in1=t1[1][:, :], op=ALU.add),
                         reads=[b_t1[0], b_t1[1]], writes=[bd[g]])
            self._attn_core(hp, qf, kf, b_q, b_k, vtm, b_v, oT, b_o, blocks, mask, [b_ta], None, pt, b_pt, rdt, b_rd)

    def _mix_B(self, l, s, oT, b_o, pes):
        S = self.S
        hT, b_h = self.hT, self.b_h
        K = 1024
        vtm = self._al(16 * K, [128, 16, 512], BF16)
        b_v = [Buf("v%d" % t) for t in range(16)]
        qf = self._al(32 * K, [128, S_LEN], BF16)
        kf = self._al(36 * K, [128, S_LEN], BF16)
        bu = self._al(40 * K, [128, S_LEN], BF16)
        bw = self._al(44 * K, [128, S_LEN], BF16)
        b_bb = Buf("bb")
        pt = [self._al(48 * K + i * K, [128, 512], BF16) for i in range(6)]
        b_pt = [Buf("pt%d" % i) for i in range(6)]
        eraw = self._al(54 * K, [128, 1920], F32)
        b_er = Buf("er")
        rdt = self._al(62 * K, [64, 512], F32)
        b_rd = Buf("rd")
        eb = self.sb("eb", [128, 1920], BF16, pes)
        b_eb = Buf("eb")
        b_q = [Buf("q%d" % g) for g in range(4)]
        b_k = [Buf("k%d" % g) for g in range(4)]
        S.dma("pool", bu[0:32, :], self.bu_d[:, :], writes=[b_bb])
        S.dma("pool", bw[0:32, :], self.bw_d[:, :], writes=[b_bb])
        self._vtm(OFF_B_V, vtm, b_v)
        rows = 32
        rs = np.clip(np.arange(rows) - 4, 0, rows - 8)

        def blocks(qg):
            a = 8 * qg
            out = []
            for i in range(16):
                ok = False
                for qr in range(a, a + 8):
                    for kr in (2 * i, 2 * i + 1):
                        if rs[qr] <= kr < rs[qr] + 8:
                            ok = True
                if ok:
                    j0 = a - 2 * i + 14
                    assert 0 <= j0 <= 22, (a, i, j0)
                    out.append(i)
            return out

        def mask(i, qg):
            j0 = 8 * qg - 2 * i + 14
            return eb[:, j0 * 64:j0 * 64 + 512]

        def extra(i, qg):
            return (bu[0:32, i * 128:(i + 1) * 128], bw[0:32, qg * 512:(qg + 1) * 512], [b_bb])
        for hp in range(4):
            wt, wb = self.wslot()
            wv = wt[:, 0:2048].rearrange("p (k n) -> p k n", k=8)
            S.dma("pool", wv, self.win_v[:, :, OFF_B_QK + hp * 256:OFF_B_QK + (hp + 1) * 256], writes=[wb])
            for g in range(4):
                tok = slice(g * 512, (g + 1) * 512)
                for wi_, (dst, bd) in enumerate(((qf, b_q), (kf, b_k))):
                    p1, p1b = self.pget()
                    rd = [b_h[k][g] for k in range(8)] + [wb]
                    self.mm(p1[:, :], [(wv[:, k, wi_ * 128:(wi_ + 1) * 128], hT[:, k, tok]) for k in range(8)], rd, p1b)
                    S.op("act", lambda e, p1=p1, dst=dst, tok=tok: e.activation(out=dst[:, tok], in_=p1[:, :], func=AF.Copy),
                         reads=[p1b], writes=[bd[g]])
            for hh in range(2):
                pass
            self._attn_core_B(l, hp, qf, kf, b_q, b_k, vtm, b_v, oT, b_o, blocks, mask, extra, pt, b_pt, rdt, b_rd, eraw, b_er, eb, b_eb)

    def _attn_core_B(self, l, hp, qf, kf, b_q, b_k, vtm, b_v, oT, b_o, blocks, mask, extra, pt, b_pt, rdt, b_rd, eraw, b_er, eb, b_eb):
        S = self.S
        for hh in range(2):
            h = hp * 2 + hh
            S.dma("sp", eraw[0:64, :], self.braw_d[l, h, :, 64:31 * 64], writes=[b_er])
            S.dma("sp", eraw[64:128, :], self.braw_d[l, h, :, 0:30 * 64], writes=[b_er])
            S.op("act", lambda e: e.activation(out=eb[:, :], in_=eraw[:, :], func=AF.Exp), reads=[b_er], writes=[b_eb])
            self._attn_core_one(hp, hh, qf, kf, b_q, b_k, vtm, b_v, oT, b_o, blocks, mask, [b_eb], extra, pt, b_pt, rdt, b_rd)

    def _mix_C(self, l, s, oT, b_o, pes):
        S = self.S
        hT, b_h = self.hT, self.b_h
        K = 1024
        cdec = DECAY_SCALE
        cur = [16 * K]

        def T(shape, dt, name):
            n = int(np.prod(shape[1:])) * (2 if dt == BF16 else 4)
            off = cur[0]
            cur[0] += n
            assert cur[0] <= 64 * K, cur[0]
            return self._al(off, [128] + list(shape[1:]), dt), Buf(name)
        B = [128, 512]
        rS, b_rS = T(B, F32, "rS")
        kS, b_kS = T(B, F32, "kS")
        sgm, b_sgm = T(B, F32, "sgm")
        ai, b_ai = T(B, F32, "ai")
        pre, b_pre = T(B, F32, "pre")
        ein, b_ein = T(B, F32, "ein")
        eex, b_eex = T(B, F32, "eex")
        rem, b_rem = T(B, F32, "rem")
        kkn, b_kkn = T(B, F32, "kkn")
        kmod, b_kmod = T(B, F32, "kmod")
        kb, b_kb = T(B, F32, "kb")
        t0, b_t0 = T(B, F32, "t0")
        t1, b_t1 = T(B, F32, "t1")
        cm, b_cm = self.sb("cm", B, F32, pes), Buf("cm")
        acc, b_acc = T([128, S_LEN], BF16, "acc")
        th, b_th = T(B, BF16, "th")
        xa, b_xa = T(B, BF16, "xa")
        aF, b_aF = T(B, BF16, "aF")
        bF, b_bF = T(B, BF16, "bF")
        kF, b_kF = T(B, BF16, "kF")
        rF, b_rF = T(B, BF16, "rF")
        bgF, b_bgF = T(B, BF16, "bgF")
        kgF, b_kgF = T(B, BF16, "kgF")
        vF, b_vF = T(B, BF16, "vF")
        sqb, b_sqb = T(B, BF16, "sqb")
        vT, b_vT = T([128, 4, 128], BF16, "vT")
        bgT, b_bgT = T([128, 4, 128], BF16, "bgT")
        kgT, b_kgT = T([128, 4, 128], BF16, "kgT")
        sq = [T([128, 128], BF16, "sqm%d" % i) for i in range(4)]
        AkT, b_AkT = T([128, 128], BF16, "AkT")
        MrbT, b_MrbT = T([128, 128], BF16, "MrbT")
        MrkT, b_MrkT = T([128, 128], BF16, "MrkT")
        ynT, b_ynT = T([128, 128], BF16, "ynT")
        X = [T([128, 64], BF16, "X%d" % i) for i in range(2)]
        H, b_H = T([128, 64], BF16, "H")
        tH, b_tH = T([128, 64], F32, "tH")
        gC, b_gC = T([128, 4], F32, "gC")
        st6, b_st6 = T([128, 6], F32, "st6")
        mv, b_mv = T([128, 2], F32, "mv")
        murow, b_mu = self.sb("murow", B, F32, pes), Buf("murow")
        gupt, b_gup = T(B, BF16, "gupt")
        b_Hh = [Buf("H0"), Buf("H1")]
        b_tHh = [Buf("tH0"), Buf("tH1")]
        self.ps_g = self.ps_all
        self.ps_rr["g"] = 0
        TS0 = (sq, AkT, b_AkT, MrbT, b_MrbT, MrkT, b_MrkT, X, st6, b_st6, mv, b_mv)
        t2 = self.sb("c2m", [128, 7, 128], BF16, pes)
        t2x = self.sb("c2x", [128, 2, 64], BF16, pes)
        t2s = self.sb("c2s", [128, 8], F32, pes)
        TS1 = ([(t2[:, i, :], Buf("sq2_%d" % i)) for i in range(4)], t2[:, 4, :], Buf("AkT2"), t2[:, 5, :], Buf("MrbT2"),
               t2[:, 6, :], Buf("MrkT2"), [(t2x[:, i, :], Buf("X2_%d" % i)) for i in range(2)], t2s[:, 0:6], Buf("st62"),
               t2s[:, 6:8], Buf("mv2"))
        cst = self.cst
        LT, LE, GT, GE, BLK = (cst[:, 256:384], cst[:, 384:512], cst[:, 512:640], cst[:, 640:768], cst[:, 768:896])
        S.dma("sp", cm[:, :], self.cst2_d[:, :], writes=[b_cm])
        S.dma("pool", gupt, self.gup_d[l], writes=[b_gup])
        wlt, b_wl = self.sb("wlt", [128, 2048], BF16, pes), Buf("wl")
        wl = wlt[0:64, 0:2048].rearrange("p (a n) -> p a n", a=4)
        for d in range(2):
            S.dma("pool", wl[:, d * 2, :], self.wup_d[l, d], writes=[b_wl])
            S.dma("pool", wl[:, d * 2 + 1, :], self.aup_d[l, d], writes=[b_wl])
        wgt, b_wg = self.sb("wgt", [128, 1024], BF16, pes), Buf("wg")
        wg = wgt[:, 0:1024].rearrange("p (k n) -> p k n", k=8)
        S.dma("pool", wg, self.win_v[:, :, OFF_C_G:OFF_C_G + 128], writes=[b_wg])
        mode = getattr(self, "cmode", None)
        for hp in range(4):
            for d in range(2):
                if mode in ("acc_d0", "yn_d0", "bonus_d0") and d == 1:
                    continue
                if mode == "acc_d1" and d == 0:
                    continue
                wrt, b_wr = self.wslot()
                w1t, b_w1 = self.wslot()
                wr = wrt[:, :].rearrange("p (k n) -> p k n", k=8)
                w1 = w1t[:, :].rearrange("p (k n) -> p k n", k=8)
                for j, c0 in enumerate((OFF_C_R + hp * 128, OFF_C_K + hp * 128, OFF_C_V + hp * 128)):
                    S.dma("pool", wr[:, :, j * 128:(j + 1) * 128], self.win_v[:, :, c0:c0 + 128], writes=[b_wr])
                    S.dma("sp", murow[:, j * 128:(j + 1) * 128],
                          self.murkv_d[l, d, j:j + 1, hp * 128:(hp + 1) * 128].to_broadcast([128, 128]), writes=[b_mu])
                S.dma("pool", wr[:, :, 384:448], self.win_v[:, :, OFF_C_LW + d * 64:OFF_C_LW + d * 64 + 64], writes=[b_wr])
                S.dma("pool", wr[:, :, 448:512], self.win_v[:, :, OFF_C_LA + d * 64:OFF_C_LA + d * 64 + 64], writes=[b_wr])
                S.dma("sp", murow[:, 384:448], self.muw_d[l, d:d + 1, :].to_broadcast([128, 64]), writes=[b_mu])
                S.dma("sp", murow[:, 448:512], self.mua_d[l, d:d + 1, :].to_broadcast([128, 64]), writes=[b_mu])
                mub = murow[:, :].unsqueeze(1).to_broadcast([128, 8, 512])
                S.op("dve", lambda e, w1=w1, wr=wr, mub=mub: e.tensor_tensor(out=w1, in0=wr, in1=mub, op=ALU.mult), reads=[b_wr, b_mu], writes=[b_w1])
                S.op("dve", lambda e, w1=w1, wr=wr: e.tensor_tensor(out=wr, in0=wr, in1=w1, op=ALU.subtract), reads=[b_wr, b_w1], writes=[b_wr])
                S.op("pool", lambda e: e.memset(H, 0.0), writes=b_Hh)
                strict_st, strict_ts, incl_st = (LT, GT, LE) if d == 0 else (GT, LT, GE)
                w0c = self.ppc("w0", (l * 2 + d) * 4 + hp)
                a0c = self.ppc("a0", (l * 2 + d) * 4 + hp)
                for bi_ in range(4):
                    bi = bi_ if d == 0 else 3 - bi_
                    t0_ = bi * 512
                    tok = slice(t0_, t0_ + 512)
                    if d == 0:
                        lo, hi = (1 if bi == 0 else 0), 512
                        nb = slice(t0_ + lo - 1, t0_ + 511)
                    else:
                        lo, hi = 0, (511 if bi == 3 else 512)
                        nb = slice(t0_ + 1, t0_ + hi + 1)
                    rdh = [b_h[k][bi] for k in range(8)]
                    if d == 0 and bi > 0:
                        rdh += [b_h[k][bi - 1] for k in range(8)]
                    if d == 1 and bi < 3:
                        rdh += [b_h[k][bi + 1] for k in range(8)]

                    def proj(c0, n, ps, psb):
                        m = ps[0:n, :]
                        for k in range(8):
                            S.op("pe", lambda e, k=k: e.matmul(m, wr[:, k, c0:c0 + n], hT[:, k, tok], start=(k == 0), stop=False),
                                 reads=rdh + [b_wr], writes=[psb])
                        for k in range(8):
                            S.op("pe", lambda e, k=k: e.matmul(ps[0:n, lo:hi], w1[:, k, c0:c0 + n], hT[:, k, nb], start=False, stop=(k == 7)),
                                 reads=rdh + [b_w1], writes=[psb])
                    ps, psb = self.pget()
                    proj(0, 128, ps, psb)
                    S.op("act", lambda e, ps=ps: e.activation(out=rS, in_=ps[:, :], func=AF.Copy), reads=[psb], writes=[b_rS])
                    ps, psb = self.pget()
                    proj(128, 128, ps, psb)
                    S.op("act", lambda e, ps=ps: e.activation(out=kS, in_=ps[:, :], func=AF.Copy), reads=[psb], writes=[b_kS])
                    ps, psb = self.pget()
                    proj(256, 128, ps, psb)
                    S.op("act", lambda e, ps=ps: e.activation(out=vF, in_=ps[:, :], func=AF.Copy), reads=[psb], writes=[b_vF])
                    ps, psb = self.pget()
                    proj(384, 64, ps, psb)
                    S.op("act", lambda e, ps=ps: e.activation(out=th[0:64, :], in_=ps[0:64, :], func=AF.Tanh), reads=[psb], writes=[b_th])
                    ps, psb = self.pget()
                    proj(448, 64, ps, psb)
                    S.op("act", lambda e, ps=ps: e.activation(out=xa[0:64, :], in_=ps[0:64, :], func=AF.Copy), reads=[psb], writes=[b_xa])
                    ps, psb = self.pget()
                    self.mm(ps[:, :], [(wl[:, d * 2, hp * 128:(hp + 1) * 128], th[0:64, :])], [b_wl, b_th], psb)
                    S.op("act", lambda e, ps=ps: e.activation(out=sgm, in_=ps[:, :], func=AF.Sigmoid, bias=w0c), reads=[psb, self.b_pp], writes=[b_sgm])
                    ps, psb = self.pget()
                    self.mm(ps[:, :], [(wl[:, d * 2 + 1, hp * 128:(hp + 1) * 128], xa[0:64, :])], [b_wl, b_xa], psb)
                    S.op("act", lambda e, ps=ps: e.activation(out=ai, in_=ps[:, :], func=AF.Sigmoid, bias=a0c), reads=[psb, self.b_pp], writes=[b_ai])
                    S.op("dve", lambda e: e.tensor_tensor_scan(out=pre, data0=cm[:, :], data1=sgm, initial=0.0, op0=ALU.mult, op1=ALU.add),
                         reads=[b_cm, b_sgm], writes=[b_pre])
                    pre3 = pre.rearrange("p (a b) -> p a b", a=4)
                    etb = pre3[:, :, 127:128].to_broadcast([128, 4, 128])
                    v3 = lambda x: x.rearrange("p (a b) -> p a b", a=4)
                    if d == 0:
                        S.op("dve", lambda e: e.tensor_copy(out=ein, in_=pre), reads=[b_pre], writes=[b_ein])
                        S.op("dve", lambda e: e.tensor_tensor(out=eex, in0=pre, in1=sgm, op=ALU.subtract), reads=[b_pre, b_sgm], writes=[b_eex])
                        S.op("dve", lambda e: e.tensor_tensor(out=v3(rem), in0=etb, in1=pre3, op=ALU.subtract), reads=[b_pre], writes=[b_rem])
                    else:
                        S.op("dve", lambda e: e.tensor_tensor(out=v3(eex), in0=etb, in1=pre3, op=ALU.subtract), reads=[b_pre], writes=[b_eex])
                        S.op("dve", lambda e: e.tensor_tensor(out=ein, in0=eex, in1=sgm, op=ALU.add), reads=[b_eex, b_sgm], writes=[b_ein])
                        S.op("dve", lambda e: e.tensor_tensor(out=rem, in0=pre, in1=sgm, op=ALU.subtract), reads=[b_pre, b_sgm], writes=[b_rem])
                    S.op("act", lambda e: e.activation(out=gC, in_=pre3[:, :, 127], func=AF.Exp, scale=-cdec), reads=[b_pre], writes=[b_gC])
                    S.op("dve", lambda e: e.tensor_scalar(out=t0, in0=kS, scalar1=self.ppc("k_k", l * 4 + hp), scalar2=None, op0=ALU.mult),
                         reads=[b_kS, self.b_pp], writes=[b_t0])
                    S.op("act", lambda e: e.activation(out=sqb, in_=t0, func=AF.Square), reads=[b_t0], writes=[b_sqb])
                    ps, psb = self.pget()
                    self.mm(ps[:, :], [(BLK, sqb)], [self.b_cst, b_sqb], psb)
                    S.op("act", lambda e, ps=ps: e.activation(out=t1, in_=ps[:, :], func=AF.Ln, bias=self.ppc("eps")), reads=[psb, self.b_pp], writes=[b_t1])
                    S.op("act", lambda e: e.activation(out=t1, in_=t1, func=AF.Exp, scale=-0.5), reads=[b_t1], writes=[b_t1])
                    S.op("dve", lambda e: e.tensor_tensor(out=kkn, in0=t0, in1=t1, op=ALU.mult), reads=[b_t0, b_t1], writes=[b_kkn])
                    S.op("dve", lambda e: e.tensor_scalar(out=t0, in0=ai, scalar1=-1.0, scalar2=self.ppc("k_a", l * 4 + hp), op0=ALU.add, op1=ALU.mult),
                         reads=[b_ai, self.b_pp], writes=[b_t0])
                    S.op("dve", lambda e: e.scalar_tensor_tensor(out=kmod, in0=t0, scalar=1.0, in1=kS, op0=ALU.add, op1=ALU.mult),
                         reads=[b_t0, b_kS], writes=[b_kmod])
                    S.op("dve", lambda e: e.tensor_tensor(out=kb, in0=kkn, in1=ai, op=ALU.mult), reads=[b_kkn, b_ai], writes=[b_kb])
                    S.op("act", lambda e: e.activation(out=t1, in_=eex, func=AF.Exp, scale=-cdec), reads=[b_eex], writes=[b_t1])
                    S.op("dve", lambda e: e.scalar_tensor_tensor(out=aF, in0=kkn, scalar=-1.0, in1=t1, op0=ALU.mult, op1=ALU.mult),
                         reads=[b_kkn, b_t1], writes=[b_aF])
                    S.op("act", lambda e: e.activation(out=t1, in_=ein, func=AF.Exp, scale=cdec), reads=[b_ein], writes=[b_t1])
                    S.op("dve", lambda e: e.tensor_tensor(out=bF, in0=kb, in1=t1, op=ALU.mult), reads=[b_kb, b_t1], writes=[b_bF])
                    S.op("dve", lambda e: e.tensor_tensor(out=kF, in0=kmod, in1=t1, op=ALU.mult), reads=[b_kmod, b_t1], writes=[b_kF])
                    S.op("act", lambda e: e.activation(out=t1, in_=ein, func=AF.Exp, scale=-cdec), reads=[b_ein], writes=[b_t1])
                    S.op("dve", lambda e: e.tensor_tensor(out=rF, in0=rS, in1=t1, op=ALU.mult), reads=[b_rS, b_t1], writes=[b_rF])
                    S.op("act", lambda e: e.activation(out=t1, in_=rem, func=AF.Exp, scale=-cdec), reads=[b_rem], writes=[b_t1])
                    S.op("dve", lambda e: e.tensor_tensor(out=bgF, in0=kb, in1=t1, op=ALU.mult), reads=[b_kb, b_t1], writes=[b_bgF])
                    S.op("dve", lambda e: e.tensor_tensor(out=kgF, in0=kmod, in1=t1, op=ALU.mult), reads=[b_kmod, b_t1], writes=[b_kgF])
                    S.op("dve", lambda e: e.scalar_tensor_tensor(out=sqb, in0=rS, scalar=self.ppc("r_k", l * 4 + hp), in1=kmod, op0=ALU.mult, op1=ALU.mult),
                         reads=[b_rS, b_kmod, self.b_pp], writes=[b_sqb])
                    ps, psb = self.pget()
                    self.mm(ps[:, :], [(BLK, sqb)], [self.b_cst, b_sqb], psb)
                    S.op("dve", lambda e, ps=ps: e.tensor_tensor(out=kb, in0=ps[:, :], in1=vF, op=ALU.mult), reads=[psb, b_vF, b_bgF], writes=[b_kb])
                    bonus, b_bonus = kb, b_kb
                    for cb in range(4):
                        cs = slice(cb * 128, (cb + 1) * 128)
                        for src, bs, dst, bd in ((vF, b_vF, vT, b_vT), (bgF, b_bgF, bgT, b_bgT), (kgF, b_kgF, kgT, b_kgT)):
                            ps, psb = self.pget()
                            self.mm(ps[:, 0:128], [(src[:, cs], self.ident)], [bs, self.b_cst], psb)
                            S.op("act", lambda e, ps=ps, dst=dst, cb=cb: e.activation(out=dst[:, cb, :], in_=ps[:, 0:128], func=AF.Copy), reads=[psb], writes=[bd])
                    for cb_ in range(4):
                        cb = cb_ if d == 0 else 3 - cb_
                        cs = slice(cb * 128, (cb + 1) * 128)
                        def chain(hh, TS):
                            sq_, AkT, b_AkT, MrbT, b_MrbT, MrkT, b_MrkT, X_, st6, b_st6, mv, b_mv = TS
                            ph = slice(hh * 64, hh * 64 + 64)
                            hc = slice(hh * 64, hh * 64 + 64)
                            (Nm, b_N), (Am, b_A), (N2, b_N2), (A2, b_A2) = sq_

                            def prod(dst, bd, lhs, bl, rhs, br, mask):
                                ps, psb = self.pget()
                                self.mm(ps[:, 0:128], [(lhs[ph, cs], rhs[ph, cs])], [bl, br], psb)
                                S.op("dve", lambda e, ps=ps: e.tensor_tensor(out=dst, in0=ps[:, 0:128], in1=mask, op=ALU.mult), reads=[psb, self.b_cst], writes=[bd])
                            prod(Nm, b_N, bF, b_bF, aF, b_aF, strict_st)
                            prod(Am, b_A, aF, b_aF, bF, b_bF, strict_ts)
                            yield
                            prod(AkT, b_AkT, kF, b_kF, aF, b_aF, strict_st)
                            prod(MrbT, b_MrbT, bF, b_bF, rF, b_rF, incl_st)
                            prod(MrkT, b_MrkT, kF, b_kF, rF, b_rF, incl_st)
                            yield
                            ps, psb = self.pget()
                            self.mm(ps[:, 0:64], [(aF[ph, cs], H[ph, :]), (AkT, vT[:, cb, hc])], [b_aF, b_Hh[hh], b_AkT, b_vT], psb)
                            (X0, b_X0), (X1, b_X1) = X_
                            S.op("act", lambda e, ps=ps: e.activation(out=X0, in_=ps[:, 0:64], func=AF.Copy), reads=[psb], writes=[b_X0])
                            yield
                            cur_ = (Nm, b_N, Am, b_A)
                            oth_ = (N2, b_N2, A2, b_A2)
                            xs = [(X0, b_X0), (X1, b_X1)]
                            for j in range(7):
                                Nj, bNj, Aj, bAj = cur_
                                xin, bxin = xs[j % 2]
                                xout, bxout = xs[(j + 1) % 2]
                                ps, psb = self.pget()
                                self.mm(ps[:, 0:64], [(Nj, xin)], [bNj, bxin], psb)
                                S.op("dve", lambda e, ps=ps, xin=xin, xout=xout: e.tensor_tensor(out=xout, in0=ps[:, 0:64], in1=xin, op=ALU.add),
                                     reads=[psb, bxin], writes=[bxout])
                                if j < 6:
                                    Nn, bNn, An, bAn = oth_
                                    ps, psb = self.pget()
                                    self.mm(ps[:, 0:128], [(Aj, Nj)], [bNj, bAj], psb)
                                    S.op("act", lambda e, ps=ps, Nn=Nn: e.activation(out=Nn, in_=ps[:, 0:128], func=AF.Copy), reads=[psb], writes=[bNn])
                                    ps, psb = self.pget()
                                    self.mm(ps[:, 0:128], [(Nj, Aj)], [bNj, bAj], psb)
                                    S.op("act", lambda e, ps=ps, An=An: e.activation(out=An, in_=ps[:, 0:128], func=AF.Copy), reads=[psb], writes=[bAn])
                                    cur_, oth_ = oth_, cur_
                                yield
                            U, b_U = xs[7 % 2]
                            ps, psb = self.pget()
                            self.mm(ps[:, 0:64], [(rF[ph, cs], H[ph, :]), (MrbT, U), (MrkT, vT[:, cb, hc])],
                                    [b_rF, b_Hh[hh], b_MrbT, b_U, b_MrkT, b_vT], psb)
                            S.op("dve", lambda e, ps=ps: e.bn_stats(out=st6, in_=ps[:, 0:64]), reads=[psb], writes=[b_st6])
                            S.op("dve", lambda e: e.bn_aggr(out=mv, in_=st6), reads=[b_st6], writes=[b_mv])
                            S.op("act", lambda e: e.activation(out=mv[:, 1:2], in_=mv[:, 1:2], func=AF.Ln, bias=self.ppc("gneps")), reads=[b_mv, self.b_pp], writes=[b_mv])
                            S.op("act", lambda e: e.activation(out=mv[:, 1:2], in_=mv[:, 1:2], func=AF.Exp, scale=-0.5), reads=[b_mv], writes=[b_mv])
                            S.op("dve", lambda e, ps=ps, hc=hc: e.tensor_scalar(out=ynT[:, hc], in0=ps[:, 0:64], scalar1=mv[:, 0:1], scalar2=mv[:, 1:2],
                                                                              op0=ALU.subtract, op1=ALU.mult), reads=[psb, b_mv], writes=[b_ynT])
                            yield
                            ps, psb = self.pget()
                            self.mm(ps[0:64, 0:64], [(bgT[:, cb, hc], U), (kgT[:, cb, hc], vT[:, cb, hc])], [b_bgT, b_U, b_kgT, b_vT], psb)
                            S.op("act", lambda e, ps=ps, ph=ph: e.activation(out=tH[ph, :], in_=ps[0:64, 0:64], func=AF.Copy), reads=[psb], writes=[b_tHh[hh]])
                            S.op("dve", lambda e, ph=ph, cb=cb: e.scalar_tensor_tensor(out=H[ph, :], in0=H[ph, :], scalar=gC[ph, cb:cb + 1], in1=tH[ph, :],
                                                                                    op0=ALU.mult, op1=ALU.add), reads=[b_Hh[hh], b_gC, b_tHh[hh]], writes=[b_Hh[hh]])

                        gens = [chain(0, TS0), chain(1, TS1)]
                        alive = [True, True]
                        while any(alive):
                            for gi in range(2):
                                if alive[gi]:
                                    try:
                                        next(gens[gi])
                                    except StopIteration:
                                        alive[gi] = False
                        ps, psb = self.pget()
                        self.mm(ps[:, 0:128], [(ynT, self.ident)], [b_ynT, self.b_cst], psb)
                        S.op("act", lambda e, ps=ps, cs=cs: e.activation(out=t0[:, cs], in_=ps[:, 0:128], func=AF.Identity,
                                                                        scale=self.ppc("gn_w", l * 4 + hp), bias=self.ppc("gn_b", l * 4 + hp)),
                             reads=[psb, self.b_pp], writes=[b_t0])
                    if mode == "yn_d0":
                        S.op("dve", lambda e, tok=tok: e.tensor_copy(out=acc[:, tok], in_=t0), reads=[b_t0], writes=[b_acc])
                    elif mode == "bonus_d0":
                        S.op("dve", lambda e, tok=tok: e.tensor_copy(out=acc[:, tok], in_=bonus), reads=[b_bonus], writes=[b_acc])
                    elif d == 0 or mode == "acc_d1":
                        S.op("dve", lambda e, tok=tok: e.tensor_tensor(out=acc[:, tok], in0=t0, in1=bonus, op=ALU.add), reads=[b_t0, b_bonus], writes=[b_acc])
                    else:
                        S.op("dve", lambda e: e.tensor_tensor(out=t0, in0=t0, in1=bonus, op=ALU.add), reads=[b_t0, b_bonus], writes=[b_t0])
                        S.op("pool", lambda e, tok=tok: e.tensor_tensor(out=acc[:, tok], in0=acc[:, tok], in1=t0, op=ALU.add), reads=[b_t0, b_acc], writes=[b_acc])
            for g in range(4):
                tok = slice(g * 512, (g + 1) * 512)
                ps, psb = self.pget()
                self.mm(ps[:, :], [(wg[:, k, :], hT[:, k, tok]) for k in range(8)], [b_h[k][g] for k in range(8)] + [b_wg], psb)
                S.op("act", lambda e, ps=ps: e.activation(out=sqb, in_=ps[:, :], func=AF.Sigmoid), reads=[psb], writes=[b_sqb])
                ps, psb = self.pget()
                self.mm(ps[:, :], [(gupt[:, hp * 128:(hp + 1) * 128], sqb)], [b_gup, b_sqb], psb)
                if mode == "gate":
                    S.op("dve", lambda e, ps=ps, tok=tok: e.tensor_copy(out=oT[:, hp, tok], in_=ps[:, :]), reads=[psb], writes=[b_o[hp][g]])
                elif mode is not None:
                    S.op("dve", lambda e, ps=ps, tok=tok: e.tensor_copy(out=oT[:, hp, tok], in_=acc[:, tok]), reads=[psb, b_acc], writes=[b_o[hp][g]])
                else:
                    S.op("dve", lambda e, ps=ps, tok=tok: e.tensor_tensor(out=oT[:, hp, tok], in0=ps[:, :], in1=acc[:, tok], op=ALU.mult),
                         reads=[psb, b_acc], writes=[b_o[hp][g]])
        self.ps_g = self.ps_all[0:4]
        self.ps_rr["g"] = 0


def _consts():
    cst = np.zeros((128, 1024), np.float32)
    cst[:, 0:128] = np.eye(128, dtype=np.float32)
    cst[:, 128:256] = 1.0
    p = np.arange(128)[:, None]
    f = np.arange(128)[None, :]
    cst[:, 256:384] = (p < f)
    cst[:, 384:512] = (p <= f)
    cst[:, 512:640] = (p > f)
    cst[:, 640:768] = (p >= f)
    cst[:, 768:896] = ((p // 64) == (f // 64))
    return cst


def _consts2():
    c2 = np.ones((128, 512), np.float32)
    c2[:, 0::128] = 0.0
    return c2


def _pp_static(pp, off):
    d = np.arange(128) % 64
    invf = np.where(d < 16, 500000.0 ** (-((d % 8) * 2.0) / 16.0), 0.0)
    pp[:, off["invf"]] = (invf / TWO_PI).astype(np.float32)
    pp[:, off["rsign"]] = np.where(d < 8, -1.0, np.where(d < 16, 1.0, 0.0))
    pp[:, off["eps"]] = RMS_EPS
    pp[:, off["one"]] = 1.0
    pp[:, off["gneps"]] = GN_EPS


def build_w_in_ext(w_in):
    L = w_in.shape[0]
    cols = []
    base = {"aq": 0, "ak": 512, "av": 1024, "nq": 1536, "nk": 2048, "nv": 2560, "pr": 3072, "pk": 3584, "pv": 4096,
            "pwf": 4608, "pwb": 4672, "paf": 4736, "pab": 4800, "pg": 4864, "gz0": 4992, "gz1": 6016, "gz2": 7040}
    perm64 = np.arange(64)
    perm64[0:8] = np.arange(8, 16)
    perm64[8:16] = np.arange(0, 8)
    for hp in range(4):
        idx = np.arange(hp * 128, (hp + 1) * 128)
        pidx = np.concatenate([hp * 128 + perm64, hp * 128 + 64 + perm64])
        cols += [base["aq"] + idx, base["aq"] + pidx, base["ak"] + idx, base["ak"] + pidx]
    cols.append(base["av"] + np.arange(512))
    for hp in range(4):
        idx = np.arange(hp * 128, (hp + 1) * 128)
        cols += [base["nq"] + idx, base["nk"] + idx]
    cols.append(base["nv"] + np.arange(512))
    cols.append(base["pr"] + np.arange(512))
    cols.append(base["pk"] + np.arange(512))
    cols.append(base["pv"] + np.arange(512))
    cols.append(base["pwf"] + np.arange(128))
    cols.append(base["paf"] + np.arange(128))
    cols.append(base["pg"] + np.arange(128))
    cols.append(base["gz0"] + np.arange(3072))
    cols = np.concatenate(cols)
    assert cols.shape[0] == NCOL
    return np.ascontiguousarray(w_in[:, :, cols])


def chunk_cols(v):
    sh = v.shape
    n = sh[-1] // 128
    return np.moveaxis(v.reshape(sh[:-1] + (n, 128)), -1, 0)


def prep_shared(inp, L):
    off, NPP = pp_layout(L)
    pp = np.zeros((128, NPP), np.float32)
    _pp_static(pp, off)

    def put(name, arr):
        arr = np.asarray(arr, np.float32).reshape(128, -1)
        pp[:, off[name]:off[name] + arr.shape[1]] = arr
    put("ng", chunk_cols(inp["norm_gains"][:L]))
    put("fn", chunk_cols(inp["final_norm"]))
    put("adab", chunk_cols(inp["ada_b"][:L]))
    put("mu_rkv", chunk_cols(inp["mu_rkv"][:L]))
    put("w0", chunk_cols(inp["w0"][:L]))
    put("a0", chunk_cols(inp["a0"][:L]))
    for nm in ("k_k", "k_a", "gn_w", "gn_b"):
        put(nm, chunk_cols(inp[nm][:L]))
    put("r_k", chunk_cols(inp["r_k"][:L].reshape(L, 512)))
    put("mu_w", np.moveaxis(inp["mu_w"][:L].reshape(L, 128), -1, 0))
    put("mu_a", np.moveaxis(inp["mu_a"][:L].reshape(L, 128), -1, 0))
    sh = {"pp": pp, "ada_w": np.ascontiguousarray(inp["ada_w"][:L]),
          "ffn_wi": np.ascontiguousarray(inp["ffn_wi"][:L]), "ffn_wo": np.ascontiguousarray(inp["ffn_wo"][:L]),
          "w_in": build_w_in_ext(inp["w_in"][:L]), "w_branch": np.ascontiguousarray(inp["w_branch"][:L]),
          "w_out": np.ascontiguousarray(inp["w_out"][:L]),
          "w_up": np.ascontiguousarray(inp["w_up"][:L]), "a_up": np.ascontiguousarray(inp["a_up"][:L]),
          "g_up": np.ascontiguousarray(inp["g_up"][:L]), "cst": _consts(), "cst2": _consts2(),
          "mu_rkv": np.ascontiguousarray(inp["mu_rkv"][:L]), "mu_w": np.ascontiguousarray(inp["mu_w"][:L]),
          "mu_a": np.ascontiguousarray(inp["mu_a"][:L])}
    sh.update(attn_tables(inp["rpb"][:L]))
    return sh


def attn_tables(rpb):
    L = rpb.shape[0]
    dd = np.arange(128)[:, None] - (np.arange(3968)[None, :] - 1920)
    ad = np.abs(dd)
    ta = ((ad <= 64).astype(np.float32) + ((dd % 4 == 0) & (ad <= 256)) + ((dd % 16 == 0) & (ad <= 1024))).astype(np.float32)
    rows = 32
    rs = np.clip(np.arange(rows) - 4, 0, rows - 8)
    valid = (np.arange(rows)[None, :] >= rs[:, None]) & (np.arange(rows)[None, :] < rs[:, None] + 8)
    tokrow = np.arange(S_LEN) // 64
    bu = (tokrow[None, :] == np.arange(32)[:, None]).astype(np.float32)
    bw = np.where(valid[tokrow, :].T, 0.0, -240000.0).astype(np.float32)
    kc = np.arange(64)[:, None, None]
    jj = np.arange(31)[None, :, None]
    qc = np.arange(64)[None, None, :]
    drp = jj - 15
    cs = np.clip(qc - 8, 0, 48)
    colv = (kc >= cs) & (kc < cs + 16)
    ok = colv & (np.abs(drp) <= 7)
    roff = np.clip(-drp + 7, 0, 14)
    coff = np.clip(kc - qc, -15, 15) + 15
    g = rpb[:, :, roff, coff]
    braw = np.where(ok[None, None], g, np.float32(-30000.0)).astype(np.float32)
    return {"ta": ta, "bu": bu, "bw": bw, "braw": np.ascontiguousarray(braw.reshape(L, 8, 64, 31 * 64))}


def prep_core(inp, seqs):
    x = inp["x"][seqs]
    n = len(seqs)
    xT = np.ascontiguousarray(x.reshape(n, S_LEN, NCH, 128).transpose(0, 3, 2, 1))
    cT = np.ascontiguousarray(inp["c"][seqs].reshape(n, NCH, 128).transpose(2, 1, 0))
    pos = np.ascontiguousarray(inp["positions"][seqs]).astype(np.int32)
    return {"xT": xT, "cT": cT, "pos": pos}


def run_model(inp, n_cores=8, NSEQ=2, L=4, parts=("ffn", "A", "B", "C"), seq_lists=None, trace=False, debug=False, cmode=None):
    prog = Prog(NSEQ=NSEQ, L=L, parts=parts)
    prog.debug = debug
    prog.cmode = cmode
    nc = prog.build()
    shared = prep_shared(inp, L)
    if seq_lists is None:
        seq_lists = [list(range(i * NSEQ, (i + 1) * NSEQ)) for i in range(n_cores)]
    in_maps = []
    for sl in seq_lists:
        m = dict(shared)
        m.update(prep_core(inp, sl))
        in_maps.append(m)
    res = run_bass_kernel_spmd(nc, in_maps, core_ids=list(range(len(seq_lists))), trace=trace)
    prog.dbg_out = res.results[0].get("dbg") if debug else None
    outs = []
    for r in res.results:
        o = r["outT"]
        outs.append(o.transpose(0, 3, 2, 1).reshape(o.shape[0], S_LEN, D))
    return np.concatenate(outs, axis=0), res, prog


def kernel(**inputs):
    inp = {k: np.asarray(v) for k, v in inputs.items()}
    out, _, _ = run_model(inp)
    return np.ascontiguousarray(out.astype(np.float32))
```
